# Optimizing a Trainium2 kernel written in Bass

```python
import jax
import jax.numpy as jnp
from jax import lax
import numpy as np

D_MODEL = 2048
BATCH = 4
SEQ = 2048
DEPTH = 4

PLE_DIM = 256
D_FF = 5632
RMS_EPS = 1e-6
N_EVEN = (DEPTH + 1) // 2
N_ODD = DEPTH // 2
N_VRES = max(N_ODD - 1, 0)

A_WIDTH = D_MODEL // 2
A_GROUP = 128
A_GROUPS = A_WIDTH // A_GROUP
A_CHUNK = 128
B_WIDTH = D_MODEL // 2
B_HEAD_DIM = 128
B_HEADS = B_WIDTH // B_HEAD_DIM
B_CHUNK = 64
B_MIN_F = 1e-30
EVEN_IN = 2 * A_WIDTH + 4 * B_WIDTH
C_HEAD = 64
C_HEADS = D_MODEL // C_HEAD
C_DECAY_LORA = 96
C_AAA_LORA = 96
C_MV_LORA = 64
C_GATE_LORA = 256
C_GN_EPS = 64e-5

kernel_name = 'hybrid_gmlp_hgrn2_rwkv7_macaron'


def rmsnorm(x, g, eps=RMS_EPS):
    xf = x.astype(jnp.float32)
    y = xf * lax.rsqrt(jnp.mean(xf * xf, axis=-1, keepdims=True) + eps)
    return (y * g.astype(jnp.float32)).astype(x.dtype)


def swiglu(x, w_gate, w_up, w_down):
    return (jax.nn.silu(x @ w_gate) * (x @ w_up)) @ w_down


def chunked_gmlp(u, v, v_gain, w_s, b_s):
    bsz, seq, _ = u.shape
    n_chunks = seq // A_CHUNK
    vg = rmsnorm(v.reshape(bsz, seq, A_GROUPS, A_GROUP), v_gain.reshape(A_GROUPS, A_GROUP))
    vg = vg.reshape(bsz, n_chunks, A_CHUNK, A_GROUPS, A_GROUP)
    causal = jnp.tril(jnp.ones((A_CHUNK, A_CHUNK), dtype=bool))
    w = jnp.where(causal[None], w_s, jnp.zeros_like(w_s))
    s = jnp.einsum('gts,bnsgc->bntgc', w, vg) + b_s.T[None, None, :, :, None]
    return u * s.reshape(bsz, seq, A_WIDTH)


def hgrn2(q, f_logit, i_in, lb):
    bsz, seq, _ = q.shape
    n_chunks = seq // B_CHUNK
    z = f_logit.astype(jnp.float32)
    lb = lb.astype(jnp.float32)
    sig = jax.nn.sigmoid(z)
    f = lb + (1.0 - lb) * sig
    log_f = jnp.log(jnp.maximum(f, B_MIN_F))
    k = 1.0 - f
    qf = jax.nn.silu(q.astype(jnp.float32))

    def to_chunks(t):
        return t.reshape(bsz, n_chunks, B_CHUNK, B_HEADS, B_HEAD_DIM).transpose(1, 0, 3, 2, 4)

    xs = (to_chunks(qf), to_chunks(k), to_chunks(i_in.astype(jnp.float32)), to_chunks(log_f))
    causal = jnp.tril(jnp.ones((B_CHUNK, B_CHUNK), dtype=bool))[:, :, None]

    def step(state, chunk):
        qb, kb, vb, gb = chunk
        cum = jnp.cumsum(gb, axis=2)
        last = cum[:, :, -1:, :]
        o_inter = jnp.einsum('bhtk,bhkv->bhtv', qb * jnp.exp(cum), state)
        rel = cum[:, :, :, None, :] - cum[:, :, None, :, :]
        dec = jnp.where(causal, jnp.exp(jnp.minimum(rel, 0.0)), 0.0)
        att = jnp.einsum('bhtk,bhsk,bhtsk->bhts', qb, kb, dec)
        o = o_inter + jnp.einsum('bhts,bhsv->bhtv', att, vb)
        new_state = jnp.exp(last[:, :, 0, :])[..., None] * state + jnp.einsum('bhsk,bhsv->bhkv', kb * jnp.exp(last - cum), vb)
        return new_state, o

    s0 = jnp.zeros((bsz, B_HEADS, B_HEAD_DIM, B_HEAD_DIM), jnp.float32)
    _, o = lax.scan(step, s0, xs)
    return o.transpose(1, 0, 3, 2, 4).reshape(bsz, seq, B_WIDTH)


def rwkv7(x, mix, w_r, w_k, w_v, w_o, w0, w1, w2, a0, a1, a2, g1, g2, k_k, k_a, r_k, gn_g, gn_b, v_first, v_res):
    bsz, seq, d = x.shape
    xx = jnp.pad(x, ((0, 0), (1, 0), (0, 0)))[:, :-1] - x
    xr, xw, xk, xv, xa, xg = [x + xx * mix[j] for j in range(6)]
    r = xr @ w_r
    k = xk @ w_k
    v = xv @ w_v
    w = -jax.nn.softplus(-(w0 + jnp.tanh(xw @ w1) @ w2)) - 0.5
    if v_res is None:
        v_first = v
    else:
        v0, v1, v2 = v_res
        v = v + (v_first - v) * jax.nn.sigmoid(v0 + (xv @ v1) @ v2)
    a = jax.nn.sigmoid(a0 + (xa @ a1) @ a2)
    g = jax.nn.sigmoid(xg @ g1) @ g2

    def heads(t):
        return t.reshape(bsz, seq, C_HEADS, C_HEAD).astype(jnp.float32)

    kk = heads(k * k_k)
    kk = kk / jnp.maximum(jnp.sqrt(jnp.sum(kk * kk, axis=-1, keepdims=True)), 1e-12)
    k = k * (1 + (a - 1) * k_a)
    decay = jnp.exp(-jnp.exp(w.astype(jnp.float32)))
    rh, kh, vh, ah = heads(r), heads(k), heads(v), heads(a)

    def step(state, inp):
        r_t, w_t, k_t, v_t, a_t, b_t = inp
        sa = jnp.einsum('bhvk,bhk->bhv', state, a_t)
        state = state * w_t[:, :, None, :] + v_t[..., None] * k_t[:, :, None, :] + sa[..., None] * b_t[:, :, None, :]
        return state, jnp.einsum('bhvk,bhk->bhv', state, r_t)

    def time_major(t):
        return jnp.moveaxis(t, 1, 0)

    xs = tuple(time_major(t) for t in (rh, heads(decay), kh, vh, -kk, kk * ah))
    s0 = jnp.zeros((bsz, C_HEADS, C_HEAD, C_HEAD), jnp.float32)
    _, y = lax.scan(step, s0, xs)
    y = jnp.moveaxis(y, 0, 1)
    mu = jnp.mean(y, axis=-1, keepdims=True)
    var = jnp.mean(jnp.square(y - mu), axis=-1, keepdims=True)
    y = ((y - mu) * lax.rsqrt(var + C_GN_EPS)).reshape(bsz, seq, d) * gn_g + gn_b
    bonus = jnp.sum(rh * kh * r_k, axis=-1, keepdims=True) * vh
    y = (y + bonus.reshape(bsz, seq, d)).astype(x.dtype)
    return ((y * g) @ w_o).astype(x.dtype), v_first


def setup_inputs(seed: int = 0) -> dict:
    key = jax.random.key(seed)
    ks = iter(jax.random.split(key, 64))
    f32 = jnp.float32

    def nrm(shape, scale):
        return jax.random.normal(next(ks), shape, f32) * scale

    def gain(shape):
        return 1.0 + nrm(shape, 0.05)

    D = D_MODEL
    return {
        'x': nrm((BATCH, SEQ, D), 1.0),
        'p': nrm((DEPTH, BATCH, SEQ, PLE_DIM), 1.0),
        'norms': gain((DEPTH, 4, D)),
        'final_norm': gain((D,)),
        'ffn_wg': nrm((DEPTH, 2, D, D_FF), D ** -0.5),
        'ffn_wu': nrm((DEPTH, 2, D, D_FF), D ** -0.5),
        'ffn_wd': nrm((DEPTH, 2, D_FF, D), 0.5 * D_FF ** -0.5),
        'ple_wp': nrm((DEPTH, PLE_DIM, D), 0.5 * PLE_DIM ** -0.5),
        'ple_wg': nrm((DEPTH, D, D), D ** -0.5),
        'e_w_in': nrm((N_EVEN, D, EVEN_IN), D ** -0.5),
        'e_w_out': nrm((N_EVEN, D, D), 0.5 * D ** -0.5),
        'a_vnorm': gain((N_EVEN, A_WIDTH)),
        'a_ws': nrm((N_EVEN, A_GROUPS, A_CHUNK, A_CHUNK), 0.5 * A_CHUNK ** -0.5),
        'a_bs': 1.0 + nrm((N_EVEN, A_GROUPS, A_CHUNK), 0.1),
        'b_onorm': gain((N_EVEN, B_WIDTH)),
        'b_lb_logits': nrm((DEPTH, B_WIDTH), 0.5),
        'c_mix': jax.random.uniform(next(ks), (N_ODD, 6, D), f32),
        'c_wr': nrm((N_ODD, D, D), D ** -0.5),
        'c_wk': nrm((N_ODD, D, D), D ** -0.5),
        'c_wv': nrm((N_ODD, D, D), D ** -0.5),
        'c_wo': nrm((N_ODD, D, D), 0.5 * D ** -0.5),
        'c_w0': jax.random.uniform(next(ks), (N_ODD, D), f32, -6.0, -1.0),
        'c_w1': nrm((N_ODD, D, C_DECAY_LORA), D ** -0.5),
        'c_w2': nrm((N_ODD, C_DECAY_LORA, D), 0.5 * C_DECAY_LORA ** -0.5),
        'c_a0': nrm((N_ODD, D), 0.5),
        'c_a1': nrm((N_ODD, D, C_AAA_LORA), D ** -0.5),
        'c_a2': nrm((N_ODD, C_AAA_LORA, D), 0.5 * C_AAA_LORA ** -0.5),
        'c_g1': nrm((N_ODD, D, C_GATE_LORA), D ** -0.5),
        'c_g2': nrm((N_ODD, C_GATE_LORA, D), C_GATE_LORA ** -0.5),
        'c_kk': 0.85 + nrm((N_ODD, D), 0.05),
        'c_ka': 1.0 + nrm((N_ODD, D), 0.05),
        'c_rk': nrm((N_ODD, C_HEADS, C_HEAD), 0.1),
        'c_gn_g': gain((N_ODD, D)),
        'c_gn_b': nrm((N_ODD, D), 0.01),
        'c_v0': 1.0 + nrm((N_VRES, D), 0.1),
        'c_v1': nrm((N_VRES, D, C_MV_LORA), 0.5 * D ** -0.5),
        'c_v2': nrm((N_VRES, C_MV_LORA, D), 0.5 * C_MV_LORA ** -0.5),
    }


def reference(x, p, norms, final_norm, ffn_wg, ffn_wu, ffn_wd, ple_wp, ple_wg, e_w_in, e_w_out,
              a_vnorm, a_ws, a_bs, b_onorm, b_lb_logits, c_mix, c_wr, c_wk, c_wv, c_wo, c_w0, c_w1, c_w2,
              c_a0, c_a1, c_a2, c_g1, c_g2, c_kk, c_ka, c_rk, c_gn_g, c_gn_b, c_v0, c_v1, c_v2):
    probs = jax.nn.softmax(b_lb_logits.astype(jnp.float32), axis=0)
    lower_bounds = jnp.cumsum(probs, axis=0) - probs[0]
    split_at = [A_WIDTH, 2 * A_WIDTH, 2 * A_WIDTH + B_WIDTH, 2 * A_WIDTH + 2 * B_WIDTH, 2 * A_WIDTH + 3 * B_WIDTH]
    h = x
    v_first = None
    for i in range(DEPTH):
        j = i // 2
        h = h + 0.5 * swiglu(rmsnorm(h, norms[i, 0]), ffn_wg[i, 0], ffn_wu[i, 0], ffn_wd[i, 0])
        hn = rmsnorm(h, norms[i, 1])
        if i % 2 == 0:
            proj = hn @ e_w_in[j]
            au, av, bq, bf, bi, bg = jnp.split(proj, split_at, axis=-1)
            a_out = chunked_gmlp(jax.nn.gelu(au), jax.nn.gelu(av), a_vnorm[j], a_ws[j], a_bs[j])
            b_o = hgrn2(bq, bf, bi, lower_bounds[i]).astype(hn.dtype)
            bsz, seq, _ = b_o.shape
            b_o = rmsnorm(b_o.reshape(bsz, seq, B_HEADS, B_HEAD_DIM), b_onorm[j].reshape(B_HEADS, B_HEAD_DIM))
            b_out = b_o.reshape(bsz, seq, B_WIDTH) * jax.nn.silu(bg)
            mixed = jnp.concatenate([a_out.astype(hn.dtype), b_out.astype(hn.dtype)], axis=-1) @ e_w_out[j]
        else:
            v_res = None if j == 0 else (c_v0[j - 1], c_v1[j - 1], c_v2[j - 1])
            mixed, v_first = rwkv7(hn, c_mix[j], c_wr[j], c_wk[j], c_wv[j], c_wo[j], c_w0[j], c_w1[j], c_w2[j],
                                   c_a0[j], c_a1[j], c_a2[j], c_g1[j], c_g2[j], c_kk[j], c_ka[j], c_rk[j],
                                   c_gn_g[j], c_gn_b[j], v_first, v_res)
        h = h + mixed.astype(h.dtype)
        h = h + 0.5 * swiglu(rmsnorm(h, norms[i, 2]), ffn_wg[i, 1], ffn_wu[i, 1], ffn_wd[i, 1])
        gate = jax.nn.sigmoid(rmsnorm(h, norms[i, 3]) @ ple_wg[i])
        h = h + gate * (p[i] @ ple_wp[i])
    return rmsnorm(h, final_norm)
```

```python
import contextlib
import numpy as np
import concourse.bass as bass
import concourse.mybir as mybir

F32 = mybir.dt.float32
BF16 = mybir.dt.bfloat16
AF = mybir.ActivationFunctionType
ALU = mybir.AluOpType
AX = mybir.AxisListType


class Buf:
    __slots__ = ("w", "r", "name", "excl")

    def __init__(self, name="", excl=False):
        self.w = {}
        self.r = {}
        self.name = name
        self.excl = excl


class Eng:
    def __init__(self, name, e, sem_key):
        self.name = name
        self.e = e
        self.key = sem_key
        self.cnt = 0
        self.waited = {}
        self.old = {}


class KB:
    N_DMA_SEMS = 8
    SEM_LIMIT = 30000

    def __init__(self):
        self.nc = bass.Bass("TRN2", target_bir_lowering=False)
        nc = self.nc
        self.es = contextlib.ExitStack()
        self.sems = []
        self.E = {}
        for name, e in (("pe", nc.tensor), ("act", nc.scalar), ("dve", nc.vector), ("pool", nc.gpsimd), ("sp", nc.sync)):
            key = self._new_sem("e_" + name)
            self.E[name] = Eng(name, e, key)
        self.dma_pool = {}
        for q in ("sp", "pool", "act"):
            keys = [self._new_sem(f"d_{q}{i}") for i in range(self.N_DMA_SEMS)]
            self.dma_pool[q] = {"keys": keys, "vals": [0] * len(keys), "next": 0}
        self.n_inst = 0
        self.out_events = []
        self.phase = 0
        self.extra = {}

    def _new_sem(self, name):
        s = self.es.enter_context(self.nc.semaphore(name))
        self.sems.append(s)
        return len(self.sems) - 1

    def sb(self, name, shape, dtype, es=None):
        t = (es or self.es).enter_context(self.nc.sbuf_tensor(f"s{self.phase}_" + name, list(shape), dtype))
        return t

    def ps(self, name, shape, dtype=F32, es=None):
        t = (es or self.es).enter_context(self.nc.psum_tensor(f"p{self.phase}_" + name, list(shape), dtype))
        return t

    def dram(self, name, shape, dtype, kind):
        return self.nc.dram_tensor(name, list(shape), dtype, kind=kind).ap()

    def _wait(self, eng, deps, skip_self=False):
        for k, v in deps.items():
            if skip_self and k == eng.key:
                continue
            if eng.waited.get(k, 0) >= v:
                continue
            eng.e.wait_ge(self.sems[k], v)
            eng.waited[k] = v
            self.n_inst += 1

    @staticmethod
    def _merge(dst, src):
        for k, v in src.items():
            if dst.get(k, 0) < v:
                dst[k] = v

    def op(self, engname, fn, R=(), W=()):
        eng = self.E[engname]
        deps = {}
        for b in R:
            self._merge(deps, b.w)
            if b.excl:
                for kk_, vv_ in b.r.items():
                    if kk_ != eng.key and deps.get(kk_, 0) < vv_:
                        deps[kk_] = vv_
        wdeps = {}
        for b in W:
            self._merge(wdeps, b.w)
            self._merge(wdeps, b.r)
        self._merge(deps, wdeps)
        if engname == "pe":
            for k_ in list(deps):
                if k_ == eng.key or k_ in eng.old:
                    del deps[k_]
        self._wait(eng, deps)
        if eng.cnt >= self.SEM_LIMIT:
            eng.old[eng.key] = eng.cnt
            eng.key = self._new_sem(f"e_{engname}_{len(eng.old)}")
            eng.cnt = 0
        ins = fn(eng.e)
        eng.cnt += 1
        ins.then_inc(self.sems[eng.key], 1)
        self.n_inst += 1
        ev = {eng.key: eng.cnt}
        for b in R:
            self._merge(b.r, ev)
        for b in W:
            b.w = dict(ev)
            b.r = {}
        return ins

    def dma(self, q, out, in_, R=(), W=(), is_output=False, **kw):
        eng = self.E[q]
        pool = self.dma_pool[q]
        i = pool["next"]
        pool["next"] = (i + 1) % len(pool["keys"])
        key = pool["keys"][i]
        deps = {}
        for b in R:
            self._merge(deps, b.w)
        for b in W:
            self._merge(deps, b.w)
            self._merge(deps, b.r)
        if pool["vals"][i] > 0:
            self._merge(deps, {key: pool["vals"][i]})
        self._wait(eng, deps)
        ins = eng.e.dma_start(out=out, in_=in_, **kw)
        pool["vals"][i] += 16
        ins.then_inc(self.sems[key], 16)
        self.n_inst += 1
        ev = {key: pool["vals"][i]}
        for b in R:
            self._merge(b.r, ev)
        for b in W:
            b.w = dict(ev)
            b.r = {}
        if is_output:
            self.out_events.append(ev)
        return ins

    def all_events(self):
        ev = {}
        for e in self.E.values():
            if e.cnt:
                ev[e.key] = e.cnt
            ev.update(e.old)
        for p in self.dma_pool.values():
            for k, v in zip(p["keys"], p["vals"]):
                if v:
                    ev[k] = v
        ev.update(self.extra)
        return ev

    def allgather_pairs(self, src, dst):
        eng = self.E["pool"]
        key = self._new_sem(f"cc{len(self.extra)}")
        ins = self.nc.gpsimd.collective_compute("AllGather", mybir.AluOpType.bypass, replica_groups=[[0, 1], [2, 3], [4, 5], [6, 7]],
                                                ins=[src], outs=[dst])
        ins.then_inc(self.sems[key], 1)
        self.n_inst += 1
        self.extra[key] = 1

    def barrier(self):
        ev = self.all_events()
        for eng in self.E.values():
            self._wait(eng, ev)

    def finish(self):
        ev = self.all_events()
        self._wait(self.E["sp"], ev)
        return self.nc


D = 2048
DC = 16
DFF = 5632
FC = 44
T = 1024
TH = 512
NTH = T // TH
EPS = 1e-6


class RowState:
    def __init__(self, kb, es=None):
        k = kb
        self.kb = kb
        self.hT = k.sb("hT", [128, DC, T], F32, es)
        self.hT_b = [[Buf(f"hT{c}_{t}") for t in range(NTH)] for c in range(DC)]
        self.xn = k.sb("xn", [128, DC, T], BF16, es)
        self.xn_b = [[Buf() for t in range(NTH)] for c in range(DC)]
        self.act = k.sb("act", [128, FC // 2, T], BF16, es)
        self.act_b = [[Buf() for t in range(NTH)] for c in range(FC // 2)]
        self.wA = [k.sb(f"wA{i}", [128, DC, 256], BF16, es) for i in range(4)]
        self.wA_b = [Buf() for _ in range(4)]
        self.wA_i = 0
        self.wD = [k.sb(f"wD{i}", [128, FC // 2, 128], BF16, es) for i in range(2)]
        self.wD_b = [Buf() for _ in range(2)]
        self.wD_i = 0
        self.psum = [k.ps(f"ps{i}", [128, TH], F32, es) for i in range(8)]
        self.psum_b = [Buf(excl=True) for _ in range(8)]
        self.ps_i = 0
        self.sq = [k.sb(f"sq{i}", [128, TH], BF16, es) for i in range(2)]
        self.sq_b = [Buf() for _ in range(2)]
        self.rstd = k.sb("rstd", [128, TH], F32, es)
        self.rstd_b = Buf()
        self.sg = [k.sb(f"sg{i}", [128, TH], F32, es) for i in range(2)]
        self.sg_b = [Buf() for _ in range(2)]
        self.sg_i = 0
        self.ones = k.sb("ones", [128, 128], BF16, es)
        self.ones_b = Buf()
        self.gcol = k.sb("gcol", [128, DC], F32, es)
        self.gcol_b = Buf()
        k.op("dve", lambda e: e.memset(self.ones[:], 1.0), W=[self.ones_b])

    def next_ps(self):
        i = self.ps_i
        self.ps_i = (i + 1) % 8
        return self.psum[i], self.psum_b[i]

    def next_wA(self):
        i = self.wA_i
        self.wA_i = (i + 1) % 4
        return self.wA[i], self.wA_b[i]

    def next_wD(self):
        i = self.wD_i
        self.wD_i = (i + 1) % 2
        return self.wD[i], self.wD_b[i]


def load_hT(rs, src):
    k = rs.kb
    v = src.rearrange("(c p) t -> p c t", p=128)
    for c in range(DC):
        k.dma("sp", rs.hT[:, c, :], v[:, c, :], W=rs.hT_b[c])


def store_hT(rs, dst, is_output=True):
    k = rs.kb
    v = dst.rearrange("(c p) t -> p c t", p=128)
    for c in range(DC):
        k.dma("sp", v[:, c, :], rs.hT[:, c, :], R=rs.hT_b[c], is_output=is_output)


def rmsnorm(rs, g_dram_pc, dst=None, dst_b=None, dst_dtype_scale=1.0):
    k = rs.kb
    dst = rs.xn if dst is None else dst
    dst_b = rs.xn_b if dst_b is None else dst_b
    k.dma("sp", rs.gcol[:], g_dram_pc, W=[rs.gcol_b])
    for th in range(NTH):
        ts = slice(th * TH, (th + 1) * TH)
        ps, ps_b = rs.next_ps()
        for c in range(DC):
            sq, sq_b = rs.sq[c % 2], rs.sq_b[c % 2]
            k.op("act", lambda e: e.activation(out=sq[:], in_=rs.hT[:, c, ts], func=AF.Square), R=[rs.hT_b[c][th]], W=[sq_b])
            k.op("pe", lambda e: e.matmul(ps[:], rs.ones[:], sq[:], start=(c == 0), stop=(c == DC - 1)),
                 R=[rs.ones_b, sq_b], W=[ps_b])
        k.op("dve", lambda e: e.tensor_scalar(out=rs.rstd[:], in0=ps[:], scalar1=float(1.0 / D), scalar2=float(EPS),
                                              op0=ALU.mult, op1=ALU.add), R=[ps_b], W=[rs.rstd_b])
        k.op("act", lambda e: e.activation(out=rs.rstd[:], in_=rs.rstd[:], func=AF.Sqrt), R=[rs.rstd_b], W=[rs.rstd_b])
        k.op("dve", lambda e: e.reciprocal(out=rs.rstd[:], in_=rs.rstd[:]), R=[rs.rstd_b], W=[rs.rstd_b])
        for c in range(DC):
            eng = "dve"
            k.op(eng, lambda e: e.scalar_tensor_tensor(out=dst[:, c, ts], in0=rs.hT[:, c, ts], scalar=rs.gcol[:, c:c + 1],
                                                       in1=rs.rstd[:], op0=ALU.mult, op1=ALU.mult),
                 R=[rs.hT_b[c][th], rs.gcol_b, rs.rstd_b], W=[dst_b[c][th]])


def load_w(rs, view, kc, ncols):
    k = rs.kb
    wt, wb = rs.next_wA()
    k.dma("pool", wt[:, :kc, :ncols], view, W=[wb])
    return wt, wb


def ffn(rs, wg, wu, wd):
    k = rs.kb
    wgv = wg.rearrange("(c p) f -> p c f", p=128)
    wuv = wu.rearrange("(c p) f -> p c f", p=128)
    wdv = wd.rearrange("(c p) d -> p c d", p=128)
    HF = FC // 2
    for fh in range(2):
        for fp in range(HF // 2):
            f0 = (fh * HF + fp * 2) * 128
            wgt, wgb = load_w(rs, wgv[:, :, f0:f0 + 256], DC, 256)
            wut, wub = load_w(rs, wuv[:, :, f0:f0 + 256], DC, 256)
            for fc in range(2):
                fl = fp * 2 + fc
                for th in range(NTH):
                    ts = slice(th * TH, (th + 1) * TH)
                    pg, pg_b = rs.next_ps()
                    pu, pu_b = rs.next_ps()
                    for c in range(DC):
                        k.op("pe", lambda e: e.matmul(pg[:], wgt[:, c, fc * 128:(fc + 1) * 128], rs.xn[:, c, ts],
                                                      start=(c == 0), stop=(c == DC - 1)),
                             R=[wgb, rs.xn_b[c][th]], W=[pg_b])
                    for c in range(DC):
                        k.op("pe", lambda e: e.matmul(pu[:], wut[:, c, fc * 128:(fc + 1) * 128], rs.xn[:, c, ts],
                                                      start=(c == 0), stop=(c == DC - 1)),
                             R=[wub, rs.xn_b[c][th]], W=[pu_b])
                    sg, sg_b = rs.sg[rs.sg_i], rs.sg_b[rs.sg_i]
                    rs.sg_i ^= 1
                    k.op("act", lambda e: e.activation(out=sg[:], in_=pg[:], func=AF.Silu), R=[pg_b], W=[sg_b])
                    k.op("dve", lambda e: e.tensor_tensor(out=rs.act[:, fl, ts], in0=sg[:], in1=pu[:], op=ALU.mult),
                         R=[sg_b, pu_b], W=[rs.act_b[fl][th]])
        for dc in range(DC):
            wdt, wdb = rs.next_wD()
            k.dma("pool", wdt[:], wdv[:, fh * HF:(fh + 1) * HF, dc * 128:(dc + 1) * 128], W=[wdb])
            for th in range(NTH):
                ts = slice(th * TH, (th + 1) * TH)
                po, po_b = rs.next_ps()
                for fl in range(HF):
                    k.op("pe", lambda e: e.matmul(po[:], wdt[:, fl, :], rs.act[:, fl, ts], start=(fl == 0), stop=(fl == HF - 1)),
                         R=[wdb, rs.act_b[fl][th]], W=[po_b])
                k.op("dve", lambda e: e.scalar_tensor_tensor(out=rs.hT[:, dc, ts], in0=po[:], scalar=0.5, in1=rs.hT[:, dc, ts],
                                                             op0=ALU.mult, op1=ALU.add),
                     R=[po_b, rs.hT_b[dc][th]], W=[rs.hT_b[dc][th]])


def load_xn_from(rs, src):
    k = rs.kb
    v = src.rearrange("(c p) t -> p c t", p=128)
    for c in range(DC):
        k.dma("sp", rs.xn[:, c, :], v[:, c, :], W=rs.xn_b[c])


def store_xn_to(rs, dst, is_output=True):
    k = rs.kb
    v = dst.rearrange("(c p) t -> p c t", p=128)
    for c in range(DC):
        k.dma("sp", v[:, c, :], rs.xn[:, c, :], R=rs.xn_b[c], is_output=is_output)


def square_proj(rs, w, evac):
    k = rs.kb
    wv = w.rearrange("(c p) n -> p c n", p=128)
    for op_ in range(DC // 2):
        wt, wb = load_w(rs, wv[:, :, op_ * 256:(op_ + 1) * 256], DC, 256)
        for o2 in range(2):
            oc = op_ * 2 + o2
            for th in range(NTH):
                ts = slice(th * TH, (th + 1) * TH)
                ps, ps_b = rs.next_ps()
                for c in range(DC):
                    k.op("pe", lambda e: e.matmul(ps[:], wt[:, c, o2 * 128:(o2 + 1) * 128], rs.xn[:, c, ts],
                                                  start=(c == 0), stop=(c == DC - 1)),
                         R=[wb, rs.xn_b[c][th]], W=[ps_b])
                evac(oc, th, ts, ps, ps_b)


def out_proj_residual(rs, w):
    k = rs.kb

    def evac(oc, th, ts, ps, ps_b):
        k.op("dve", lambda e: e.tensor_tensor(out=rs.hT[:, oc, ts], in0=ps[:], in1=rs.hT[:, oc, ts], op=ALU.add),
             R=[ps_b, rs.hT_b[oc][th]], W=[rs.hT_b[oc][th]])
    square_proj(rs, w, evac)


class PleState:
    def __init__(self, rs, es=None):
        k = rs.kb
        self.pT = k.sb("pT_sb", [128, 2, T], BF16, es)
        self.pT_b = Buf()
        self.wp = k.sb("wp_sb", [128, 2, D], BF16, es)
        self.wp_b = Buf()
        self.tmp = k.sb("pletmp", [128, TH], F32, es)
        self.tmp_b = Buf()


def ple(rs, pst, wgate, wp, pT_dram):
    k = rs.kb
    k.dma("pool", pst.pT[:], pT_dram.rearrange("(c p) t -> p c t", p=128), W=[pst.pT_b])
    k.dma("pool", pst.wp[:], wp.rearrange("(c p) n -> p c n", p=128), W=[pst.wp_b])

    def evac(oc, th, ts, ps, ps_b):
        sg, sg_b = rs.sg[rs.sg_i], rs.sg_b[rs.sg_i]
        rs.sg_i ^= 1
        k.op("act", lambda e: e.activation(out=sg[:], in_=ps[:], func=AF.Sigmoid), R=[ps_b], W=[sg_b])
        pp, pp_b = rs.next_ps()
        for c in range(2):
            k.op("pe", lambda e: e.matmul(pp[:], pst.wp[:, c, oc * 128:(oc + 1) * 128], pst.pT[:, c, ts], start=(c == 0), stop=(c == 1)),
                 R=[pst.wp_b, pst.pT_b], W=[pp_b])
        k.op("dve", lambda e: e.tensor_tensor(out=pst.tmp[:], in0=pp[:], in1=sg[:], op=ALU.mult), R=[pp_b, sg_b], W=[pst.tmp_b])
        k.op("dve", lambda e: e.tensor_tensor(out=rs.hT[:, oc, ts], in0=pst.tmp[:], in1=rs.hT[:, oc, ts], op=ALU.add),
             R=[pst.tmp_b, rs.hT_b[oc][th]], W=[rs.hT_b[oc][th]])
    square_proj(rs, wgate, evac)


def load_y_select(rs, y_pairs, sel, sel_b):
    k = rs.kb
    for c in range(DC):
        v = y_pairs[c // 8].rearrange("(c p) t -> p c t", p=128)
        k.dma("sp", rs.xn[:, c, :], v[:, c % 8, 0:T], W=rs.xn_b[c])
        k.dma("sp", rs.act[:, c, :], v[:, c % 8, T:2 * T], W=rs.act_b[c])
    for c in range(DC):
        k.op("pool", lambda e: e.tensor_scalar(out=rs.xn[:, c, :], in0=rs.xn[:, c, :], scalar1=sel[:, 0:1], scalar2=None, op0=ALU.mult),
             R=rs.xn_b[c] + [sel_b], W=rs.xn_b[c])
        k.op("dve", lambda e: e.scalar_tensor_tensor(out=rs.xn[:, c, :], in0=rs.act[:, c, :], scalar=sel[:, 1:2], in1=rs.xn[:, c, :],
                                                     op0=ALU.mult, op1=ALU.add),
             R=rs.act_b[c] + rs.xn_b[c] + [sel_b], W=rs.xn_b[c])


def store_xn_split(rs, dsts):
    k = rs.kb
    for c in range(DC):
        v = dsts[c // 8].rearrange("(c p) t -> p c t", p=128)
        k.dma("sp", v[:, c % 8, :], rs.xn[:, c, :], R=rs.xn_b[c])


S = 2048
NB = S // 512
NT = S // 128
NCH = S // 64
GC1 = 1.5957691216057308
GC2 = 0.044715


class PS:
    def __init__(self, kb, es=None):
        self.t = [kb.ps(f"ps{i}", [128, 512], F32, es) for i in range(8)]
        self.b = [Buf(excl=True) for _ in range(8)]
        self.i = 0
        self.ngen = 4

    def next(self):
        i = self.i
        self.i = (i + 1) % self.ngen
        return self.t[i], self.b[i]

    def fixed(self, i):
        return self.t[i], self.b[i]


def gelu_tanh(kb, out, x_ps, x_b, tmp, tmp_b, out_b, shape_slice=slice(None)):
    k = kb
    k.op("act", lambda e: e.activation(out=tmp, in_=x_ps, func=AF.Square), R=[x_b], W=[tmp_b])
    k.op("dve", lambda e: e.tensor_scalar(out=tmp, in0=tmp, scalar1=GC2, scalar2=1.0, op0=ALU.mult, op1=ALU.add), R=[tmp_b], W=[tmp_b])
    k.op("dve", lambda e: e.tensor_tensor(out=tmp, in0=tmp, in1=x_ps, op=ALU.mult), R=[tmp_b, x_b], W=[tmp_b])
    k.op("act", lambda e: e.activation(out=tmp, in_=tmp, func=AF.Sigmoid, scale=GC1), R=[tmp_b], W=[tmp_b])
    k.op("dve", lambda e: e.tensor_tensor(out=out, in0=tmp, in1=x_ps, op=ALU.mult), R=[tmp_b, x_b], W=[out_b])


def mix_even(kb, io, es=None):
    k = kb
    layer = io["layer"]
    ps = PS(kb, es)
    xn = k.sb("xnf", [128, DC, S], BF16, es)
    xn_b = [Buf() for _ in range(DC)]
    if io.get("hn_src") is not None:
        for c in range(DC):
            for r_ in range(2):
                k.dma("sp", xn[:, c, r_ * 1024:(r_ + 1) * 1024], io["hn_src"](c, r_), W=[xn_b[c]])
    else:
        hnv = io["hn"].rearrange("(c p) t -> p c t", p=128)
        for c in range(DC):
            k.dma("sp", xn[:, c, :], hnv[:, c, :], W=[xn_b[c]])
    w_in = io["w_in"].rearrange("(c p) n -> p c n", p=128)
    ones = k.sb("ones", [128, 128], BF16, es); ones_b = Buf()
    k.op("dve", lambda e: e.memset(ones[:], 1.0), W=[ones_b])
    ident = k.sb("ident", [128, 128], BF16, es); ident_b = Buf()
    k.dma("pool", ident[:], io["ident"], W=[ident_b])
    mask = k.sb("mask", [128, 128], F32, es); mask_b = Buf()
    k.dma("sp", mask[:], io["mask"], W=[mask_b])
    mask64 = k.sb("mask64", [128, 64], F32, es); mask64_b = Buf()
    k.dma("sp", mask64[:], io["mask64"], W=[mask64_b])
    rmask = k.sb("rmask", [128, S], F32, es); rmask_b = Buf()
    k.dma("sp", rmask[:], io["rmask"], W=[rmask_b])
    yrow = io.get("y_rows") or (lambda r0: io["yT"][r0:r0 + 128, :])

    wbig = [k.sb(f"wbig{i}", [128, DC, 512], BF16, es) for i in range(2)]
    wbig_b = [Buf(), Buf()]
    k.dma("pool", wbig[0][:], w_in[:, :, 512:1024], W=[wbig_b[0]])
    k.dma("pool", wbig[1][:], w_in[:, :, 0:512], W=[wbig_b[1]])
    esA = contextlib.ExitStack()
    gainv = k.sb("gainv", [128, 512], F32, esA); gainv_b = Buf()
    k.dma("sp", gainv[:], io["gainv"].partition_broadcast(128), W=[gainv_b])
    biasb = k.sb("biasb", [128, 4, 128], F32, esA); biasb_b = Buf()
    k.dma("sp", biasb[:].rearrange("p g t -> p (g t)"), io["bias"].partition_broadcast(128), W=[biasb_b])
    wsT = k.sb("wsT", [128, 4, 128], F32, esA); wsT_b = Buf()
    k.dma("sp", wsT[:], io["wsT"].rearrange("g s t -> s g t"), W=[wsT_b])
    wsm = k.sb("wsm", [128, 4, 128], BF16, esA); wsm_b = Buf()
    k.op("dve", lambda e: e.tensor_tensor(out=wsm[:], in0=wsT[:], in1=mask[:].unsqueeze(1).to_broadcast([128, 4, 128]), op=ALU.mult),
         R=[wsT_b, mask_b], W=[wsm_b])
    vg = k.sb("vg", [128, NT, 512], BF16, esA)
    vg_b = [Buf() for _ in range(NT)]
    tmpA = [k.sb(f"tmpA{i}", [128, 512], F32, esA) for i in range(2)]; tmpA_b = [Buf(), Buf()]
    gvt = [k.sb(f"gvt{i}", [128, 512], F32, esA) for i in range(2)]; gvt_b = [Buf(), Buf()]
    ssq = k.sb("ssq", [128, 4], F32, esA); ssq_b = Buf()
    for tt in range(NT):
        p_, p_b = ps.next()
        for c in range(DC):
            k.op("pe", lambda e: e.matmul(p_[:], xn[:, c, tt * 128:(tt + 1) * 128], wbig[0][:, c, :], start=(c == 0), stop=(c == DC - 1)),
                 R=[xn_b[c], wbig_b[0]], W=[p_b])
        i2 = tt % 2
        gelu_tanh(k, gvt[i2][:], p_[:], p_b, tmpA[i2][:], tmpA_b[i2], gvt_b[i2])
        k.op("dve", lambda e: e.tensor_tensor(out=tmpA[i2][:], in0=gvt[i2][:], in1=gvt[i2][:], op=ALU.mult), R=[gvt_b[i2]], W=[tmpA_b[i2]])
        k.op("dve", lambda e: e.tensor_reduce(out=ssq[:], in_=tmpA[i2][:].rearrange("p (g c) -> p g c", g=4), axis=AX.X, op=ALU.add),
             R=[tmpA_b[i2]], W=[ssq_b])
        k.op("dve", lambda e: e.tensor_scalar(out=ssq[:], in0=ssq[:], scalar1=1.0 / 128, scalar2=1e-6, op0=ALU.mult, op1=ALU.add), R=[ssq_b], W=[ssq_b])
        k.op("act", lambda e: e.activation(out=ssq[:], in_=ssq[:], func=AF.Sqrt), R=[ssq_b], W=[ssq_b])
        k.op("dve", lambda e: e.reciprocal(out=ssq[:], in_=ssq[:]), R=[ssq_b], W=[ssq_b])
        k.op("dve", lambda e: e.tensor_tensor(out=gvt[i2][:].rearrange("p (g c) -> p g c", g=4), in0=gvt[i2][:].rearrange("p (g c) -> p g c", g=4),
                                              in1=ssq[:].unsqueeze(2).to_broadcast([128, 4, 128]), op=ALU.mult), R=[gvt_b[i2], ssq_b], W=[gvt_b[i2]])
        k.op("dve", lambda e: e.tensor_tensor(out=vg[:, tt, :], in0=gvt[i2][:], in1=gainv[:], op=ALU.mult), R=[gvt_b[i2], gainv_b], W=[vg_b[tt]])
    yst = [k.sb(f"yst{i}", [128, 512], BF16, esA) for i in range(2)]; yst_b = [Buf(), Buf()]
    yi = 0
    for g in range(4):
        for blk in range(NB):
            bs = slice(blk * 512, (blk + 1) * 512)
            pu, pu_b = ps.next()
            for c in range(DC):
                k.op("pe", lambda e: e.matmul(pu[:], wbig[1][:, c, g * 128:(g + 1) * 128], xn[:, c, bs], start=(c == 0), stop=(c == DC - 1)),
                     R=[xn_b[c], wbig_b[1]], W=[pu_b])
            pss, pss_b = ps.next()
            for q in range(4):
                tt = blk * 4 + q
                k.op("pe", lambda e: e.matmul(pss[:, q * 128:(q + 1) * 128], vg[:, tt, g * 128:(g + 1) * 128], wsm[:, g, :], start=True, stop=True),
                     R=[vg_b[tt], wsm_b], W=[pss_b])
            i2 = yi % 2
            gelu_tanh(k, gvt[i2][:], pu[:], pu_b, tmpA[i2][:], tmpA_b[i2], gvt_b[i2])
            k.op("dve", lambda e: e.tensor_tensor(out=tmpA[i2][:].rearrange("p (q t) -> p q t", q=4), in0=pss[:].rearrange("p (q t) -> p q t", q=4),
                                                  in1=biasb[:, g, :].unsqueeze(1).to_broadcast([128, 4, 128]), op=ALU.add),
                 R=[pss_b, biasb_b], W=[tmpA_b[i2]])
            k.op("dve", lambda e: e.tensor_tensor(out=yst[i2][:], in0=gvt[i2][:], in1=tmpA[i2][:], op=ALU.mult), R=[gvt_b[i2], tmpA_b[i2]], W=[yst_b[i2]])
            k.dma("sp", yrow(g * 128)[:, bs], yst[i2][:], R=[yst_b[i2]], is_output=True)
            yi += 1

    k.barrier()
    esA.close()
    lbl = k.sb("lbl", [128, 4, 4], F32, es); lbl_b = Buf()
    k.dma("sp", lbl[:], io["lbl"], W=[lbl_b])
    k.op("act", lambda e: e.activation(out=lbl[:], in_=lbl[:], func=AF.Exp), R=[lbl_b], W=[lbl_b])
    tot = k.sb("tot", [128, 4], F32, es); tot_b = Buf()
    lb = k.sb("lb", [128, 4], F32, es); lb_b = Buf()
    oml = k.sb("oml", [128, 4], F32, es); oml_b = Buf()
    k.op("dve", lambda e: e.tensor_reduce(out=tot[:], in_=lbl[:], axis=AX.X, op=ALU.add), R=[lbl_b], W=[tot_b])
    k.op("dve", lambda e: e.reciprocal(out=tot[:], in_=tot[:]), R=[tot_b], W=[tot_b])
    if layer == 0:
        k.op("dve", lambda e: e.memset(lb[:], 0.0), W=[lb_b])
    else:
        k.op("dve", lambda e: e.tensor_reduce(out=lb[:], in_=lbl[:, :, 1:layer + 1], axis=AX.X, op=ALU.add), R=[lbl_b], W=[lb_b])
        k.op("dve", lambda e: e.tensor_tensor(out=lb[:], in0=lb[:], in1=tot[:], op=ALU.mult), R=[lb_b, tot_b], W=[lb_b])
    k.op("dve", lambda e: e.tensor_scalar(out=oml[:], in0=lb[:], scalar1=-1.0, scalar2=1.0, op0=ALU.mult, op1=ALU.add), R=[lb_b], W=[oml_b])
    onorm = k.sb("onorm", [128, 4], F32, es); onorm_b = Buf()
    k.dma("sp", onorm[:], io["onorm"], W=[onorm_b])

    def big(name, dt=F32):
        return k.sb(name, [128, S], dt, es), Buf()
    B1, B1_b = big("B1"); B2, B2_b = big("B2"); B3, B3_b = big("B3"); B4, B4_b = big("B4"); G1, G1_b = big("G1")
    Q1, Q1_b = big("Q1", BF16); Q2, Q2_b = big("Q2", BF16); K1, K1_b = big("K1", BF16); K2, K2_b = big("K2", BF16); I1, I1_b = big("I1", BF16)
    Vt = k.sb("Vt", [128, NT, 128], BF16, es); Vt_b = Buf()
    Kt = k.sb("Kt", [128, NT, 128], BF16, es); Kt_b = Buf()
    att = k.sb("att", [128, NT, 64], BF16, es); att_b = Buf()
    oT, oT_b = big("oT")
    cmid = k.sb("cmid", [128, NCH], F32, es); cmid_b = Buf()
    cend = k.sb("cend", [128, NCH], F32, es); cend_b = Buf()
    decL = k.sb("decL", [128, NCH], F32, es); decL_b = Buf()
    Sf = k.sb("Sf", [128, 128], F32, es); Sf_b = Buf()
    Sb = k.sb("Sb", [128, 128], BF16, es); Sb_b = Buf()
    sqb = [k.sb(f"sqb{i}", [128, 512], BF16, es) for i in range(2)]; sqb_b = [Buf(), Buf()]
    rst = k.sb("rst", [128, 512], F32, es); rst_b = Buf()
    ybig, ybig_b = big("ybig", BF16)

    def proj(col0, evac):
        for blk in range(NB):
            bs = slice(blk * 512, (blk + 1) * 512)
            p_, p_b = ps.next()
            for c in range(DC):
                k.op("pe", lambda e: e.matmul(p_[:], wh[:, c, col0:col0 + 128], xn[:, c, bs], start=(c == 0), stop=(c == DC - 1)),
                     R=[xn_b[c], wh_b], W=[p_b])
            evac(blk, bs, p_, p_b)

    c3 = lambda ap: ap.rearrange("p (n t) -> p n t", t=64)
    for hd in range(4):
        wh, wh_b = wbig[hd % 2], wbig_b[hd % 2]
        k.dma("pool", wh[:], w_in[:, :, 1024 + hd * 512:1024 + (hd + 1) * 512], W=[wh_b])
        proj(0, lambda blk, bs, p_, p_b: k.op("act", lambda e: e.activation(out=B4[:, bs], in_=p_[:], func=AF.Silu), R=[p_b], W=[B4_b]))
        proj(128, lambda blk, bs, p_, p_b: k.op("act", lambda e: e.activation(out=B1[:, bs], in_=p_[:], func=AF.Sigmoid), R=[p_b], W=[B1_b]))
        proj(256, lambda blk, bs, p_, p_b: k.op("act", lambda e: e.activation(out=I1[:, bs], in_=p_[:], func=AF.Copy), R=[p_b], W=[I1_b]))
        proj(384, lambda blk, bs, p_, p_b: k.op("act", lambda e: e.activation(out=G1[:, bs], in_=p_[:], func=AF.Silu), R=[p_b], W=[G1_b]))
        k.op("dve", lambda e: e.tensor_scalar(out=B1[:], in0=B1[:], scalar1=oml[:, hd:hd + 1], scalar2=lb[:, hd:hd + 1], op0=ALU.mult, op1=ALU.add),
             R=[B1_b, oml_b, lb_b], W=[B1_b])
        k.op("dve", lambda e: e.tensor_scalar(out=B3[:], in0=B1[:], scalar1=-1.0, scalar2=1.0, op0=ALU.mult, op1=ALU.add), R=[B1_b], W=[B3_b])
        k.op("dve", lambda e: e.tensor_scalar_max(out=B1[:], in0=B1[:], scalar1=1e-30), R=[B1_b], W=[B1_b])
        k.op("act", lambda e: e.activation(out=B1[:], in_=B1[:], func=AF.Ln), R=[B1_b], W=[B1_b])
        k.op("dve", lambda e: e.tensor_tensor_scan(out=B2[:], data0=rmask[:], data1=B1[:], initial=0.0, op0=ALU.mult, op1=ALU.add),
             R=[rmask_b, B1_b], W=[B2_b])
        k.op("dve", lambda e: e.tensor_copy(out=cmid[:], in_=c3(B2[:])[:, :, 31]), R=[B2_b], W=[cmid_b])
        k.op("dve", lambda e: e.tensor_copy(out=cend[:], in_=c3(B2[:])[:, :, 63]), R=[B2_b], W=[cend_b])
        k.op("act", lambda e: e.activation(out=decL[:], in_=cend[:], func=AF.Exp), R=[cend_b], W=[decL_b])
        k.op("act", lambda e: e.activation(out=B1[:], in_=B2[:], func=AF.Exp), R=[B2_b], W=[B1_b])
        k.op("dve", lambda e: e.tensor_tensor(out=Q1[:], in0=B4[:], in1=B1[:], op=ALU.mult), R=[B4_b, B1_b], W=[Q1_b])
        k.op("dve", lambda e: e.tensor_tensor(out=c3(B1[:]), in0=c3(B2[:]), in1=cmid[:].unsqueeze(2).to_broadcast([128, NCH, 64]), op=ALU.subtract),
             R=[B2_b, cmid_b], W=[B1_b])
        k.op("act", lambda e: e.activation(out=oT[:], in_=B1[:], func=AF.Exp), R=[B1_b], W=[oT_b])
        k.op("dve", lambda e: e.tensor_tensor(out=Q2[:], in0=B4[:], in1=oT[:], op=ALU.mult), R=[B4_b, oT_b], W=[Q2_b])
        k.op("act", lambda e: e.activation(out=B1[:], in_=B1[:], func=AF.Exp, scale=-1.0), R=[B1_b], W=[B1_b])
        k.op("dve", lambda e: e.tensor_tensor(out=K1[:], in0=B3[:], in1=B1[:], op=ALU.mult), R=[B3_b, B1_b], W=[K1_b])
        k.op("dve", lambda e: e.tensor_tensor(out=c3(B1[:]), in0=cend[:].unsqueeze(2).to_broadcast([128, NCH, 64]), in1=c3(B2[:]), op=ALU.subtract),
             R=[B2_b, cend_b], W=[B1_b])
        k.op("act", lambda e: e.activation(out=B1[:], in_=B1[:], func=AF.Exp), R=[B1_b], W=[B1_b])
        k.op("dve", lambda e: e.tensor_tensor(out=K2[:], in0=B3[:], in1=B1[:], op=ALU.mult), R=[B3_b, B1_b], W=[K2_b])
        for src, src_b, dst, dst_b in ((I1, I1_b, Vt, Vt_b), (K2, K2_b, Kt, Kt_b)):
            for q4 in range(NT // 4):
                p_, p_b = ps.next()
                pbf = p_[:].bitcast(BF16)
                for q in range(4):
                    tt = q4 * 4 + q
                    k.op("pe", lambda e: e.transpose(pbf[:, q * 128:(q + 1) * 128], src[:, tt * 128:(tt + 1) * 128], ident[:]),
                         R=[src_b, ident_b], W=[p_b])
                k.op("act", lambda e: e.activation(out=dst[:, q4 * 4:(q4 + 1) * 4, :], in_=pbf[:, 0:512].rearrange("p (q c) -> p q c", q=4), func=AF.Copy),
                     R=[p_b], W=[dst_b])
        for q4 in range(NT // 4):
            p_, p_b = ps.next()
            for q in range(4):
                tt = q4 * 4 + q
                k.op("pe", lambda e: e.matmul(p_[:, q * 128:(q + 1) * 128], K1[:, tt * 128:(tt + 1) * 128], Q2[:, tt * 128:(tt + 1) * 128], start=True, stop=True),
                     R=[K1_b, Q2_b], W=[p_b])
            pv = p_[:].rearrange("p (q t) -> p q t", q=4)
            for half in range(2):
                prt = slice(half * 64, (half + 1) * 64)
                k.op("dve", lambda e: e.tensor_tensor(out=att[prt, q4 * 4:(q4 + 1) * 4, :], in0=pv[prt, :, half * 64:(half + 1) * 64],
                                                      in1=mask64[prt, :].unsqueeze(1).to_broadcast([64, 4, 64]), op=ALU.mult),
                     R=[p_b, mask64_b], W=[att_b])
        k.op("dve", lambda e: e.memset(Sf[:], 0.0), W=[Sf_b])
        k.op("dve", lambda e: e.memset(Sb[:], 0.0), W=[Sb_b])
        psS = [None] * NCH

        def emit_mmS(n):
            bank, bank_b = ps.fixed(6 + n % 2)
            psS[n] = (bank, bank_b)
            tt, prt = n // 2, slice((n % 2) * 64, (n % 2) * 64 + 64)
            k.op("pe", lambda e: e.matmul(bank[:, 0:128], Kt[prt, tt, :], Vt[prt, tt, :], start=True, stop=True), R=[Kt_b, Vt_b], W=[bank_b])
        emit_mmS(0)
        po = po_b = None
        for n in range(NCH):
            tt, prt = n // 2, slice((n % 2) * 64, (n % 2) * 64 + 64)
            if n % 8 == 0:
                po, po_b = ps.fixed(4 + (n // 8) % 2)
            osl = po[:, (n % 8) * 64:(n % 8 + 1) * 64]
            k.op("pe", lambda e: e.matmul(osl, Vt[prt, tt, :], att[prt, tt, :], start=True, stop=False), R=[Vt_b, att_b], W=[po_b])
            if n + 1 < NCH:
                emit_mmS(n + 1)
            k.op("pe", lambda e: e.matmul(osl, Sb[:], Q1[:, n * 64:(n + 1) * 64], start=False, stop=True), R=[Sb_b, Q1_b], W=[po_b])
            bank, bank_b = psS[n]
            k.op("dve", lambda e: e.scalar_tensor_tensor(out=Sf[:], in0=Sf[:], scalar=decL[:, n:n + 1], in1=bank[:, 0:128], op0=ALU.mult, op1=ALU.add),
                 R=[Sf_b, decL_b, bank_b], W=[Sf_b])
            k.op("act", lambda e: e.activation(out=Sb[:], in_=Sf[:], func=AF.Copy), R=[Sf_b], W=[Sb_b])
            if n % 8 == 7:
                blk = n // 8
                k.op("act", lambda e: e.activation(out=oT[:, blk * 512:(blk + 1) * 512], in_=po[:], func=AF.Copy), R=[po_b], W=[oT_b])
        for blk in range(NB):
            bs = slice(blk * 512, (blk + 1) * 512)
            i2 = blk % 2
            k.op("act", lambda e: e.activation(out=sqb[i2][:], in_=oT[:, bs], func=AF.Square), R=[oT_b], W=[sqb_b[i2]])
            p_, p_b = ps.next()
            k.op("pe", lambda e: e.matmul(p_[:], ones[:], sqb[i2][:], start=True, stop=True), R=[ones_b, sqb_b[i2]], W=[p_b])
            k.op("dve", lambda e: e.tensor_scalar(out=rst[:], in0=p_[:], scalar1=1.0 / 128, scalar2=1e-6, op0=ALU.mult, op1=ALU.add), R=[p_b], W=[rst_b])
            k.op("act", lambda e: e.activation(out=rst[:], in_=rst[:], func=AF.Sqrt), R=[rst_b], W=[rst_b])
            k.op("dve", lambda e: e.reciprocal(out=rst[:], in_=rst[:]), R=[rst_b], W=[rst_b])
            k.op("dve", lambda e: e.scalar_tensor_tensor(out=rst[:], in0=oT[:, bs], scalar=onorm[:, hd:hd + 1], in1=rst[:], op0=ALU.mult, op1=ALU.mult),
                 R=[oT_b, onorm_b, rst_b], W=[rst_b])
            k.op("dve", lambda e: e.tensor_tensor(out=ybig[:, bs], in0=rst[:], in1=G1[:, bs], op=ALU.mult), R=[rst_b, G1_b], W=[ybig_b])
        k.dma("sp", yrow(512 + hd * 128), ybig[:], R=[ybig_b], is_output=True)


TP = 1024
NBP = TP // 512
L = 64
CPB = 8
WSCALE = -0.6065306597126334
GN_EPS = 64e-5


class PSO:
    def __init__(self, kb, es=None):
        self.t = [kb.ps(f"ps{i}", [128, 512], F32, es) for i in range(8)]
        self.b = [Buf(excl=True) for _ in range(8)]
        self.i = 0
        self.ngen = 5

    def next(self):
        i = self.i
        self.i = (i + 1) % self.ngen
        return self.t[i], self.b[i]

    def fixed(self, i):
        return self.t[i], self.b[i]


def mix_odd(kb, io, es=None):
    k = kb
    first = io["first"]
    ps = PSO(kb, es)
    xe = k.sb("xe", [128, DC, S + 8], BF16, es)
    xe_b = [Buf() for _ in range(DC)]
    for c in range(DC):
        k.op("pool", lambda e: e.memset(xe[:, c, 0:8], 0.0), W=[xe_b[c]])
        if io.get("hn_src") is not None:
            for r_ in range(2):
                k.dma("sp", xe[:, c, 8 + r_ * 1024:8 + (r_ + 1) * 1024], io["hn_src"](c, r_), W=[xe_b[c]])
        else:
            k.dma("sp", xe[:, c, 8:S + 8], io["hn"].rearrange("(c p) t -> p c t", p=128)[:, c, :], W=[xe_b[c]])

    def cst(name, shape, dt, src, q="sp"):
        t = k.sb(name, shape, dt, es)
        b = Buf()
        k.dma(q, t[:], src, W=[b])
        return t, b
    ident, ident_b = cst("ident", [128, 128], BF16, io["ident"], "pool")
    bones, bones_b = cst("bones", [128, 128], BF16, io["bones"], "pool")
    bones1, bones1_b = cst("bones1", [128, 128], BF16, io["bones1"], "pool")
    cmask, cmask_b = cst("cmask", [64, 5, 64], F32, io["cmask"])
    rmask, rmask_b = cst("rmask", [128, TP], F32, io["rmask"])
    mixm, mixm_b = cst("mixm", [128, 6, DC], F32, io["mixpc"])
    cols, cols_b = cst("cols", [128, 8, 8], F32, io["cols"])
    mixo = k.sb("mixo", [128, 6, DC], F32, es); mixo_b = Buf()
    k.op("dve", lambda e: e.tensor_scalar(out=mixo[:], in0=mixm[:], scalar1=-1.0, scalar2=1.0, op0=ALU.mult, op1=ALU.add), R=[mixm_b], W=[mixo_b])
    omka = k.sb("omka", [128, 8], F32, es); omka_b = Buf()
    k.op("dve", lambda e: e.tensor_scalar(out=omka[:], in0=cols[:, :, 3], scalar1=-1.0, scalar2=1.0, op0=ALU.mult, op1=ALU.add), R=[cols_b], W=[omka_b])
    w2t, w2t_b = cst("w2t", [96, 1024], BF16, io["w2"], "pool")
    a2t, a2t_b = cst("a2t", [96, 1024], BF16, io["a2"], "pool")
    g2t, g2t_b = cst("g2t", [128, 2, 1024], BF16, io["g2"].rearrange("(c p) n -> p c n", p=128), "pool")
    if not first:
        v2t, v2t_b = cst("v2t", [64, 1024], BF16, io["v2"], "pool")

    stage = k.sb("stage", [128, DC, 128], F32, es); stage_b = Buf()
    Wa = k.sb("Wa", [128, DC, 128], BF16, es); Wa_b = Buf()
    Wb = k.sb("Wb", [128, DC, 128], BF16, es); Wb_b = Buf()

    def load_mixed(wview, ncols, mi):
        k.dma("sp", stage[:, :, :ncols], wview, W=[stage_b])
        k.op("pool", lambda e: e.tensor_tensor(out=Wa[:, :, :ncols], in0=stage[:, :, :ncols],
                                               in1=mixo[:, mi, :].unsqueeze(2).to_broadcast([128, DC, ncols]), op=ALU.mult),
             R=[stage_b, mixo_b], W=[Wa_b])
        k.op("pool", lambda e: e.tensor_tensor(out=Wb[:, :, :ncols], in0=stage[:, :, :ncols],
                                               in1=mixm[:, mi, :].unsqueeze(2).to_broadcast([128, DC, ncols]), op=ALU.mult),
             R=[stage_b, mixm_b], W=[Wb_b])

    def mixed_proj(ncols, t0, evac, col0=0):
        p_, p_b = ps.next()
        for c in range(DC):
            k.op("pe", lambda e: e.matmul(p_[:ncols, :], Wa[:, c, col0:col0 + ncols], xe[:, c, 8 + t0:8 + t0 + 512], start=(c == 0), stop=False),
                 R=[Wa_b, xe_b[c]], W=[p_b])
        for c in range(DC):
            k.op("pe", lambda e: e.matmul(p_[:ncols, :], Wb[:, c, col0:col0 + ncols], xe[:, c, 7 + t0:7 + t0 + 512], start=False, stop=(c == DC - 1)),
                 R=[Wb_b, xe_b[c]], W=[p_b])
        evac(p_, p_b)

    hw = k.sb("hw", [96, S], BF16, es); hw_b = Buf()
    ha = k.sb("ha", [96, S], BF16, es); ha_b = Buf()
    hg = k.sb("hg", [128, 2, S], BF16, es); hg_b = Buf()
    wv_ = lambda name: io[name].rearrange("(c p) n -> p c n", p=128)
    load_mixed(wv_("w1"), 96, 1)
    for blk in range(S // 512):
        bs = slice(blk * 512, (blk + 1) * 512)
        mixed_proj(96, blk * 512, lambda p_, p_b: k.op("act", lambda e: e.activation(out=hw[:, bs], in_=p_[:96, :], func=AF.Tanh), R=[p_b], W=[hw_b]))
    load_mixed(wv_("a1"), 96, 4)
    for blk in range(S // 512):
        bs = slice(blk * 512, (blk + 1) * 512)
        mixed_proj(96, blk * 512, lambda p_, p_b: k.op("act", lambda e: e.activation(out=ha[:, bs], in_=p_[:96, :], func=AF.Copy), R=[p_b], W=[ha_b]))
    for gc in range(2):
        load_mixed(wv_("g1")[:, :, gc * 128:(gc + 1) * 128], 128, 5)
        for blk in range(S // 512):
            bs = slice(blk * 512, (blk + 1) * 512)
            mixed_proj(128, blk * 512, lambda p_, p_b: k.op("act", lambda e: e.activation(out=hg[:, gc, bs], in_=p_[:, :], func=AF.Sigmoid), R=[p_b], W=[hg_b]))
    if not first:
        hv = k.sb("hv", [64, S], BF16, es); hv_b = Buf()
        load_mixed(wv_("v1"), 64, 3)
        for blk in range(S // 512):
            bs = slice(blk * 512, (blk + 1) * 512)
            mixed_proj(64, blk * 512, lambda p_, p_b: k.op("act", lambda e: e.activation(out=hv[:, bs], in_=p_[:64, :], func=AF.Copy), R=[p_b], W=[hv_b]))

    import os as _os
    STOP = int(_os.environ.get("STOP", "99"))
    if STOP <= 1:
        return
    def big(name, dt=F32):
        return k.sb(name, [128, TP], dt, es), Buf()
    V, V_b = big("V"); K, K_b = big("K"); A, A_b = big("A"); KK, KK_b = big("KK"); CUM, CUM_b = big("CUM"); T1, T1_b = big("T1"); T2, T2_b = big("T2")
    Vb, Vb_b = big("Vb", BF16); G, G_b = big("G", BF16)
    NCP = TP // L
    AR = k.sb("AR", [128, NCP, 2, L], BF16, es); AR_b = Buf()
    BK = k.sb("BK", [128, NCP, 2, L], BF16, es); BK_b = Buf()
    BhKh = k.sb("BhKh", [128, 2, TP], BF16, es); BhKh_b = Buf()
    cend = k.sb("cend", [128, NCP], F32, es); cend_b = Buf()
    decL = k.sb("decL", [128, NCP], F32, es); decL_b = Buf()
    sqb = k.sb("sqb", [128, 512], BF16, es); sqb_b = Buf()
    Vpad = k.sb("Vpad", [64, CPB, 2, 128], BF16, es); Vpad_b = Buf()
    Bpad = k.sb("Bpad", [64, CPB, 2, 128], BF16, es); Bpad_b = Buf()
    Kpad = k.sb("Kpad", [64, CPB, 2, 128], BF16, es); Kpad_b = Buf()
    for t_, b_ in ((Vpad, Vpad_b), (Bpad, Bpad_b), (Kpad, Kpad_b)):
        k.op("pool", lambda e: e.memset(t_[:], 0.0), W=[b_])
    sc = {}
    for nm in ("N", "NT", "N2", "NT2", "X", "X2", "AAK", "PBR", "PKR"):
        sc[nm] = (k.sb("sc" + nm, [64, 2, CPB, L], BF16, es), Buf())
    W0 = k.sb("W0", [64, 2, L], BF16, es); W0_b = Buf()
    Upad = k.sb("Upad", [64, 384], BF16, es); Upad_b = Buf()
    k.op("pool", lambda e: e.memset(Upad[:], 0.0), W=[Upad_b])
    Upad_w = Upad[:, 0:384].rearrange("p (h w) -> p h w", w=192)[:, :, 0:64]
    Upad_r = Upad[:, 0:256].rearrange("p (h w) -> p h w", w=128)
    Mf = k.sb("Mf", [128, 128], F32, es); Mf_b = Buf()
    Mb = k.sb("Mb", [128, 128], BF16, es); Mb_b = Buf()
    yf = k.sb("yf", [128, 512], F32, es); yf_b = Buf()
    ysb = k.sb("ysb", [128, 512], BF16, es); ysb_b = Buf()
    ysq = k.sb("ysq", [128, 512], BF16, es); ysq_b = Buf()
    t3 = k.sb("t3", [128, 512], F32, es); t3_b = Buf()
    yo = k.sb("yo", [128, 512], BF16, es); yo_b = Buf()
    identb = ident
    c3 = lambda ap: ap.rearrange("p (n t) -> p n t", t=L)

    def colv(pr, vi):
        return cols[:, pr, vi:vi + 1]

    wr, wk, wv = wv_("wr"), wv_("wk"), wv_("wv")
    yrow = io.get("y_rows") or (lambda r0: io["yT"][r0:r0 + 128, :])
    for pr in range(8):
        pc = slice(pr * 128, (pr + 1) * 128)
        k.op("dve", lambda e: e.memset(Mf[:], 0.0), W=[Mf_b])
        k.op("dve", lambda e: e.memset(Mb[:], 0.0), W=[Mb_b])
        for half in range(S // TP):
            tb = half * TP
            load_mixed(wv[:, :, pc], 128, 3)
            for blk in range(NBP):
                bs = slice(blk * 512, (blk + 1) * 512)
                mixed_proj(128, tb + blk * 512, lambda p_, p_b: k.op("act", lambda e: e.activation(out=V[:, bs], in_=p_[:], func=AF.Copy), R=[p_b], W=[V_b]))
            if first:
                k.dma("sp", io["vf_out"][pc, tb:tb + TP], V[:], R=[V_b], is_output=True)
            else:
                k.dma("sp", T1[:], io["vf_in"][pc, tb:tb + TP], W=[T1_b])
                for blk in range(NBP):
                    bs = slice(blk * 512, (blk + 1) * 512)
                    p_, p_b = ps.next()
                    k.op("pe", lambda e: e.matmul(p_[:], v2t[:, pc], hv[:, tb + blk * 512:tb + (blk + 1) * 512], start=True, stop=True), R=[v2t_b, hv_b], W=[p_b])
                    k.op("act", lambda e: e.activation(out=T2[:, bs], in_=p_[:], func=AF.Sigmoid, bias=colv(pr, 7)), R=[p_b, cols_b], W=[T2_b])
                k.op("dve", lambda e: e.tensor_tensor(out=T1[:], in0=T1[:], in1=V[:], op=ALU.subtract), R=[T1_b, V_b], W=[T1_b])
                k.op("dve", lambda e: e.tensor_tensor(out=T1[:], in0=T1[:], in1=T2[:], op=ALU.mult), R=[T1_b, T2_b], W=[T1_b])
                k.op("dve", lambda e: e.tensor_tensor(out=V[:], in0=V[:], in1=T1[:], op=ALU.add), R=[T1_b, V_b], W=[V_b])
            k.op("act", lambda e: e.activation(out=Vb[:], in_=V[:], func=AF.Copy), R=[V_b], W=[Vb_b])
            load_mixed(wk[:, :, pc], 128, 2)
            for blk in range(NBP):
                bs = slice(blk * 512, (blk + 1) * 512)
                mixed_proj(128, tb + blk * 512, lambda p_, p_b: k.op("act", lambda e: e.activation(out=K[:, bs], in_=p_[:], func=AF.Copy), R=[p_b], W=[K_b]))
            for blk in range(NBP):
                bs = slice(blk * 512, (blk + 1) * 512)
                gs = slice(tb + blk * 512, tb + (blk + 1) * 512)
                p_, p_b = ps.next()
                k.op("pe", lambda e: e.matmul(p_[:], a2t[:, pc], ha[:, gs], start=True, stop=True), R=[a2t_b, ha_b], W=[p_b])
                k.op("act", lambda e: e.activation(out=A[:, bs], in_=p_[:], func=AF.Sigmoid, bias=colv(pr, 1)), R=[p_b, cols_b], W=[A_b])
                p_, p_b = ps.next()
                k.op("pe", lambda e: e.matmul(p_[:], w2t[:, pc], hw[:, gs], start=True, stop=True), R=[w2t_b, hw_b], W=[p_b])
                k.op("act", lambda e: e.activation(out=T1[:, bs], in_=p_[:], func=AF.Sigmoid, bias=colv(pr, 0)), R=[p_b, cols_b], W=[T1_b])
                p_, p_b = ps.next()
                for gc in range(2):
                    k.op("pe", lambda e: e.matmul(p_[:], g2t[:, gc, pc], hg[:, gc, gs], start=(gc == 0), stop=(gc == 1)), R=[g2t_b, hg_b], W=[p_b])
                k.op("act", lambda e: e.activation(out=G[:, bs], in_=p_[:], func=AF.Copy), R=[p_b], W=[G_b])
            k.op("dve", lambda e: e.tensor_scalar(out=T1[:], in0=T1[:], scalar1=WSCALE, scalar2=None, op0=ALU.mult), R=[T1_b], W=[T1_b])
            k.op("dve", lambda e: e.tensor_tensor_scan(out=CUM[:], data0=rmask[:], data1=T1[:], initial=0.0, op0=ALU.mult, op1=ALU.add),
                 R=[rmask_b, T1_b], W=[CUM_b])
            k.op("dve", lambda e: e.tensor_copy(out=cend[:], in_=c3(CUM[:])[:, :, L - 1]), R=[CUM_b], W=[cend_b])
            k.op("act", lambda e: e.activation(out=decL[:], in_=cend[:], func=AF.Exp), R=[cend_b], W=[decL_b])
            k.op("dve", lambda e: e.tensor_scalar(out=KK[:], in0=K[:], scalar1=colv(pr, 2), scalar2=None, op0=ALU.mult), R=[K_b, cols_b], W=[KK_b])
            for blk in range(NBP):
                bs = slice(blk * 512, (blk + 1) * 512)
                k.op("act", lambda e: e.activation(out=sqb[:], in_=KK[:, bs], func=AF.Square), R=[KK_b], W=[sqb_b])
                p_, p_b = ps.next()
                k.op("pe", lambda e: e.matmul(p_[:], bones1[:], sqb[:], start=True, stop=True), R=[bones1_b, sqb_b], W=[p_b])
                k.op("dve", lambda e: e.tensor_scalar_max(out=T2[:, bs], in0=p_[:], scalar1=1e-24), R=[p_b], W=[T2_b])
            k.op("act", lambda e: e.activation(out=T2[:], in_=T2[:], func=AF.Sqrt), R=[T2_b], W=[T2_b])
            k.op("dve", lambda e: e.reciprocal(out=T2[:], in_=T2[:]), R=[T2_b], W=[T2_b])
            k.op("dve", lambda e: e.tensor_tensor(out=KK[:], in0=KK[:], in1=T2[:], op=ALU.mult), R=[KK_b, T2_b], W=[KK_b])
            k.op("dve", lambda e: e.tensor_scalar(out=T2[:], in0=A[:], scalar1=colv(pr, 3), scalar2=omka[:, pr:pr + 1], op0=ALU.mult, op1=ALU.add),
                 R=[A_b, cols_b, omka_b], W=[T2_b])
            k.op("dve", lambda e: e.tensor_tensor(out=K[:], in0=K[:], in1=T2[:], op=ALU.mult), R=[K_b, T2_b], W=[K_b])
            k.op("dve", lambda e: e.tensor_tensor(out=T2[:], in0=CUM[:], in1=T1[:], op=ALU.subtract), R=[CUM_b, T1_b], W=[T2_b])
            k.op("act", lambda e: e.activation(out=T2[:], in_=T2[:], func=AF.Exp), R=[T2_b], W=[T2_b])
            k.op("dve", lambda e: e.scalar_tensor_tensor(out=AR[:, :, 0, :], in0=c3(KK[:]), scalar=-1.0, in1=c3(T2[:]), op0=ALU.mult, op1=ALU.mult),
                 R=[KK_b, T2_b], W=[AR_b])
            k.op("dve", lambda e: e.tensor_tensor(out=T1[:], in0=KK[:], in1=A[:], op=ALU.mult), R=[KK_b, A_b], W=[T1_b])
            k.op("act", lambda e: e.activation(out=T2[:], in_=CUM[:], func=AF.Exp, scale=-1.0), R=[CUM_b], W=[T2_b])
            k.op("dve", lambda e: e.tensor_tensor(out=BK[:, :, 0, :], in0=c3(T1[:]), in1=c3(T2[:]), op=ALU.mult), R=[T1_b, T2_b], W=[BK_b])
            k.op("dve", lambda e: e.tensor_tensor(out=BK[:, :, 1, :], in0=c3(K[:]), in1=c3(T2[:]), op=ALU.mult), R=[K_b, T2_b], W=[BK_b])
            k.op("dve", lambda e: e.tensor_tensor(out=c3(T2[:]), in0=cend[:].unsqueeze(2).to_broadcast([128, NCP, L]), in1=c3(CUM[:]), op=ALU.subtract),
                 R=[cend_b, CUM_b], W=[T2_b])
            k.op("act", lambda e: e.activation(out=T2[:], in_=T2[:], func=AF.Exp), R=[T2_b], W=[T2_b])
            k.op("dve", lambda e: e.tensor_tensor(out=BhKh[:, 0, :], in0=T1[:], in1=T2[:], op=ALU.mult), R=[T1_b, T2_b], W=[BhKh_b])
            k.op("dve", lambda e: e.tensor_tensor(out=BhKh[:, 1, :], in0=K[:], in1=T2[:], op=ALU.mult), R=[K_b, T2_b], W=[BhKh_b])
            k.op("act", lambda e: e.activation(out=T2[:], in_=CUM[:], func=AF.Exp), R=[CUM_b], W=[T2_b])
            load_mixed(wr[:, :, pc], 128, 0)
            for blk in range(NBP):
                bs = slice(blk * 512, (blk + 1) * 512)

                def ev(p_, p_b):
                    k.op("dve", lambda e: e.tensor_tensor(out=AR[:, blk * CPB:(blk + 1) * CPB, 1, :], in0=c3(p_[:]), in1=c3(T2[:, bs]), op=ALU.mult),
                         R=[p_b, T2_b], W=[AR_b])
                    k.op("dve", lambda e: e.scalar_tensor_tensor(out=sqb[:], in0=p_[:], scalar=colv(pr, 4), in1=K[:, bs], op0=ALU.mult, op1=ALU.mult),
                         R=[p_b, cols_b, K_b], W=[sqb_b])
                    p2, p2_b = ps.next()
                    k.op("pe", lambda e: e.matmul(p2[:], bones1[:], sqb[:], start=True, stop=True), R=[bones1_b, sqb_b], W=[p2_b])
                    k.op("dve", lambda e: e.tensor_tensor(out=T1[:, bs], in0=p2[:], in1=V[:, bs], op=ALU.mult), R=[p2_b, V_b], W=[T1_b])
                mixed_proj(128, tb + blk * 512, ev)
            if STOP <= 2:
                return
            for blk in range(NBP):
                bs = slice(blk * 512, (blk + 1) * 512)
                c0 = blk * CPB
                for src, dst, dst_b, src_b in ((Vb[:, :], Vpad, Vpad_b, Vb_b), (BhKh[:, 0, :], Bpad, Bpad_b, BhKh_b), (BhKh[:, 1, :], Kpad, Kpad_b, BhKh_b)):
                    p_, p_b = ps.next()
                    pbf = p_[:].bitcast(BF16)
                    for n in range(CPB):
                        t0 = (c0 + n) * L
                        k.op("pe", lambda e: e.transpose(pbf[0:64, n * 128:(n + 1) * 128], src[:, t0:t0 + L], identb[:]), R=[src_b, ident_b], W=[p_b])
                    pv = pbf[0:64, :].rearrange("p (n c) -> p n c", c=128)
                    k.op("act", lambda e: e.activation(out=dst[:, :, 0, 0:64], in_=pv[:, :, 0:64], func=AF.Copy), R=[p_b], W=[dst_b])
                    k.op("act", lambda e: e.activation(out=dst[:, :, 1, 64:128], in_=pv[:, :, 64:128], func=AF.Copy), R=[p_b], W=[dst_b])
                for h in range(2):
                    ph = slice(h * 64, (h + 1) * 64)
                    for nm, lhs, li, rhs, ri, mi in (("N", BK, 0, AR, 0, 0), ("PBR", BK, 0, AR, 1, 1), ("AAK", BK, 1, AR, 0, 2), ("PKR", BK, 1, AR, 1, 3), ("NT", AR, 0, BK, 0, 4)):
                        p_, p_b = ps.next()
                        for n in range(CPB):
                            k.op("pe", lambda e: e.matmul(p_[0:64, n * L:(n + 1) * L], lhs[ph, c0 + n, li, :], rhs[ph, c0 + n, ri, :], start=True, stop=True),
                                 R=[AR_b, BK_b], W=[p_b])
                        dst, dst_b = sc[nm]
                        k.op("dve", lambda e: e.tensor_tensor(out=dst[:, h, :, :], in0=p_[0:64, :].rearrange("p (n t) -> p n t", t=L),
                                                              in1=cmask[:, mi, :].unsqueeze(1).to_broadcast([64, CPB, L]), op=ALU.mult),
                             R=[p_b, cmask_b], W=[dst_b])
                if STOP <= 3:
                    return
                N_, N_b_ = sc["N"]; NT_, NT_b_ = sc["NT"]; N2_, N2_b_ = sc["N2"]; NT2_, NT2_b_ = sc["NT2"]; X_, X_b_ = sc["X"]; X2_, X2_b_ = sc["X2"]
                k.op("dve", lambda e: e.tensor_tensor(out=X_[:], in0=N_[:], in1=identb[0:64, 0:64].unsqueeze(1).unsqueeze(1).to_broadcast([64, 2, CPB, L]), op=ALU.add),
                     R=[N_b_, ident_b], W=[X_b_])
                cur = (N_, N_b_, NT_, NT_b_, X_, X_b_)
                nxt = (N2_, N2_b_, NT2_, NT2_b_, X2_, X2_b_)
                for j in range(1, 6):
                    cN, cN_b, cNT, cNT_b, cX, cX_b = cur
                    nN, nN_b, nNT, nNT_b, nX, nX_b = nxt
                    for h in range(2):
                        if j < 5:
                            p_, p_b = ps.next()
                            for n in range(CPB):
                                k.op("pe", lambda e: e.matmul(p_[0:64, n * L:(n + 1) * L], cNT[:, h, n, :], cN[:, h, n, :], start=True, stop=True), R=[cNT_b, cN_b], W=[p_b])
                            k.op("act", lambda e: e.activation(out=nN[:, h, :, :], in_=p_[0:64, :].rearrange("p (n t) -> p n t", t=L), func=AF.Copy), R=[p_b], W=[nN_b])
                        p_, p_b = ps.next()
                        for n in range(CPB):
                            k.op("pe", lambda e: e.matmul(p_[0:64, n * L:(n + 1) * L], cN[:, h, n, :], cNT[:, h, n, :], start=True, stop=True), R=[cNT_b, cN_b], W=[p_b])
                        k.op("dve", lambda e: e.tensor_copy(out=nNT[:, h, :, :], in_=p_[0:64, :].rearrange("p (n t) -> p n t", t=L)), R=[p_b], W=[nNT_b])
                    for h in range(2):
                        p_, p_b = ps.next()
                        for n in range(CPB):
                            k.op("pe", lambda e: e.matmul(p_[0:64, n * L:(n + 1) * L], nNT[:, h, n, :], cX[:, h, n, :], start=True, stop=False), R=[nNT_b, cX_b], W=[p_b])
                            k.op("pe", lambda e: e.matmul(p_[0:64, n * L:(n + 1) * L], identb[0:64, 0:64], cX[:, h, n, :], start=False, stop=True), R=[ident_b, cX_b], W=[p_b])
                        k.op("act", lambda e: e.activation(out=nX[:, h, :, :], in_=p_[0:64, :].rearrange("p (n t) -> p n t", t=L), func=AF.Copy), R=[p_b], W=[nX_b])
                    cur, nxt = nxt, cur
                Xf, Xf_b = cur[4], cur[5]
                AAK, AAK_b = sc["AAK"]; PBR, PBR_b = sc["PBR"]; PKR, PKR_b = sc["PKR"]
                if STOP <= 4:
                    return
                py, py_b = ps.fixed(5 + (half * NBP + blk) % 2)
                pch, pch_b = ps.fixed(7)
                for n_ in range(CPB):
                    n = 0 if _os.environ.get("REPEAT0") else n_
                    cn = c0 + n
                    if _os.environ.get("ALTBANK"):
                        pch, pch_b = ps.fixed(7 if n % 2 == 0 else 4)
                    k.op("pe", lambda e: e.matmul(pch[0:64, 0:128], AR[:, cn, 0, :], Mb[:], start=True, stop=False), R=[AR_b, Mb_b], W=[pch_b])
                    for h in range(2):
                        k.op("pe", lambda e: e.matmul(pch[0:64, 0:128], AAK[:, h, n, :], Vpad[:, n, h, :], start=False, stop=(h == 1)),
                             R=[AAK_b, Vpad_b], W=[pch_b])
                    STOP2 = int(_os.environ.get("STOP2", "99")); STOPN = int(_os.environ.get("STOPN", "0"))
                    if n == STOPN and STOP2 <= 0:
                        return
                    k.op("act", lambda e: e.activation(out=W0[:], in_=pch[0:64, 0:128].rearrange("p (h v) -> p h v", h=2), func=AF.Copy), R=[pch_b], W=[W0_b])
                    for h in range(2):
                        k.op("pe", lambda e: e.matmul(pch[0:64, 128 + h * L:128 + (h + 1) * L], Xf[:, h, n, :], W0[:, h, :], start=True, stop=True),
                             R=[Xf_b, W0_b], W=[pch_b])
                    if n == STOPN and STOP2 <= 1:
                        return
                    k.op("act", lambda e: e.activation(out=Upad_w, in_=pch[0:64, 128:256].rearrange("p (h v) -> p h v", h=2), func=AF.Copy), R=[pch_b], W=[Upad_b])
                    ysl = py[:, n * L:(n + 1) * L]
                    if n == STOPN and STOP2 <= 2:
                        return
                    k.op("pe", lambda e: e.matmul(ysl, Mb[:], AR[:, cn, 1, :], start=True, stop=False), R=[Mb_b, AR_b], W=[py_b])
                    for h in range(2):
                        k.op("pe", lambda e: e.matmul(ysl, Upad_r[:, h, :], PBR[:, h, n, :], start=False, stop=False), R=[Upad_b, PBR_b], W=[py_b])
                        k.op("pe", lambda e: e.matmul(ysl, Vpad[:, n, h, :], PKR[:, h, n, :], start=False, stop=(h == 1)), R=[Vpad_b, PKR_b], W=[py_b])
                    if n == STOPN and STOP2 <= 3:
                        return
                    for h in range(2):
                        k.op("pe", lambda e: e.matmul(pch[:, 256:384], Bpad[:, n, h, :], Upad_r[:, h, :], start=(h == 0), stop=False), R=[Bpad_b, Upad_b], W=[pch_b])
                        k.op("pe", lambda e: e.matmul(pch[:, 256:384], Kpad[:, n, h, :], Vpad[:, n, h, :], start=False, stop=(h == 1)), R=[Kpad_b, Vpad_b], W=[pch_b])
                    if n == STOPN and STOP2 <= 4:
                        return
                    k.op("dve", lambda e: e.scalar_tensor_tensor(out=Mf[:], in0=Mf[:], scalar=decL[:, cn:cn + 1], in1=pch[:, 256:384], op0=ALU.mult, op1=ALU.add),
                         R=[Mf_b, decL_b, pch_b], W=[Mf_b])
                    if n == STOPN and STOP2 <= 5:
                        return
                    k.op("act", lambda e: e.activation(out=Mb[:], in_=Mf[:], func=AF.Copy), R=[Mf_b], W=[Mb_b])
                    if n_ + 1 >= int(_os.environ.get("STOPC", "99")):
                        if io.get("dbgM") is not None:
                            k.dma("sp", io["dbgM"], Mf[:], R=[Mf_b], is_output=True)
                            k.dma("sp", io["dbgU"], Upad[:], R=[Upad_b], is_output=True)
                            k.dma("sp", io["dbgW"], W0[:].rearrange("p h v -> p (h v)"), R=[W0_b], is_output=True)
                            k.dma("sp", io["dbgX"], Xf[:].rearrange("p h n t -> p (h n t)"), R=[Xf_b], is_output=True)
                        return
                if STOP <= 5:
                    return
                k.op("act", lambda e: e.activation(out=yf[:], in_=py[:], func=AF.Copy), R=[py_b], W=[yf_b])
                if io.get("yraw") is not None:
                    k.dma("sp", io["yraw"][pc, tb + blk * 512:tb + (blk + 1) * 512], yf[:], R=[yf_b], is_output=True)
                k.op("dve", lambda e: e.tensor_copy(out=ysb[:], in_=yf[:]), R=[yf_b], W=[ysb_b])
                k.op("act", lambda e: e.activation(out=ysq[:], in_=yf[:], func=AF.Square), R=[yf_b], W=[ysq_b])
                pm, pm_b = ps.next()
                k.op("pe", lambda e: e.matmul(pm[:], bones[:], ysb[:], start=True, stop=True), R=[bones_b, ysb_b], W=[pm_b])
                pq, pq_b = ps.next()
                k.op("pe", lambda e: e.matmul(pq[:], bones[:], ysq[:], start=True, stop=True), R=[bones_b, ysq_b], W=[pq_b])
                k.op("act", lambda e: e.activation(out=t3[:], in_=pm[:], func=AF.Square), R=[pm_b], W=[t3_b])
                k.op("dve", lambda e: e.tensor_tensor(out=t3[:], in0=pq[:], in1=t3[:], op=ALU.subtract), R=[pq_b, t3_b], W=[t3_b])
                k.op("dve", lambda e: e.tensor_scalar(out=t3[:], in0=t3[:], scalar1=GN_EPS, scalar2=None, op0=ALU.add), R=[t3_b], W=[t3_b])
                k.op("act", lambda e: e.activation(out=t3[:], in_=t3[:], func=AF.Sqrt), R=[t3_b], W=[t3_b])
                k.op("dve", lambda e: e.reciprocal(out=t3[:], in_=t3[:]), R=[t3_b], W=[t3_b])
                k.op("dve", lambda e: e.tensor_tensor(out=yf[:], in0=yf[:], in1=pm[:], op=ALU.subtract), R=[yf_b, pm_b], W=[yf_b])
                k.op("dve", lambda e: e.scalar_tensor_tensor(out=yf[:], in0=yf[:], scalar=colv(pr, 5), in1=t3[:], op0=ALU.mult, op1=ALU.mult),
                     R=[yf_b, cols_b, t3_b], W=[yf_b])
                k.op("dve", lambda e: e.scalar_tensor_tensor(out=yf[:], in0=yf[:], scalar=colv(pr, 6), in1=T1[:, bs], op0=ALU.add, op1=ALU.add),
                     R=[yf_b, cols_b, T1_b], W=[yf_b])
                k.op("dve", lambda e: e.tensor_tensor(out=yo[:], in0=yf[:], in1=G[:, bs], op=ALU.mult), R=[yf_b, G_b], W=[yo_b])
                k.dma("sp", yrow(pr * 128)[:, tb + blk * 512:tb + (blk + 1) * 512], yo[:], R=[yo_b], is_output=True)


from concourse.bass_utils import run_bass_kernel_spmd
import ml_dtypes
_bf = ml_dtypes.bfloat16
NCORES = 8
DEPTH = 4
LITE = False


def _pc(g):
    return np.ascontiguousarray(np.asarray(g, np.float32).reshape(16, 128).T)


def _consts():
    s = np.arange(128)
    mask = (s[:, None] <= s[None, :]).astype(np.float32)
    mask64 = ((s[:, None] % 64) <= np.arange(64)[None, :]).astype(np.float32)
    rmaskE = np.ones((128, 2048), np.float32); rmaskE[:, ::64] = 0.0
    s6 = np.arange(64)
    su = (s6[:, None] < s6[None, :]).astype(np.float32); iu = (s6[:, None] <= s6[None, :]).astype(np.float32); sl = (s6[:, None] > s6[None, :]).astype(np.float32)
    cmask = np.ascontiguousarray(np.stack([su, iu, su, iu, sl], axis=1))
    blk = np.kron(np.eye(2, dtype=np.float32), np.ones((64, 64), np.float32))
    rmaskO = np.ones((128, 1024), np.float32); rmaskO[:, ::64] = 0.0
    return dict(mask=mask, mask64=mask64, rmaskE=rmaskE, ident=np.eye(128, dtype=np.float32), bones=blk / 64.0, bones1=blk, cmask=cmask, rmaskO=rmaskO)


def _even_inputs(d, j, hf):
    w = d["e_w_in"][j]
    g0 = hf * 4
    cols = [w[:, g0 * 128:(g0 + 4) * 128], w[:, 1024 + g0 * 128:1024 + (g0 + 4) * 128]]
    for hd in range(g0, g0 + 4):
        for part in range(4):
            cols.append(w[:, 2048 + part * 1024 + hd * 128:2048 + part * 1024 + (hd + 1) * 128])
    return dict(w_in=np.ascontiguousarray(np.concatenate(cols, axis=1)),
                wsT=np.ascontiguousarray(np.transpose(d["a_ws"][j, g0:g0 + 4], (0, 2, 1))),
                bias=np.ascontiguousarray(d["a_bs"][j, g0:g0 + 4].reshape(1, 512)),
                gainv=np.ascontiguousarray(d["a_vnorm"][j, g0 * 128:(g0 + 4) * 128].reshape(1, 512)),
                onorm=np.ascontiguousarray(d["b_onorm"][j, g0 * 128:(g0 + 4) * 128].reshape(4, 128).T),
                lbl=np.ascontiguousarray(np.transpose(d["b_lb_logits"][:, g0 * 128:(g0 + 4) * 128].reshape(4, 4, 128), (2, 1, 0))))


def _odd_inputs(d, j, hf):
    cs = slice(hf * 1024, (hf + 1) * 1024)
    m = {}
    for nm, key in (("wr", "c_wr"), ("wk", "c_wk"), ("wv", "c_wv"), ("w2", "c_w2"), ("a2", "c_a2"), ("g2", "c_g2")):
        m[nm] = np.ascontiguousarray(d[key][j][:, cs])
    for nm, key in (("w1", "c_w1"), ("a1", "c_a1"), ("g1", "c_g1")):
        m[nm] = np.ascontiguousarray(d[key][j])
    v0 = d["c_v0"][j - 1] if j > 0 else np.zeros(2048, np.float32)
    vecs = [d["c_w0"][j], d["c_a0"][j], d["c_kk"][j], d["c_ka"][j], d["c_rk"][j].reshape(-1), d["c_gn_g"][j], d["c_gn_b"][j], v0]
    cols = np.stack([np.asarray(v)[cs].reshape(8, 128) for v in vecs], axis=-1)
    m["cols"] = np.ascontiguousarray(np.transpose(cols, (1, 0, 2)))
    m["mixpc"] = np.ascontiguousarray(np.transpose(d["c_mix"][j].reshape(6, 16, 128), (2, 0, 1)))
    if j > 0:
        m["v1"] = np.ascontiguousarray(d["c_v1"][j - 1]); m["v2"] = np.ascontiguousarray(d["c_v2"][j - 1][:, cs])
    return m


_EVEN_SHAPES = dict(w_in=[2048, 3072], wsT=[4, 128, 128], bias=[1, 512], gainv=[1, 512], onorm=[128, 4], lbl=[128, 4, 4])
_ODD_SHAPES = dict(wr=[2048, 1024], wk=[2048, 1024], wv=[2048, 1024], w1=[2048, 96], a1=[2048, 96], g1=[2048, 256], w2=[96, 1024], a2=[96, 1024],
                   g2=[256, 1024], mixpc=[128, 6, 16], cols=[128, 8, 8], v1=[2048, 64], v2=[64, 1024])
_CONST_SHAPES = dict(mask=[128, 128], mask64=[128, 64], rmaskE=[128, 2048], ident=[128, 128], bones=[128, 128], bones1=[128, 128],
                     cmask=[64, 5, 64], rmaskO=[128, 1024])


def _wo_perm(d, i):
    j = i // 2
    if i % 2 == 0:
        return d["e_w_out"][j]
    w = d["c_wo"][j]
    return np.concatenate([w[0:512], w[1024:1536], w[512:1024], w[1536:2048]], axis=0)


def build_fused():
    kb = KB()
    nc = kb.nc
    di = lambda n, s, dt=F32: kb.dram(n, s, dt, "ExternalInput")
    xT = di("xT", [2048, 1024])
    if LITE:
        wg = wu = wd = None
    else:
        wg = di("ffn_wg", [DEPTH, 2, 2048, 5632]); wu = di("ffn_wu", [DEPTH, 2, 2048, 5632]); wd = di("ffn_wd", [DEPTH, 2, 5632, 2048])
    norms = di("norms_pc", [DEPTH, 4, 128, 16]); fnorm = di("fnorm_pc", [128, 16])
    plg = di("ple_wg", [DEPTH, 2048, 2048]); plp = di("ple_wp", [DEPTH, 256, 2048]); pT = di("pT", [DEPTH, 256, 1024])
    wo = di("wo", [DEPTH, 2048, 2048]); sel_d = di("sel", [128, 2])
    cst = {n: di(n, s) for n, s in _CONST_SHAPES.items()}
    ev_in = [{n: di(f"e{j}_{n}", s) for n, s in _EVEN_SHAPES.items()} for j in range(2)]
    od_in = [{n: di(f"o{j}_{n}", s) for n, s in _ODD_SHAPES.items() if j > 0 or n not in ("v1", "v2")} for j in range(2)]
    out = kb.dram("out", [2048, 1024], F32, "ExternalOutput")
    internal = lambda n, s, dt: nc.dram_tensor(n, list(s), dt, kind="Internal").ap()
    hspill = internal("hspill", [2048, 1024], F32)
    hn_loc = [internal(f"hn_loc{a}", [1024, 1024], BF16) for a in range(2)]
    hn_pair = [internal(f"hn_pair{a}", [2048, 1024], BF16) for a in range(2)]
    y_loc = [internal(f"y_loc{a}", [512, 2048], BF16) for a in range(2)]
    y_pair = [internal(f"y_pair{a}", [1024, 2048], BF16) for a in range(2)]
    vf = internal("vf", [1024, 2048], F32)

    def row_phase(i):
        kb.phase += 1
        es = contextlib.ExitStack()
        rs = RowState(kb, es)
        if i < 0:
            load_hT(rs, xT)
        else:
            pst = PleState(rs, es)
            selt = kb.sb("sel", [128, 2], F32, es); sel_b = Buf()
            kb.dma("sp", selt[:], sel_d, W=[sel_b])
            load_hT(rs, hspill)
            load_y_select(rs, y_pair, selt, sel_b)
            out_proj_residual(rs, wo[i])
            rmsnorm(rs, norms[i, 2])
            if not LITE:
                ffn(rs, wg[i, 1], wu[i, 1], wd[i, 1])
            rmsnorm(rs, norms[i, 3])
            if not LITE:
                ple(rs, pst, plg[i], plp[i], pT[i])
        if i == DEPTH - 1:
            rmsnorm(rs, fnorm, dst=rs.hT, dst_b=rs.hT_b)
            store_hT(rs, out, is_output=True)
        else:
            rmsnorm(rs, norms[i + 1, 0])
            if not LITE:
                ffn(rs, wg[i + 1, 0], wu[i + 1, 0], wd[i + 1, 0])
            rmsnorm(rs, norms[i + 1, 1])
            store_hT(rs, hspill, is_output=False)
            store_xn_split(rs, hn_loc)
        kb.barrier()
        es.close()

    def mix_phase(i):
        kb.phase += 1
        j = i // 2
        es = contextlib.ExitStack()
        hn_src = lambda c, r_: hn_pair[c // 8][r_ * 1024 + (c % 8) * 128:r_ * 1024 + (c % 8 + 1) * 128, :]
        y_rows = lambda r0: y_loc[r0 // 512][r0 % 512:r0 % 512 + 128, :]
        if i % 2 == 0:
            io = dict(ev_in[j]); io.update(layer=i, hn_src=hn_src, y_rows=y_rows, mask=cst["mask"], mask64=cst["mask64"], ident=cst["ident"], rmask=cst["rmaskE"])
            mix_even(kb, io, es)
        else:
            io = dict(od_in[j]); io.update(first=(j == 0), hn_src=hn_src, y_rows=y_rows, ident=cst["ident"], bones=cst["bones"], bones1=cst["bones1"],
                                           cmask=cst["cmask"], rmask=cst["rmaskO"], vf_out=vf, vf_in=vf)
            mix_odd(kb, io, es)
        kb.barrier()
        es.close()

    row_phase(-1)
    for i in range(DEPTH):
        for a in range(2):
            kb.allgather_pairs(hn_loc[a], hn_pair[a])
        kb.barrier()
        mix_phase(i)
        for a in range(2):
            kb.allgather_pairs(y_loc[a], y_pair[a])
        kb.barrier()
        row_phase(i)
    return kb.finish()


def kernel(**inputs):
    d = {k_: np.asarray(v_) for k_, v_ in inputs.items()}
    x = d["x"].astype(np.float32)
    tsl = lambda hf: slice(hf * 1024, (hf + 1) * 1024)
    shared = dict(ffn_wg=d["ffn_wg"], ffn_wu=d["ffn_wu"], ffn_wd=d["ffn_wd"], ple_wg=d["ple_wg"], ple_wp=d["ple_wp"],
                  norms_pc=np.ascontiguousarray(np.transpose(d["norms"].reshape(DEPTH, 4, 16, 128), (0, 1, 3, 2))), fnorm_pc=_pc(d["final_norm"]),
                  wo=np.ascontiguousarray(np.stack([_wo_perm(d, i) for i in range(DEPTH)])))
    shared.update(_consts())
    per_half = []
    for hf in range(2):
        m = {}
        for j in range(2):
            for n, v in _even_inputs(d, j, hf).items():
                m[f"e{j}_{n}"] = v
            for n, v in _odd_inputs(d, j, hf).items():
                m[f"o{j}_{n}"] = v
        sel = np.zeros((128, 2), np.float32); sel[:, hf] = 1.0
        m["sel"] = sel
        per_half.append(m)
    ims = []
    for c in range(NCORES):
        b, hf = c // 2, c % 2
        m = dict(shared); m.update(per_half[hf])
        m["xT"] = np.ascontiguousarray(x[b, tsl(hf)].T)
        m["pT"] = np.ascontiguousarray(np.transpose(d["p"][:, b, tsl(hf)], (0, 2, 1)))
        ims.append(m)
    nc = build_fused()
    res = run_bass_kernel_spmd(nc, ims, core_ids=list(range(NCORES))).results
    out = np.empty((4, 2048, 2048), np.float32)
    for c in range(NCORES):
        b, hf = c // 2, c % 2
        out[b, tsl(hf)] = res[c]["out"].T
    return out
```

```python
import contextlib
import numpy as np
import concourse.bass as bass
import concourse.mybir as mybir

F32 = mybir.dt.float32
BF16 = mybir.dt.bfloat16
AF = mybir.ActivationFunctionType
ALU = mybir.AluOpType
AX = mybir.AxisListType


class Buf:
    __slots__ = ("w", "r", "name", "excl")

    def __init__(self, name="", excl=False):
        self.w = {}
        self.r = {}
        self.name = name
        self.excl = excl


class Eng:
    def __init__(self, name, e, sem_key):
        self.name = name
        self.e = e
        self.key = sem_key
        self.cnt = 0
        self.waited = {}
        self.old = {}


class KB:
    N_DMA_SEMS = 8
    SEM_LIMIT = 30000

    def __init__(self):
        self.nc = bass.Bass("TRN2", target_bir_lowering=False)
        nc = self.nc
        self.es = contextlib.ExitStack()
        self.sems = []
        self.E = {}
        for name, e in (("pe", nc.tensor), ("act", nc.scalar), ("dve", nc.vector), ("pool", nc.gpsimd), ("sp", nc.sync)):
            key = self._new_sem("e_" + name)
            self.E[name] = Eng(name, e, key)
        self.dma_pool = {}
        for q in ("sp", "pool", "act"):
            keys = [self._new_sem(f"d_{q}{i}") for i in range(self.N_DMA_SEMS)]
            self.dma_pool[q] = {"keys": keys, "vals": [0] * len(keys), "next": 0}
        self.n_inst = 0
        self.out_events = []
        self.phase = 0
        self.extra = {}

    def _new_sem(self, name):
        s = self.es.enter_context(self.nc.semaphore(name))
        self.sems.append(s)
        return len(self.sems) - 1

    def sb(self, name, shape, dtype, es=None):
        t = (es or self.es).enter_context(self.nc.sbuf_tensor(f"s{self.phase}_" + name, list(shape), dtype))
        return t

    def ps(self, name, shape, dtype=F32, es=None):
        t = (es or self.es).enter_context(self.nc.psum_tensor(f"p{self.phase}_" + name, list(shape), dtype))
        return t

    def dram(self, name, shape, dtype, kind):
        return self.nc.dram_tensor(name, list(shape), dtype, kind=kind).ap()

    def _wait(self, eng, deps, skip_self=False):
        for k, v in deps.items():
            if skip_self and k == eng.key:
                continue
            if eng.waited.get(k, 0) >= v:
                continue
            eng.e.wait_ge(self.sems[k], v)
            eng.waited[k] = v
            self.n_inst += 1

    @staticmethod
    def _merge(dst, src):
        for k, v in src.items():
            if dst.get(k, 0) < v:
                dst[k] = v

    def op(self, engname, fn, R=(), W=()):
        eng = self.E[engname]
        deps = {}
        for b in R:
            self._merge(deps, b.w)
            if b.excl:
                for kk_, vv_ in b.r.items():
                    if kk_ != eng.key and deps.get(kk_, 0) < vv_:
                        deps[kk_] = vv_
        wdeps = {}
        for b in W:
            self._merge(wdeps, b.w)
            self._merge(wdeps, b.r)
        self._merge(deps, wdeps)
        if engname == "pe":
            for k_ in list(deps):
                if k_ == eng.key or k_ in eng.old:
                    del deps[k_]
        self._wait(eng, deps)
        if eng.cnt >= self.SEM_LIMIT:
            eng.old[eng.key] = eng.cnt
            eng.key = self._new_sem(f"e_{engname}_{len(eng.old)}")
            eng.cnt = 0
        ins = fn(eng.e)
        eng.cnt += 1
        ins.then_inc(self.sems[eng.key], 1)
        self.n_inst += 1
        ev = {eng.key: eng.cnt}
        for b in R:
            self._merge(b.r, ev)
        for b in W:
            b.w = dict(ev)
            b.r = {}
        return ins

    def dma(self, q, out, in_, R=(), W=(), is_output=False, **kw):
        eng = self.E[q]
        pool = self.dma_pool[q]
        i = pool["next"]
        pool["next"] = (i + 1) % len(pool["keys"])
        key = pool["keys"][i]
        deps = {}
        for b in R:
            self._merge(deps, b.w)
        for b in W:
            self._merge(deps, b.w)
            self._merge(deps, b.r)
        if pool["vals"][i] > 0:
            self._merge(deps, {key: pool["vals"][i]})
        self._wait(eng, deps)
        ins = eng.e.dma_start(out=out, in_=in_, **kw)
        pool["vals"][i] += 16
        ins.then_inc(self.sems[key], 16)
        self.n_inst += 1
        ev = {key: pool["vals"][i]}
        for b in R:
            self._merge(b.r, ev)
        for b in W:
            b.w = dict(ev)
            b.r = {}
        if is_output:
            self.out_events.append(ev)
        return ins

    def all_events(self):
        ev = {}
        for e in self.E.values():
            if e.cnt:
                ev[e.key] = e.cnt
            ev.update(e.old)
        for p in self.dma_pool.values():
            for k, v in zip(p["keys"], p["vals"]):
                if v:
                    ev[k] = v
        ev.update(self.extra)
        return ev

    def allgather_pairs(self, src, dst):
        eng = self.E["pool"]
        key = self._new_sem(f"cc{len(self.extra)}")
        ins = self.nc.gpsimd.collective_compute("AllGather", mybir.AluOpType.bypass, replica_groups=[[0, 1], [2, 3], [4, 5], [6, 7]],
                                                ins=[src], outs=[dst])
        ins.then_inc(self.sems[key], 1)
        self.n_inst += 1
        self.extra[key] = 1

    def barrier(self):
        ev = self.all_events()
        for eng in self.E.values():
            self._wait(eng, ev)

    def finish(self):
        ev = self.all_events()
        self._wait(self.E["sp"], ev)
        return self.nc


D = 2048
DC = 16
DFF = 5632
FC = 44
T = 1024
TH = 512
NTH = T // TH
EPS = 1e-6


class RowState:
    def __init__(self, kb, es=None):
        k = kb
        self.kb = kb
        self.hT = k.sb("hT", [128, DC, T], F32, es)
        self.hT_b = [[Buf(f"hT{c}_{t}") for t in range(NTH)] for c in range(DC)]
        self.xn = k.sb("xn", [128, DC, T], BF16, es)
        self.xn_b = [[Buf() for t in range(NTH)] for c in range(DC)]
        self.act = k.sb("act", [128, FC // 2, T], BF16, es)
        self.act_b = [[Buf() for t in range(NTH)] for c in range(FC // 2)]
        self.wA = [k.sb(f"wA{i}", [128, DC, 256], BF16, es) for i in range(4)]
        self.wA_b = [Buf() for _ in range(4)]
        self.wA_i = 0
        self.wD = [k.sb(f"wD{i}", [128, FC // 2, 128], BF16, es) for i in range(2)]
        self.wD_b = [Buf() for _ in range(2)]
        self.wD_i = 0
        self.psum = [k.ps(f"ps{i}", [128, TH], F32, es) for i in range(8)]
        self.psum_b = [Buf(excl=True) for _ in range(8)]
        self.ps_i = 0
        self.sq = [k.sb(f"sq{i}", [128, TH], BF16, es) for i in range(2)]
        self.sq_b = [Buf() for _ in range(2)]
        self.rstd = k.sb("rstd", [128, TH], F32, es)
        self.rstd_b = Buf()
        self.sg = [k.sb(f"sg{i}", [128, TH], F32, es) for i in range(2)]
        self.sg_b = [Buf() for _ in range(2)]
        self.sg_i = 0
        self.ones = k.sb("ones", [128, 128], BF16, es)
        self.ones_b = Buf()
        self.gcol = k.sb("gcol", [128, DC], F32, es)
        self.gcol_b = Buf()
        k.op("dve", lambda e: e.memset(self.ones[:], 1.0), W=[self.ones_b])

    def next_ps(self):
        i = self.ps_i
        self.ps_i = (i + 1) % 8
        return self.psum[i], self.psum_b[i]

    def next_wA(self):
        i = self.wA_i
        self.wA_i = (i + 1) % 4
        return self.wA[i], self.wA_b[i]

    def next_wD(self):
        i = self.wD_i
        self.wD_i = (i + 1) % 2
        return self.wD[i], self.wD_b[i]


def load_hT(rs, src):
    k = rs.kb
    v = src.rearrange("(c p) t -> p c t", p=128)
    for c in range(DC):
        k.dma("sp", rs.hT[:, c, :], v[:, c, :], W=rs.hT_b[c])


def store_hT(rs, dst, is_output=True):
    k = rs.kb
    v = dst.rearrange("(c p) t -> p c t", p=128)
    for c in range(DC):
        k.dma("sp", v[:, c, :], rs.hT[:, c, :], R=rs.hT_b[c], is_output=is_output)


def rmsnorm(rs, g_dram_pc, dst=None, dst_b=None, dst_dtype_scale=1.0):
    k = rs.kb
    dst = rs.xn if dst is None else dst
    dst_b = rs.xn_b if dst_b is None else dst_b
    k.dma("sp", rs.gcol[:], g_dram_pc, W=[rs.gcol_b])
    for th in range(NTH):
        ts = slice(th * TH, (th + 1) * TH)
        ps, ps_b = rs.next_ps()
        for c in range(DC):
            sq, sq_b = rs.sq[c % 2], rs.sq_b[c % 2]
            k.op("act", lambda e: e.activation(out=sq[:], in_=rs.hT[:, c, ts], func=AF.Square), R=[rs.hT_b[c][th]], W=[sq_b])
            k.op("pe", lambda e: e.matmul(ps[:], rs.ones[:], sq[:], start=(c == 0), stop=(c == DC - 1)),
                 R=[rs.ones_b, sq_b], W=[ps_b])
        k.op("dve", lambda e: e.tensor_scalar(out=rs.rstd[:], in0=ps[:], scalar1=float(1.0 / D), scalar2=float(EPS),
                                              op0=ALU.mult, op1=ALU.add), R=[ps_b], W=[rs.rstd_b])
        k.op("act", lambda e: e.activation(out=rs.rstd[:], in_=rs.rstd[:], func=AF.Sqrt), R=[rs.rstd_b], W=[rs.rstd_b])
        k.op("dve", lambda e: e.reciprocal(out=rs.rstd[:], in_=rs.rstd[:]), R=[rs.rstd_b], W=[rs.rstd_b])
        for c in range(DC):
            eng = "dve"
            k.op(eng, lambda e: e.scalar_tensor_tensor(out=dst[:, c, ts], in0=rs.hT[:, c, ts], scalar=rs.gcol[:, c:c + 1],
                                                       in1=rs.rstd[:], op0=ALU.mult, op1=ALU.mult),
                 R=[rs.hT_b[c][th], rs.gcol_b, rs.rstd_b], W=[dst_b[c][th]])


def load_w(rs, view, kc, ncols):
    k = rs.kb
    wt, wb = rs.next_wA()
    k.dma("pool", wt[:, :kc, :ncols], view, W=[wb])
    return wt, wb


def ffn(rs, wg, wu, wd):
    k = rs.kb
    wgv = wg.rearrange("(c p) f -> p c f", p=128)
    wuv = wu.rearrange("(c p) f -> p c f", p=128)
    wdv = wd.rearrange("(c p) d -> p c d", p=128)
    HF = FC // 2
    for fh in range(2):
        for fp in range(HF // 2):
            f0 = (fh * HF + fp * 2) * 128
            wgt, wgb = load_w(rs, wgv[:, :, f0:f0 + 256], DC, 256)
            wut, wub = load_w(rs, wuv[:, :, f0:f0 + 256], DC, 256)
            for fc in range(2):
                fl = fp * 2 + fc
                for th in range(NTH):
                    ts = slice(th * TH, (th + 1) * TH)
                    pg, pg_b = rs.next_ps()
                    pu, pu_b = rs.next_ps()
                    for c in range(DC):
                        k.op("pe", lambda e: e.matmul(pg[:], wgt[:, c, fc * 128:(fc + 1) * 128], rs.xn[:, c, ts],
                                                      start=(c == 0), stop=(c == DC - 1)),
                             R=[wgb, rs.xn_b[c][th]], W=[pg_b])
                    for c in range(DC):
                        k.op("pe", lambda e: e.matmul(pu[:], wut[:, c, fc * 128:(fc + 1) * 128], rs.xn[:, c, ts],
                                                      start=(c == 0), stop=(c == DC - 1)),
                             R=[wub, rs.xn_b[c][th]], W=[pu_b])
                    sg, sg_b = rs.sg[rs.sg_i], rs.sg_b[rs.sg_i]
                    rs.sg_i ^= 1
                    k.op("act", lambda e: e.activation(out=sg[:], in_=pg[:], func=AF.Silu), R=[pg_b], W=[sg_b])
                    k.op("dve", lambda e: e.tensor_tensor(out=rs.act[:, fl, ts], in0=sg[:], in1=pu[:], op=ALU.mult),
                         R=[sg_b, pu_b], W=[rs.act_b[fl][th]])
        for dc in range(DC):
            wdt, wdb = rs.next_wD()
            k.dma("pool", wdt[:], wdv[:, fh * HF:(fh + 1) * HF, dc * 128:(dc + 1) * 128], W=[wdb])
            for th in range(NTH):
                ts = slice(th * TH, (th + 1) * TH)
                po, po_b = rs.next_ps()
                for fl in range(HF):
                    k.op("pe", lambda e: e.matmul(po[:], wdt[:, fl, :], rs.act[:, fl, ts], start=(fl == 0), stop=(fl == HF - 1)),
                         R=[wdb, rs.act_b[fl][th]], W=[po_b])
                k.op("dve", lambda e: e.scalar_tensor_tensor(out=rs.hT[:, dc, ts], in0=po[:], scalar=0.5, in1=rs.hT[:, dc, ts],
                                                             op0=ALU.mult, op1=ALU.add),
                     R=[po_b, rs.hT_b[dc][th]], W=[rs.hT_b[dc][th]])


def load_xn_from(rs, src):
    k = rs.kb
    v = src.rearrange("(c p) t -> p c t", p=128)
    for c in range(DC):
        k.dma("sp", rs.xn[:, c, :], v[:, c, :], W=rs.xn_b[c])


def store_xn_to(rs, dst, is_output=True):
    k = rs.kb
    v = dst.rearrange("(c p) t -> p c t", p=128)
    for c in range(DC):
        k.dma("sp", v[:, c, :], rs.xn[:, c, :], R=rs.xn_b[c], is_output=is_output)


def square_proj(rs, w, evac):
    k = rs.kb
    wv = w.rearrange("(c p) n -> p c n", p=128)
    for op_ in range(DC // 2):
        wt, wb = load_w(rs, wv[:, :, op_ * 256:(op_ + 1) * 256], DC, 256)
        for o2 in range(2):
            oc = op_ * 2 + o2
            for th in range(NTH):
                ts = slice(th * TH, (th + 1) * TH)
                ps, ps_b = rs.next_ps()
                for c in range(DC):
                    k.op("pe", lambda e: e.matmul(ps[:], wt[:, c, o2 * 128:(o2 + 1) * 128], rs.xn[:, c, ts],
                                                  start=(c == 0), stop=(c == DC - 1)),
                         R=[wb, rs.xn_b[c][th]], W=[ps_b])
                evac(oc, th, ts, ps, ps_b)


def out_proj_residual(rs, w):
    k = rs.kb

    def evac(oc, th, ts, ps, ps_b):
        k.op("dve", lambda e: e.tensor_tensor(out=rs.hT[:, oc, ts], in0=ps[:], in1=rs.hT[:, oc, ts], op=ALU.add),
             R=[ps_b, rs.hT_b[oc][th]], W=[rs.hT_b[oc][th]])
    square_proj(rs, w, evac)


class PleState:
    def __init__(self, rs, es=None):
        k = rs.kb
        self.pT = k.sb("pT_sb", [128, 2, T], BF16, es)
        self.pT_b = Buf()
        self.wp = k.sb("wp_sb", [128, 2, D], BF16, es)
        self.wp_b = Buf()
        self.tmp = k.sb("pletmp", [128, TH], F32, es)
        self.tmp_b = Buf()


def ple(rs, pst, wgate, wp, pT_dram):
    k = rs.kb
    k.dma("pool", pst.pT[:], pT_dram.rearrange("(c p) t -> p c t", p=128), W=[pst.pT_b])
    k.dma("pool", pst.wp[:], wp.rearrange("(c p) n -> p c n", p=128), W=[pst.wp_b])

    def evac(oc, th, ts, ps, ps_b):
        sg, sg_b = rs.sg[rs.sg_i], rs.sg_b[rs.sg_i]
        rs.sg_i ^= 1
        k.op("act", lambda e: e.activation(out=sg[:], in_=ps[:], func=AF.Sigmoid), R=[ps_b], W=[sg_b])
        pp, pp_b = rs.next_ps()
        for c in range(2):
            k.op("pe", lambda e: e.matmul(pp[:], pst.wp[:, c, oc * 128:(oc + 1) * 128], pst.pT[:, c, ts], start=(c == 0), stop=(c == 1)),
                 R=[pst.wp_b, pst.pT_b], W=[pp_b])
        k.op("dve", lambda e: e.tensor_tensor(out=pst.tmp[:], in0=pp[:], in1=sg[:], op=ALU.mult), R=[pp_b, sg_b], W=[pst.tmp_b])
        k.op("dve", lambda e: e.tensor_tensor(out=rs.hT[:, oc, ts], in0=pst.tmp[:], in1=rs.hT[:, oc, ts], op=ALU.add),
             R=[pst.tmp_b, rs.hT_b[oc][th]], W=[rs.hT_b[oc][th]])
    square_proj(rs, wgate, evac)


def load_y_select(rs, y_pairs, sel, sel_b):
    k = rs.kb
    for c in range(DC):
        v = y_pairs[c // 8].rearrange("(c p) t -> p c t", p=128)
        k.dma("sp", rs.xn[:, c, :], v[:, c % 8, 0:T], W=rs.xn_b[c])
        k.dma("sp", rs.act[:, c, :], v[:, c % 8, T:2 * T], W=rs.act_b[c])
    for c in range(DC):
        k.op("pool", lambda e: e.tensor_scalar(out=rs.xn[:, c, :], in0=rs.xn[:, c, :], scalar1=sel[:, 0:1], scalar2=None, op0=ALU.mult),
             R=rs.xn_b[c] + [sel_b], W=rs.xn_b[c])
        k.op("dve", lambda e: e.scalar_tensor_tensor(out=rs.xn[:, c, :], in0=rs.act[:, c, :], scalar=sel[:, 1:2], in1=rs.xn[:, c, :],
                                                     op0=ALU.mult, op1=ALU.add),
             R=rs.act_b[c] + rs.xn_b[c] + [sel_b], W=rs.xn_b[c])


def store_xn_split(rs, dsts):
    k = rs.kb
    for c in range(DC):
        v = dsts[c // 8].rearrange("(c p) t -> p c t", p=128)
        k.dma("sp", v[:, c % 8, :], rs.xn[:, c, :], R=rs.xn_b[c])


S = 2048
NB = S // 512
NT = S // 128
NCH = S // 64
GC1 = 1.5957691216057308
GC2 = 0.044715


class PS:
    def __init__(self, kb, es=None):
        self.t = [kb.ps(f"ps{i}", [128, 512], F32, es) for i in range(8)]
        self.b = [Buf(excl=True) for _ in range(8)]
        self.i = 0
        self.ngen = 4

    def next(self):
        i = self.i
        self.i = (i + 1) % self.ngen
        return self.t[i], self.b[i]

    def fixed(self, i):
        return self.t[i], self.b[i]


def gelu_tanh(kb, out, x_ps, x_b, tmp, tmp_b, out_b, shape_slice=slice(None)):
    k = kb
    k.op("act", lambda e: e.activation(out=tmp, in_=x_ps, func=AF.Square), R=[x_b], W=[tmp_b])
    k.op("dve", lambda e: e.tensor_scalar(out=tmp, in0=tmp, scalar1=GC2, scalar2=1.0, op0=ALU.mult, op1=ALU.add), R=[tmp_b], W=[tmp_b])
    k.op("dve", lambda e: e.tensor_tensor(out=tmp, in0=tmp, in1=x_ps, op=ALU.mult), R=[tmp_b, x_b], W=[tmp_b])
    k.op("act", lambda e: e.activation(out=tmp, in_=tmp, func=AF.Sigmoid, scale=GC1), R=[tmp_b], W=[tmp_b])
    k.op("dve", lambda e: e.tensor_tensor(out=out, in0=tmp, in1=x_ps, op=ALU.mult), R=[tmp_b, x_b], W=[out_b])


def mix_even(kb, io, es=None):
    k = kb
    layer = io["layer"]
    ps = PS(kb, es)
    xn = k.sb("xnf", [128, DC, S], BF16, es)
    xn_b = [Buf() for _ in range(DC)]
    if io.get("hn_src") is not None:
        for c in range(DC):
            for r_ in range(2):
                k.dma("sp", xn[:, c, r_ * 1024:(r_ + 1) * 1024], io["hn_src"](c, r_), W=[xn_b[c]])
    else:
        hnv = io["hn"].rearrange("(c p) t -> p c t", p=128)
        for c in range(DC):
            k.dma("sp", xn[:, c, :], hnv[:, c, :], W=[xn_b[c]])
    w_in = io["w_in"].rearrange("(c p) n -> p c n", p=128)
    ones = k.sb("ones", [128, 128], BF16, es); ones_b = Buf()
    k.op("dve", lambda e: e.memset(ones[:], 1.0), W=[ones_b])
    ident = k.sb("ident", [128, 128], BF16, es); ident_b = Buf()
    k.dma("pool", ident[:], io["ident"], W=[ident_b])
    mask = k.sb("mask", [128, 128], F32, es); mask_b = Buf()
    k.dma("sp", mask[:], io["mask"], W=[mask_b])
    mask64 = k.sb("mask64", [128, 64], F32, es); mask64_b = Buf()
    k.dma("sp", mask64[:], io["mask64"], W=[mask64_b])
    rmask = k.sb("rmask", [128, S], F32, es); rmask_b = Buf()
    k.dma("sp", rmask[:], io["rmask"], W=[rmask_b])
    yrow = io.get("y_rows") or (lambda r0: io["yT"][r0:r0 + 128, :])

    wbig = [k.sb(f"wbig{i}", [128, DC, 512], BF16, es) for i in range(2)]
    wbig_b = [Buf(), Buf()]
    k.dma("pool", wbig[0][:], w_in[:, :, 512:1024], W=[wbig_b[0]])
    k.dma("pool", wbig[1][:], w_in[:, :, 0:512], W=[wbig_b[1]])
    esA = contextlib.ExitStack()
    gainv = k.sb("gainv", [128, 512], F32, esA); gainv_b = Buf()
    k.dma("sp", gainv[:], io["gainv"].partition_broadcast(128), W=[gainv_b])
    biasb = k.sb("biasb", [128, 4, 128], F32, esA); biasb_b = Buf()
    k.dma("sp", biasb[:].rearrange("p g t -> p (g t)"), io["bias"].partition_broadcast(128), W=[biasb_b])
    wsT = k.sb("wsT", [128, 4, 128], F32, esA); wsT_b = Buf()
    k.dma("sp", wsT[:], io["wsT"].rearrange("g s t -> s g t"), W=[wsT_b])
    wsm = k.sb("wsm", [128, 4, 128], BF16, esA); wsm_b = Buf()
    k.op("dve", lambda e: e.tensor_tensor(out=wsm[:], in0=wsT[:], in1=mask[:].unsqueeze(1).to_broadcast([128, 4, 128]), op=ALU.mult),
         R=[wsT_b, mask_b], W=[wsm_b])
    vg = k.sb("vg", [128, NT, 512], BF16, esA)
    vg_b = [Buf() for _ in range(NT)]
    tmpA = [k.sb(f"tmpA{i}", [128, 512], F32, esA) for i in range(2)]; tmpA_b = [Buf(), Buf()]
    gvt = [k.sb(f"gvt{i}", [128, 512], F32, esA) for i in range(2)]; gvt_b = [Buf(), Buf()]
    ssq = k.sb("ssq", [128, 4], F32, esA); ssq_b = Buf()
    for tt in range(NT):
        p_, p_b = ps.next()
        for c in range(DC):
            k.op("pe", lambda e: e.matmul(p_[:], xn[:, c, tt * 128:(tt + 1) * 128], wbig[0][:, c, :], start=(c == 0), stop=(c == DC - 1)),
                 R=[xn_b[c], wbig_b[0]], W=[p_b])
        i2 = tt % 2
        gelu_tanh(k, gvt[i2][:], p_[:], p_b, tmpA[i2][:], tmpA_b[i2], gvt_b[i2])
        k.op("dve", lambda e: e.tensor_tensor(out=tmpA[i2][:], in0=gvt[i2][:], in1=gvt[i2][:], op=ALU.mult), R=[gvt_b[i2]], W=[tmpA_b[i2]])
        k.op("dve", lambda e: e.tensor_reduce(out=ssq[:], in_=tmpA[i2][:].rearrange("p (g c) -> p g c", g=4), axis=AX.X, op=ALU.add),
             R=[tmpA_b[i2]], W=[ssq_b])
        k.op("dve", lambda e: e.tensor_scalar(out=ssq[:], in0=ssq[:], scalar1=1.0 / 128, scalar2=1e-6, op0=ALU.mult, op1=ALU.add), R=[ssq_b], W=[ssq_b])
        k.op("act", lambda e: e.activation(out=ssq[:], in_=ssq[:], func=AF.Sqrt), R=[ssq_b], W=[ssq_b])
        k.op("dve", lambda e: e.reciprocal(out=ssq[:], in_=ssq[:]), R=[ssq_b], W=[ssq_b])
        k.op("dve", lambda e: e.tensor_tensor(out=gvt[i2][:].rearrange("p (g c) -> p g c", g=4), in0=gvt[i2][:].rearrange("p (g c) -> p g c", g=4),
                                              in1=ssq[:].unsqueeze(2).to_broadcast([128, 4, 128]), op=ALU.mult), R=[gvt_b[i2], ssq_b], W=[gvt_b[i2]])
        k.op("dve", lambda e: e.tensor_tensor(out=vg[:, tt, :], in0=gvt[i2][:], in1=gainv[:], op=ALU.mult), R=[gvt_b[i2], gainv_b], W=[vg_b[tt]])
    yst = [k.sb(f"yst{i}", [128, 512], BF16, esA) for i in range(2)]; yst_b = [Buf(), Buf()]
    yi = 0
    for g in range(4):
        for blk in range(NB):
            bs = slice(blk * 512, (blk + 1) * 512)
            pu, pu_b = ps.next()
            for c in range(DC):
                k.op("pe", lambda e: e.matmul(pu[:], wbig[1][:, c, g * 128:(g + 1) * 128], xn[:, c, bs], start=(c == 0), stop=(c == DC - 1)),
                     R=[xn_b[c], wbig_b[1]], W=[pu_b])
            pss, pss_b = ps.next()
            for q in range(4):
                tt = blk * 4 + q
                k.op("pe", lambda e: e.matmul(pss[:, q * 128:(q + 1) * 128], vg[:, tt, g * 128:(g + 1) * 128], wsm[:, g, :], start=True, stop=True),
                     R=[vg_b[tt], wsm_b], W=[pss_b])
            i2 = yi % 2
            gelu_tanh(k, gvt[i2][:], pu[:], pu_b, tmpA[i2][:], tmpA_b[i2], gvt_b[i2])
            k.op("dve", lambda e: e.tensor_tensor(out=tmpA[i2][:].rearrange("p (q t) -> p q t", q=4), in0=pss[:].rearrange("p (q t) -> p q t", q=4),
                                                  in1=biasb[:, g, :].unsqueeze(1).to_broadcast([128, 4, 128]), op=ALU.add),
                 R=[pss_b, biasb_b], W=[tmpA_b[i2]])
            k.op("dve", lambda e: e.tensor_tensor(out=yst[i2][:], in0=gvt[i2][:], in1=tmpA[i2][:], op=ALU.mult), R=[gvt_b[i2], tmpA_b[i2]], W=[yst_b[i2]])
            k.dma("sp", yrow(g * 128)[:, bs], yst[i2][:], R=[yst_b[i2]], is_output=True)
            yi += 1

    k.barrier()
    esA.close()
    lbl = k.sb("lbl", [128, 4, 4], F32, es); lbl_b = Buf()
    k.dma("sp", lbl[:], io["lbl"], W=[lbl_b])
    k.op("act", lambda e: e.activation(out=lbl[:], in_=lbl[:], func=AF.Exp), R=[lbl_b], W=[lbl_b])
    tot = k.sb("tot", [128, 4], F32, es); tot_b = Buf()
    lb = k.sb("lb", [128, 4], F32, es); lb_b = Buf()
    oml = k.sb("oml", [128, 4], F32, es); oml_b = Buf()
    k.op("dve", lambda e: e.tensor_reduce(out=tot[:], in_=lbl[:], axis=AX.X, op=ALU.add), R=[lbl_b], W=[tot_b])
    k.op("dve", lambda e: e.reciprocal(out=tot[:], in_=tot[:]), R=[tot_b], W=[tot_b])
    if layer == 0:
        k.op("dve", lambda e: e.memset(lb[:], 0.0), W=[lb_b])
    else:
        k.op("dve", lambda e: e.tensor_reduce(out=lb[:], in_=lbl[:, :, 1:layer + 1], axis=AX.X, op=ALU.add), R=[lbl_b], W=[lb_b])
        k.op("dve", lambda e: e.tensor_tensor(out=lb[:], in0=lb[:], in1=tot[:], op=ALU.mult), R=[lb_b, tot_b], W=[lb_b])
    k.op("dve", lambda e: e.tensor_scalar(out=oml[:], in0=lb[:], scalar1=-1.0, scalar2=1.0, op0=ALU.mult, op1=ALU.add), R=[lb_b], W=[oml_b])
    onorm = k.sb("onorm", [128, 4], F32, es); onorm_b = Buf()
    k.dma("sp", onorm[:], io["onorm"], W=[onorm_b])

    def big(name, dt=F32):
        return k.sb(name, [128, S], dt, es), Buf()
    B1, B1_b = big("B1"); B2, B2_b = big("B2"); B3, B3_b = big("B3"); B4, B4_b = big("B4"); G1, G1_b = big("G1")
    Q1, Q1_b = big("Q1", BF16); Q2, Q2_b = big("Q2", BF16); K1, K1_b = big("K1", BF16); K2, K2_b = big("K2", BF16); I1, I1_b = big("I1", BF16)
    Vt = k.sb("Vt", [128, NT, 128], BF16, es); Vt_b = Buf()
    Kt = k.sb("Kt", [128, NT, 128], BF16, es); Kt_b = Buf()
    att = k.sb("att", [128, NT, 64], BF16, es); att_b = Buf()
    oT, oT_b = big("oT")
    cmid = k.sb("cmid", [128, NCH], F32, es); cmid_b = Buf()
    cend = k.sb("cend", [128, NCH], F32, es); cend_b = Buf()
    decL = k.sb("decL", [128, NCH], F32, es); decL_b = Buf()
    Sf = k.sb("Sf", [128, 128], F32, es); Sf_b = Buf()
    Sb = k.sb("Sb", [128, 128], BF16, es); Sb_b = Buf()
    sqb = [k.sb(f"sqb{i}", [128, 512], BF16, es) for i in range(2)]; sqb_b = [Buf(), Buf()]
    rst = k.sb("rst", [128, 512], F32, es); rst_b = Buf()
    ybig, ybig_b = big("ybig", BF16)

    def proj(col0, evac):
        for blk in range(NB):
            bs = slice(blk * 512, (blk + 1) * 512)
            p_, p_b = ps.next()
            for c in range(DC):
                k.op("pe", lambda e: e.matmul(p_[:], wh[:, c, col0:col0 + 128], xn[:, c, bs], start=(c == 0), stop=(c == DC - 1)),
                     R=[xn_b[c], wh_b], W=[p_b])
            evac(blk, bs, p_, p_b)

    c3 = lambda ap: ap.rearrange("p (n t) -> p n t", t=64)
    for hd in range(4):
        wh, wh_b = wbig[hd % 2], wbig_b[hd % 2]
        k.dma("pool", wh[:], w_in[:, :, 1024 + hd * 512:1024 + (hd + 1) * 512], W=[wh_b])
        proj(0, lambda blk, bs, p_, p_b: k.op("act", lambda e: e.activation(out=B4[:, bs], in_=p_[:], func=AF.Silu), R=[p_b], W=[B4_b]))
        proj(128, lambda blk, bs, p_, p_b: k.op("act", lambda e: e.activation(out=B1[:, bs], in_=p_[:], func=AF.Sigmoid), R=[p_b], W=[B1_b]))
        proj(256, lambda blk, bs, p_, p_b: k.op("act", lambda e: e.activation(out=I1[:, bs], in_=p_[:], func=AF.Copy), R=[p_b], W=[I1_b]))
        proj(384, lambda blk, bs, p_, p_b: k.op("act", lambda e: e.activation(out=G1[:, bs], in_=p_[:], func=AF.Silu), R=[p_b], W=[G1_b]))
        k.op("dve", lambda e: e.tensor_scalar(out=B1[:], in0=B1[:], scalar1=oml[:, hd:hd + 1], scalar2=lb[:, hd:hd + 1], op0=ALU.mult, op1=ALU.add),
             R=[B1_b, oml_b, lb_b], W=[B1_b])
        k.op("dve", lambda e: e.tensor_scalar(out=B3[:], in0=B1[:], scalar1=-1.0, scalar2=1.0, op0=ALU.mult, op1=ALU.add), R=[B1_b], W=[B3_b])
        k.op("dve", lambda e: e.tensor_scalar_max(out=B1[:], in0=B1[:], scalar1=1e-30), R=[B1_b], W=[B1_b])
        k.op("act", lambda e: e.activation(out=B1[:], in_=B1[:], func=AF.Ln), R=[B1_b], W=[B1_b])
        k.op("dve", lambda e: e.tensor_tensor_scan(out=B2[:], data0=rmask[:], data1=B1[:], initial=0.0, op0=ALU.mult, op1=ALU.add),
             R=[rmask_b, B1_b], W=[B2_b])
        k.op("dve", lambda e: e.tensor_copy(out=cmid[:], in_=c3(B2[:])[:, :, 31]), R=[B2_b], W=[cmid_b])
        k.op("dve", lambda e: e.tensor_copy(out=cend[:], in_=c3(B2[:])[:, :, 63]), R=[B2_b], W=[cend_b])
        k.op("act", lambda e: e.activation(out=decL[:], in_=cend[:], func=AF.Exp), R=[cend_b], W=[decL_b])
        k.op("act", lambda e: e.activation(out=B1[:], in_=B2[:], func=AF.Exp), R=[B2_b], W=[B1_b])
        k.op("dve", lambda e: e.tensor_tensor(out=Q1[:], in0=B4[:], in1=B1[:], op=ALU.mult), R=[B4_b, B1_b], W=[Q1_b])
        k.op("dve", lambda e: e.tensor_tensor(out=c3(B1[:]), in0=c3(B2[:]), in1=cmid[:].unsqueeze(2).to_broadcast([128, NCH, 64]), op=ALU.subtract),
             R=[B2_b, cmid_b], W=[B1_b])
        k.op("act", lambda e: e.activation(out=oT[:], in_=B1[:], func=AF.Exp), R=[B1_b], W=[oT_b])
        k.op("dve", lambda e: e.tensor_tensor(out=Q2[:], in0=B4[:], in1=oT[:], op=ALU.mult), R=[B4_b, oT_b], W=[Q2_b])
        k.op("act", lambda e: e.activation(out=B1[:], in_=B1[:], func=AF.Exp, scale=-1.0), R=[B1_b], W=[B1_b])
        k.op("dve", lambda e: e.tensor_tensor(out=K1[:], in0=B3[:], in1=B1[:], op=ALU.mult), R=[B3_b, B1_b], W=[K1_b])
        k.op("dve", lambda e: e.tensor_tensor(out=c3(B1[:]), in0=cend[:].unsqueeze(2).to_broadcast([128, NCH, 64]), in1=c3(B2[:]), op=ALU.subtract),
             R=[B2_b, cend_b], W=[B1_b])
        k.op("act", lambda e: e.activation(out=B1[:], in_=B1[:], func=AF.Exp), R=[B1_b], W=[B1_b])
        k.op("dve", lambda e: e.tensor_tensor(out=K2[:], in0=B3[:], in1=B1[:], op=ALU.mult), R=[B3_b, B1_b], W=[K2_b])
        for src, src_b, dst, dst_b in ((I1, I1_b, Vt, Vt_b), (K2, K2_b, Kt, Kt_b)):
            for q4 in range(NT // 4):
                p_, p_b = ps.next()
                pbf = p_[:].bitcast(BF16)
                for q in range(4):
                    tt = q4 * 4 + q
                    k.op("pe", lambda e: e.transpose(pbf[:, q * 128:(q + 1) * 128], src[:, tt * 128:(tt + 1) * 128], ident[:]),
                         R=[src_b, ident_b], W=[p_b])
                k.op("act", lambda e: e.activation(out=dst[:, q4 * 4:(q4 + 1) * 4, :], in_=pbf[:, 0:512].rearrange("p (q c) -> p q c", q=4), func=AF.Copy),
                     R=[p_b], W=[dst_b])
        for q4 in range(NT // 4):
            p_, p_b = ps.next()
            for q in range(4):
                tt = q4 * 4 + q
                k.op("pe", lambda e: e.matmul(p_[:, q * 128:(q + 1) * 128], K1[:, tt * 128:(tt + 1) * 128], Q2[:, tt * 128:(tt + 1) * 128], start=True, stop=True),
                     R=[K1_b, Q2_b], W=[p_b])
            pv = p_[:].rearrange("p (q t) -> p q t", q=4)
            for half in range(2):
                prt = slice(half * 64, (half + 1) * 64)
                k.op("dve", lambda e: e.tensor_tensor(out=att[prt, q4 * 4:(q4 + 1) * 4, :], in0=pv[prt, :, half * 64:(half + 1) * 64],
                                                      in1=mask64[prt, :].unsqueeze(1).to_broadcast([64, 4, 64]), op=ALU.mult),
                     R=[p_b, mask64_b], W=[att_b])
        k.op("dve", lambda e: e.memset(Sf[:], 0.0), W=[Sf_b])
        k.op("dve", lambda e: e.memset(Sb[:], 0.0), W=[Sb_b])
        psS = [None] * NCH

        def emit_mmS(n):
            bank, bank_b = ps.fixed(6 + n % 2)
            psS[n] = (bank, bank_b)
            tt, prt = n // 2, slice((n % 2) * 64, (n % 2) * 64 + 64)
            k.op("pe", lambda e: e.matmul(bank[:, 0:128], Kt[prt, tt, :], Vt[prt, tt, :], start=True, stop=True), R=[Kt_b, Vt_b], W=[bank_b])
        emit_mmS(0)
        po = po_b = None
        for n in range(NCH):
            tt, prt = n // 2, slice((n % 2) * 64, (n % 2) * 64 + 64)
            if n % 8 == 0:
                po, po_b = ps.fixed(4 + (n // 8) % 2)
            osl = po[:, (n % 8) * 64:(n % 8 + 1) * 64]
            k.op("pe", lambda e: e.matmul(osl, Vt[prt, tt, :], att[prt, tt, :], start=True, stop=False), R=[Vt_b, att_b], W=[po_b])
            if n + 1 < NCH:
                emit_mmS(n + 1)
            k.op("pe", lambda e: e.matmul(osl, Sb[:], Q1[:, n * 64:(n + 1) * 64], start=False, stop=True), R=[Sb_b, Q1_b], W=[po_b])
            bank, bank_b = psS[n]
            k.op("dve", lambda e: e.scalar_tensor_tensor(out=Sf[:], in0=Sf[:], scalar=decL[:, n:n + 1], in1=bank[:, 0:128], op0=ALU.mult, op1=ALU.add),
                 R=[Sf_b, decL_b, bank_b], W=[Sf_b])
            k.op("act", lambda e: e.activation(out=Sb[:], in_=Sf[:], func=AF.Copy), R=[Sf_b], W=[Sb_b])
            if n % 8 == 7:
                blk = n // 8
                k.op("act", lambda e: e.activation(out=oT[:, blk * 512:(blk + 1) * 512], in_=po[:], func=AF.Copy), R=[po_b], W=[oT_b])
        for blk in range(NB):
            bs = slice(blk * 512, (blk + 1) * 512)
            i2 = blk % 2
            k.op("act", lambda e: e.activation(out=sqb[i2][:], in_=oT[:, bs], func=AF.Square), R=[oT_b], W=[sqb_b[i2]])
            p_, p_b = ps.next()
            k.op("pe", lambda e: e.matmul(p_[:], ones[:], sqb[i2][:], start=True, stop=True), R=[ones_b, sqb_b[i2]], W=[p_b])
            k.op("dve", lambda e: e.tensor_scalar(out=rst[:], in0=p_[:], scalar1=1.0 / 128, scalar2=1e-6, op0=ALU.mult, op1=ALU.add), R=[p_b], W=[rst_b])
            k.op("act", lambda e: e.activation(out=rst[:], in_=rst[:], func=AF.Sqrt), R=[rst_b], W=[rst_b])
            k.op("dve", lambda e: e.reciprocal(out=rst[:], in_=rst[:]), R=[rst_b], W=[rst_b])
            k.op("dve", lambda e: e.scalar_tensor_tensor(out=rst[:], in0=oT[:, bs], scalar=onorm[:, hd:hd + 1], in1=rst[:], op0=ALU.mult, op1=ALU.mult),
                 R=[oT_b, onorm_b, rst_b], W=[rst_b])
            k.op("dve", lambda e: e.tensor_tensor(out=ybig[:, bs], in0=rst[:], in1=G1[:, bs], op=ALU.mult), R=[rst_b, G1_b], W=[ybig_b])
        k.dma("sp", yrow(512 + hd * 128), ybig[:], R=[ybig_b], is_output=True)


L = 64
CPB = 8
WSCALE = -0.6065306597126334
GN_EPS = 64e-5
BT = 512


def mix_odd(kb, io, es=None):
    k = kb
    first = io["first"]
    DC_ = 16
    S_ = 2048
    NBLK = S_ // BT
    pst = [k.ps(f"ps{i}", [128, 512], F32, es) for i in range(8)]
    psb = [Buf(excl=True) for _ in range(8)]
    rot = [0]

    def ps_next():
        i = rot[0]
        rot[0] = (i + 1) % 4
        return pst[i], psb[i]

    def cst(name, shape, dt, src, q="sp"):
        t = k.sb(name, shape, dt, es)
        b = Buf()
        k.dma(q, t[:], src, W=[b])
        return t, b
    ident, ident_b = cst("ident", [128, 128], BF16, io["ident"], "pool")
    bones, bones_b = cst("bones", [128, 128], BF16, io["bones"], "pool")
    bones1, bones1_b = cst("bones1", [128, 128], BF16, io["bones1"], "pool")
    cmask, cmask_b = cst("cmask", [64, 5, 64], F32, io["cmask"])
    rmask, rmask_b = cst("rmask", [128, BT], F32, io["rmask"][:, 0:BT])
    mixm, mixm_b = cst("mixm", [128, 6, DC_], F32, io["mixpc"])
    cols, cols_b = cst("cols", [128, 8, 8], F32, io["cols"])
    mixo = k.sb("mixo", [128, 6, DC_], F32, es); mixo_b = Buf()
    k.op("dve", lambda e: e.tensor_scalar(out=mixo[:], in0=mixm[:], scalar1=-1.0, scalar2=1.0, op0=ALU.mult, op1=ALU.add), R=[mixm_b], W=[mixo_b])
    omka = k.sb("omka", [128, 8], F32, es); omka_b = Buf()
    k.op("dve", lambda e: e.tensor_scalar(out=omka[:], in0=cols[:, :, 3], scalar1=-1.0, scalar2=1.0, op0=ALU.mult, op1=ALU.add), R=[cols_b], W=[omka_b])
    w2t, w2t_b = cst("w2t", [96, 1024], BF16, io["w2"], "pool")
    a2t, a2t_b = cst("a2t", [96, 1024], BF16, io["a2"], "pool")
    g2t, g2t_b = cst("g2t", [128, 2, 1024], BF16, io["g2"].rearrange("(c p) n -> p c n", p=128), "pool")
    if not first:
        v2t, v2t_b = cst("v2t", [64, 1024], BF16, io["v2"], "pool")
    wv_ = lambda name: io[name].rearrange("(c p) n -> p c n", p=128)
    wr, wk, wv = wv_("wr"), wv_("wk"), wv_("wv")
    yrow = io.get("y_rows") or (lambda r0: io["yT"][r0:r0 + 128, :])
    c3 = lambda ap: ap.rearrange("p (n t) -> p n t", t=L)

    def colv(pr, vi):
        return cols[:, pr, vi:vi + 1]

    xe = k.sb("xe", [128, DC_, BT + 8], BF16, es)
    xe_b = [Buf() for _ in range(DC_)]
    xcar = k.sb("xcar", [128, DC_, 1], BF16, es); xcar_b = Buf()
    k.op("pool", lambda e: e.memset(xcar[:], 0.0), W=[xcar_b])
    hw = k.sb("hw", [96, BT], BF16, es); hw_b = Buf()
    ha = k.sb("ha", [96, BT], BF16, es); ha_b = Buf()
    hg = k.sb("hg", [128, 2, BT], BF16, es); hg_b = Buf()
    if not first:
        hv = k.sb("hv", [64, BT], BF16, es); hv_b = Buf()
    Mf = k.sb("Mf", [128, 8, 128], F32, es); Mf_b = [Buf() for _ in range(8)]
    Mb = k.sb("Mb", [128, 8, 128], BF16, es); Mb_b = [Buf() for _ in range(8)]
    k.op("dve", lambda e: e.memset(Mf[:], 0.0), W=Mf_b)
    k.op("pool", lambda e: e.memset(Mb[:], 0.0), W=Mb_b)

    class TS:
        pass

    def make_ts(sfx, py_bank, pch_bank):
        t = TS()
        t.sfx = sfx
        t.py, t.py_b = pst[py_bank], psb[py_bank]
        t.pch, t.pch_b = pst[pch_bank], psb[pch_bank]

        def big(name, dt=F32):
            return k.sb(name + sfx, [128, BT], dt, es), Buf()
        t.V, t.V_b = big("V"); t.K, t.K_b = big("K"); t.A, t.A_b = big("A"); t.KK, t.KK_b = big("KK"); t.CUM, t.CUM_b = big("CUM")
        t.T1, t.T1_b = big("T1"); t.T2, t.T2_b = big("T2"); t.Vb, t.Vb_b = big("Vb", BF16); t.G, t.G_b = big("G", BF16)
        t.AR = k.sb("AR" + sfx, [128, CPB, 2, L], BF16, es); t.AR_b = Buf()
        t.BK = k.sb("BK" + sfx, [128, CPB, 2, L], BF16, es); t.BK_b = Buf()
        t.BhKh = k.sb("BhKh" + sfx, [128, 2, BT], BF16, es); t.BhKh_b = Buf()
        t.cend = k.sb("cend" + sfx, [128, CPB], F32, es); t.cend_b = Buf()
        t.decL = k.sb("decL" + sfx, [128, CPB], F32, es); t.decL_b = Buf()
        t.sqb = k.sb("sqb" + sfx, [128, 512], BF16, es); t.sqb_b = Buf()
        t.pads = []
        for nm in ("Vpad", "Bpad", "Kpad"):
            tt = k.sb(nm + sfx, [64, CPB, 2, 128], BF16, es); bb = Buf()
            k.op("pool", lambda e: e.memset(tt[:], 0.0), W=[bb])
            t.pads.append((tt, bb))
        t.sc = {}
        for nm in ("N", "NT", "N2", "NT2", "X", "X2", "AAK", "PBR", "PKR"):
            t.sc[nm] = (k.sb("sc" + nm + sfx, [64, 2, CPB, L], BF16, es), Buf())
        t.W0 = k.sb("W0" + sfx, [64, 2, L], BF16, es); t.W0_b = Buf()
        t.Upad = k.sb("Upad" + sfx, [64, 384], BF16, es); t.Upad_b = Buf()
        k.op("pool", lambda e: e.memset(t.Upad[:], 0.0), W=[t.Upad_b])
        t.Upad_w = t.Upad[:, 0:384].rearrange("p (h w) -> p h w", w=192)[:, :, 0:64]
        t.Upad_r = t.Upad[:, 0:256].rearrange("p (h w) -> p h w", w=128)
        t.yf = k.sb("yf" + sfx, [128, 512], F32, es); t.yf_b = Buf()
        t.ysb = k.sb("ysb" + sfx, [128, 512], BF16, es); t.ysb_b = Buf()
        t.ysq = k.sb("ysq" + sfx, [128, 512], BF16, es); t.ysq_b = Buf()
        t.t3 = k.sb("t3" + sfx, [128, 512], F32, es); t.t3_b = Buf()
        t.yo = k.sb("yo" + sfx, [128, 512], BF16, es); t.yo_b = Buf()
        t.stage = k.sb("stage" + sfx, [128, DC_, 128], F32, es); t.stage_b = Buf()
        t.Wa = k.sb("Wa" + sfx, [128, DC_, 128], BF16, es); t.Wa_b = Buf()
        t.Wb = k.sb("Wb" + sfx, [128, DC_, 128], BF16, es); t.Wb_b = Buf()
        return t

    def load_mixed(t, wview, ncols, mi):
        k.dma("sp", t.stage[:, :, :ncols], wview, W=[t.stage_b])
        k.op("pool", lambda e: e.tensor_tensor(out=t.Wa[:, :, :ncols], in0=t.stage[:, :, :ncols],
                                               in1=mixo[:, mi, :].unsqueeze(2).to_broadcast([128, DC_, ncols]), op=ALU.mult),
             R=[t.stage_b, mixo_b], W=[t.Wa_b])
        k.op("pool", lambda e: e.tensor_tensor(out=t.Wb[:, :, :ncols], in0=t.stage[:, :, :ncols],
                                               in1=mixm[:, mi, :].unsqueeze(2).to_broadcast([128, DC_, ncols]), op=ALU.mult),
             R=[t.stage_b, mixm_b], W=[t.Wb_b])

    def mixed_proj(t, ncols, evac):
        p_, p_b = ps_next()
        for c in range(DC_):
            k.op("pe", lambda e: e.matmul(p_[:ncols, :], t.Wa[:, c, 0:ncols], xe[:, c, 8:8 + BT], start=(c == 0), stop=False),
                 R=[t.Wa_b, xe_b[c]], W=[p_b])
        for c in range(DC_):
            k.op("pe", lambda e: e.matmul(p_[:ncols, :], t.Wb[:, c, 0:ncols], xe[:, c, 7:7 + BT], start=False, stop=(c == DC_ - 1)),
                 R=[t.Wb_b, xe_b[c]], W=[p_b])
        evac(p_, p_b)

    tsA = make_ts("A", 4, 6)
    tsB = make_ts("B", 5, 7)

    def unit(t, pr, blk):
        pc = slice(pr * 128, (pr + 1) * 128)
        tb = blk * BT
        V, V_b, K, K_b, A, A_b, KK, KK_b, CUM, CUM_b, T1, T1_b, T2, T2_b = t.V, t.V_b, t.K, t.K_b, t.A, t.A_b, t.KK, t.KK_b, t.CUM, t.CUM_b, t.T1, t.T1_b, t.T2, t.T2_b
        Vb, Vb_b, G, G_b, AR, AR_b, BK, BK_b, BhKh, BhKh_b = t.Vb, t.Vb_b, t.G, t.G_b, t.AR, t.AR_b, t.BK, t.BK_b, t.BhKh, t.BhKh_b
        cend, cend_b, decL, decL_b, sqb, sqb_b = t.cend, t.cend_b, t.decL, t.decL_b, t.sqb, t.sqb_b
        load_mixed(t, wv[:, :, pc], 128, 3)
        yield
        mixed_proj(t, 128, lambda p_, p_b: k.op("act", lambda e: e.activation(out=V[:], in_=p_[:], func=AF.Copy), R=[p_b], W=[V_b]))
        yield
        if first:
            k.dma("sp", io["vf_out"][pc, tb:tb + BT], V[:], R=[V_b], is_output=True)
        else:
            k.dma("sp", T1[:], io["vf_in"][pc, tb:tb + BT], W=[T1_b])
            p_, p_b = ps_next()
            k.op("pe", lambda e: e.matmul(p_[:], v2t[:, pc], hv[:], start=True, stop=True), R=[v2t_b, hv_b], W=[p_b])
            k.op("act", lambda e: e.activation(out=T2[:], in_=p_[:], func=AF.Sigmoid, bias=colv(pr, 7)), R=[p_b, cols_b], W=[T2_b])
            k.op("dve", lambda e: e.tensor_tensor(out=T1[:], in0=T1[:], in1=V[:], op=ALU.subtract), R=[T1_b, V_b], W=[T1_b])
            k.op("dve", lambda e: e.tensor_tensor(out=T1[:], in0=T1[:], in1=T2[:], op=ALU.mult), R=[T1_b, T2_b], W=[T1_b])
            k.op("dve", lambda e: e.tensor_tensor(out=V[:], in0=V[:], in1=T1[:], op=ALU.add), R=[T1_b, V_b], W=[V_b])
        k.op("act", lambda e: e.activation(out=Vb[:], in_=V[:], func=AF.Copy), R=[V_b], W=[Vb_b])
        load_mixed(t, wk[:, :, pc], 128, 2)
        yield
        mixed_proj(t, 128, lambda p_, p_b: k.op("act", lambda e: e.activation(out=K[:], in_=p_[:], func=AF.Copy), R=[p_b], W=[K_b]))
        yield
        p_, p_b = ps_next()
        k.op("pe", lambda e: e.matmul(p_[:], a2t[:, pc], ha[:], start=True, stop=True), R=[a2t_b, ha_b], W=[p_b])
        k.op("act", lambda e: e.activation(out=A[:], in_=p_[:], func=AF.Sigmoid, bias=colv(pr, 1)), R=[p_b, cols_b], W=[A_b])
        p_, p_b = ps_next()
        k.op("pe", lambda e: e.matmul(p_[:], w2t[:, pc], hw[:], start=True, stop=True), R=[w2t_b, hw_b], W=[p_b])
        k.op("act", lambda e: e.activation(out=T1[:], in_=p_[:], func=AF.Sigmoid, bias=colv(pr, 0)), R=[p_b, cols_b], W=[T1_b])
        p_, p_b = ps_next()
        for gc in range(2):
            k.op("pe", lambda e: e.matmul(p_[:], g2t[:, gc, pc], hg[:, gc, :], start=(gc == 0), stop=(gc == 1)), R=[g2t_b, hg_b], W=[p_b])
        k.op("act", lambda e: e.activation(out=G[:], in_=p_[:], func=AF.Copy), R=[p_b], W=[G_b])
        yield
        k.op("dve", lambda e: e.tensor_scalar(out=T1[:], in0=T1[:], scalar1=WSCALE, scalar2=None, op0=ALU.mult), R=[T1_b], W=[T1_b])
        k.op("dve", lambda e: e.tensor_tensor_scan(out=CUM[:], data0=rmask[:], data1=T1[:], initial=0.0, op0=ALU.mult, op1=ALU.add),
             R=[rmask_b, T1_b], W=[CUM_b])
        k.op("dve", lambda e: e.tensor_copy(out=cend[:], in_=c3(CUM[:])[:, :, L - 1]), R=[CUM_b], W=[cend_b])
        k.op("act", lambda e: e.activation(out=decL[:], in_=cend[:], func=AF.Exp), R=[cend_b], W=[decL_b])
        k.op("dve", lambda e: e.tensor_scalar(out=KK[:], in0=K[:], scalar1=colv(pr, 2), scalar2=None, op0=ALU.mult), R=[K_b, cols_b], W=[KK_b])
        k.op("act", lambda e: e.activation(out=sqb[:], in_=KK[:], func=AF.Square), R=[KK_b], W=[sqb_b])
        p_, p_b = ps_next()
        k.op("pe", lambda e: e.matmul(p_[:], bones1[:], sqb[:], start=True, stop=True), R=[bones1_b, sqb_b], W=[p_b])
        k.op("dve", lambda e: e.tensor_scalar_max(out=T2[:], in0=p_[:], scalar1=1e-24), R=[p_b], W=[T2_b])
        yield
        k.op("act", lambda e: e.activation(out=T2[:], in_=T2[:], func=AF.Sqrt), R=[T2_b], W=[T2_b])
        k.op("dve", lambda e: e.reciprocal(out=T2[:], in_=T2[:]), R=[T2_b], W=[T2_b])
        k.op("dve", lambda e: e.tensor_tensor(out=KK[:], in0=KK[:], in1=T2[:], op=ALU.mult), R=[KK_b, T2_b], W=[KK_b])
        k.op("dve", lambda e: e.tensor_scalar(out=T2[:], in0=A[:], scalar1=colv(pr, 3), scalar2=omka[:, pr:pr + 1], op0=ALU.mult, op1=ALU.add),
             R=[A_b, cols_b, omka_b], W=[T2_b])
        k.op("dve", lambda e: e.tensor_tensor(out=K[:], in0=K[:], in1=T2[:], op=ALU.mult), R=[K_b, T2_b], W=[K_b])
        yield
        k.op("dve", lambda e: e.tensor_tensor(out=T2[:], in0=CUM[:], in1=T1[:], op=ALU.subtract), R=[CUM_b, T1_b], W=[T2_b])
        k.op("act", lambda e: e.activation(out=T2[:], in_=T2[:], func=AF.Exp), R=[T2_b], W=[T2_b])
        k.op("dve", lambda e: e.scalar_tensor_tensor(out=AR[:, :, 0, :], in0=c3(KK[:]), scalar=-1.0, in1=c3(T2[:]), op0=ALU.mult, op1=ALU.mult),
             R=[KK_b, T2_b], W=[AR_b])
        k.op("dve", lambda e: e.tensor_tensor(out=T1[:], in0=KK[:], in1=A[:], op=ALU.mult), R=[KK_b, A_b], W=[T1_b])
        k.op("act", lambda e: e.activation(out=T2[:], in_=CUM[:], func=AF.Exp, scale=-1.0), R=[CUM_b], W=[T2_b])
        yield
        k.op("dve", lambda e: e.tensor_tensor(out=BK[:, :, 0, :], in0=c3(T1[:]), in1=c3(T2[:]), op=ALU.mult), R=[T1_b, T2_b], W=[BK_b])
        k.op("dve", lambda e: e.tensor_tensor(out=BK[:, :, 1, :], in0=c3(K[:]), in1=c3(T2[:]), op=ALU.mult), R=[K_b, T2_b], W=[BK_b])
        k.op("dve", lambda e: e.tensor_tensor(out=c3(T2[:]), in0=cend[:].unsqueeze(2).to_broadcast([128, CPB, L]), in1=c3(CUM[:]), op=ALU.subtract),
             R=[cend_b, CUM_b], W=[T2_b])
        k.op("act", lambda e: e.activation(out=T2[:], in_=T2[:], func=AF.Exp), R=[T2_b], W=[T2_b])
        yield
        k.op("dve", lambda e: e.tensor_tensor(out=BhKh[:, 0, :], in0=T1[:], in1=T2[:], op=ALU.mult), R=[T1_b, T2_b], W=[BhKh_b])
        k.op("dve", lambda e: e.tensor_tensor(out=BhKh[:, 1, :], in0=K[:], in1=T2[:], op=ALU.mult), R=[K_b, T2_b], W=[BhKh_b])
        k.op("act", lambda e: e.activation(out=T2[:], in_=CUM[:], func=AF.Exp), R=[CUM_b], W=[T2_b])
        load_mixed(t, wr[:, :, pc], 128, 0)
        yield

        def ev(p_, p_b):
            k.op("dve", lambda e: e.tensor_tensor(out=AR[:, :, 1, :], in0=c3(p_[:]), in1=c3(T2[:]), op=ALU.mult), R=[p_b, T2_b], W=[AR_b])
            k.op("dve", lambda e: e.scalar_tensor_tensor(out=sqb[:], in0=p_[:], scalar=colv(pr, 4), in1=K[:], op0=ALU.mult, op1=ALU.mult),
                 R=[p_b, cols_b, K_b], W=[sqb_b])
            p2, p2_b = ps_next()
            k.op("pe", lambda e: e.matmul(p2[:], bones1[:], sqb[:], start=True, stop=True), R=[bones1_b, sqb_b], W=[p2_b])
            k.op("dve", lambda e: e.tensor_tensor(out=T1[:], in0=p2[:], in1=V[:], op=ALU.mult), R=[p2_b, V_b], W=[T1_b])
        mixed_proj(t, 128, ev)
        yield
        (Vpad, Vpad_b), (Bpad, Bpad_b), (Kpad, Kpad_b) = t.pads
        for src, dst, dst_b, src_b in ((Vb[:, :], Vpad, Vpad_b, Vb_b), (BhKh[:, 0, :], Bpad, Bpad_b, BhKh_b), (BhKh[:, 1, :], Kpad, Kpad_b, BhKh_b)):
            p_, p_b = ps_next()
            pbf = p_[:].bitcast(BF16)
            for n in range(CPB):
                k.op("pe", lambda e: e.transpose(pbf[0:64, n * 128:(n + 1) * 128], src[:, n * L:(n + 1) * L], ident[:]), R=[src_b, ident_b], W=[p_b])
            pv = pbf[0:64, :].rearrange("p (n c) -> p n c", c=128)
            k.op("act", lambda e: e.activation(out=dst[:, :, 0, 0:64], in_=pv[:, :, 0:64], func=AF.Copy), R=[p_b], W=[dst_b])
            k.op("act", lambda e: e.activation(out=dst[:, :, 1, 64:128], in_=pv[:, :, 64:128], func=AF.Copy), R=[p_b], W=[dst_b])
            yield
        sc = t.sc
        for h in range(2):
            ph = slice(h * 64, (h + 1) * 64)
            for nm, lhs, li, rhs, ri, mi in (("N", BK, 0, AR, 0, 0), ("PBR", BK, 0, AR, 1, 1), ("AAK", BK, 1, AR, 0, 2), ("PKR", BK, 1, AR, 1, 3), ("NT", AR, 0, BK, 0, 4)):
                p_, p_b = ps_next()
                for n in range(CPB):
                    k.op("pe", lambda e: e.matmul(p_[0:64, n * L:(n + 1) * L], lhs[ph, n, li, :], rhs[ph, n, ri, :], start=True, stop=True),
                         R=[AR_b, BK_b], W=[p_b])
                dst, dst_b = sc[nm]
                k.op("dve", lambda e: e.tensor_tensor(out=dst[:, h, :, :], in0=p_[0:64, :].rearrange("p (n t) -> p n t", t=L),
                                                      in1=cmask[:, mi, :].unsqueeze(1).to_broadcast([64, CPB, L]), op=ALU.mult),
                     R=[p_b, cmask_b], W=[dst_b])
                yield
        N_, N_b_ = sc["N"]; NT_, NT_b_ = sc["NT"]; N2_, N2_b_ = sc["N2"]; NT2_, NT2_b_ = sc["NT2"]; X_, X_b_ = sc["X"]; X2_, X2_b_ = sc["X2"]
        k.op("dve", lambda e: e.tensor_tensor(out=X_[:], in0=N_[:], in1=ident[0:64, 0:64].unsqueeze(1).unsqueeze(1).to_broadcast([64, 2, CPB, L]), op=ALU.add),
             R=[N_b_, ident_b], W=[X_b_])
        cur = (N_, N_b_, NT_, NT_b_, X_, X_b_)
        nxt = (N2_, N2_b_, NT2_, NT2_b_, X2_, X2_b_)
        for j in range(1, 6):
            cN, cN_b, cNT, cNT_b, cX, cX_b = cur
            nN, nN_b, nNT, nNT_b, nX, nX_b = nxt
            for h in range(2):
                if j < 5:
                    p_, p_b = ps_next()
                    for n in range(CPB):
                        k.op("pe", lambda e: e.matmul(p_[0:64, n * L:(n + 1) * L], cNT[:, h, n, :], cN[:, h, n, :], start=True, stop=True), R=[cNT_b, cN_b], W=[p_b])
                    k.op("act", lambda e: e.activation(out=nN[:, h, :, :], in_=p_[0:64, :].rearrange("p (n t) -> p n t", t=L), func=AF.Copy), R=[p_b], W=[nN_b])
                p_, p_b = ps_next()
                for n in range(CPB):
                    k.op("pe", lambda e: e.matmul(p_[0:64, n * L:(n + 1) * L], cN[:, h, n, :], cNT[:, h, n, :], start=True, stop=True), R=[cNT_b, cN_b], W=[p_b])
                k.op("dve", lambda e: e.tensor_copy(out=nNT[:, h, :, :], in_=p_[0:64, :].rearrange("p (n t) -> p n t", t=L)), R=[p_b], W=[nNT_b])
                yield
            for h in range(2):
                p_, p_b = ps_next()
                for n in range(CPB):
                    k.op("pe", lambda e: e.matmul(p_[0:64, n * L:(n + 1) * L], nNT[:, h, n, :], cX[:, h, n, :], start=True, stop=True), R=[nNT_b, cX_b], W=[p_b])
                k.op("dve", lambda e: e.tensor_tensor(out=nX[:, h, :, :], in0=p_[0:64, :].rearrange("p (n t) -> p n t", t=L), in1=cX[:, h, :, :], op=ALU.add),
                     R=[p_b, cX_b], W=[nX_b])
                yield
            cur, nxt = nxt, cur
        Xf, Xf_b = cur[4], cur[5]
        AAK, AAK_b = sc["AAK"]; PBR, PBR_b = sc["PBR"]; PKR, PKR_b = sc["PKR"]
        py, py_b, pch, pch_b = t.py, t.py_b, t.pch, t.pch_b
        W0, W0_b, Upad_b, Upad_w, Upad_r = t.W0, t.W0_b, t.Upad_b, t.Upad_w, t.Upad_r
        mf, mb, mf_b, mb_b = Mf[:, pr, :], Mb[:, pr, :], Mf_b[pr], Mb_b[pr]
        for n in range(CPB):
            k.op("pe", lambda e: e.matmul(pch[0:64, 0:128], AR[:, n, 0, :], mb, start=True, stop=False), R=[AR_b, mb_b], W=[pch_b])
            for h in range(2):
                k.op("pe", lambda e: e.matmul(pch[0:64, 0:128], AAK[:, h, n, :], Vpad[:, n, h, :], start=False, stop=(h == 1)),
                     R=[AAK_b, Vpad_b], W=[pch_b])
            k.op("act", lambda e: e.activation(out=W0[:], in_=pch[0:64, 0:128].rearrange("p (h v) -> p h v", h=2), func=AF.Copy), R=[pch_b], W=[W0_b])
            yield
            for h in range(2):
                k.op("pe", lambda e: e.matmul(pch[0:64, 128 + h * L:128 + (h + 1) * L], Xf[:, h, n, :], W0[:, h, :], start=True, stop=True),
                     R=[Xf_b, W0_b], W=[pch_b])
            k.op("act", lambda e: e.activation(out=Upad_w, in_=pch[0:64, 128:256].rearrange("p (h v) -> p h v", h=2), func=AF.Copy), R=[pch_b], W=[Upad_b])
            yield
            ysl = py[:, n * L:(n + 1) * L]
            k.op("pe", lambda e: e.matmul(ysl, mb, AR[:, n, 1, :], start=True, stop=False), R=[mb_b, AR_b], W=[py_b])
            for h in range(2):
                k.op("pe", lambda e: e.matmul(ysl, Upad_r[:, h, :], PBR[:, h, n, :], start=False, stop=False), R=[Upad_b, PBR_b], W=[py_b])
                k.op("pe", lambda e: e.matmul(ysl, Vpad[:, n, h, :], PKR[:, h, n, :], start=False, stop=(h == 1)), R=[Vpad_b, PKR_b], W=[py_b])
            for h in range(2):
                k.op("pe", lambda e: e.matmul(pch[:, 256:384], Bpad[:, n, h, :], Upad_r[:, h, :], start=(h == 0), stop=False), R=[Bpad_b, Upad_b], W=[pch_b])
                k.op("pe", lambda e: e.matmul(pch[:, 256:384], Kpad[:, n, h, :], Vpad[:, n, h, :], start=False, stop=(h == 1)), R=[Kpad_b, Vpad_b], W=[pch_b])
            k.op("dve", lambda e: e.scalar_tensor_tensor(out=mb, in0=mf, scalar=decL[:, n:n + 1], in1=pch[:, 256:384], op0=ALU.mult, op1=ALU.add),
                 R=[mf_b, decL_b, pch_b], W=[mb_b])
            k.op("dve", lambda e: e.scalar_tensor_tensor(out=mf, in0=mf, scalar=decL[:, n:n + 1], in1=pch[:, 256:384], op0=ALU.mult, op1=ALU.add),
                 R=[mf_b, decL_b, pch_b], W=[mf_b])
            yield
        yf, yf_b, ysb, ysb_b, ysq, ysq_b, t3, t3_b, yo, yo_b = t.yf, t.yf_b, t.ysb, t.ysb_b, t.ysq, t.ysq_b, t.t3, t.t3_b, t.yo, t.yo_b
        k.op("act", lambda e: e.activation(out=yf[:], in_=py[:], func=AF.Copy), R=[py_b], W=[yf_b])
        if io.get("yraw") is not None:
            k.dma("sp", io["yraw"][pc, tb:tb + BT], yf[:], R=[yf_b], is_output=True)
        k.op("dve", lambda e: e.tensor_copy(out=ysb[:], in_=yf[:]), R=[yf_b], W=[ysb_b])
        k.op("act", lambda e: e.activation(out=ysq[:], in_=yf[:], func=AF.Square), R=[yf_b], W=[ysq_b])
        pm, pm_b = ps_next()
        k.op("pe", lambda e: e.matmul(pm[:], bones[:], ysb[:], start=True, stop=True), R=[bones_b, ysb_b], W=[pm_b])
        pq, pq_b = ps_next()
        k.op("pe", lambda e: e.matmul(pq[:], bones[:], ysq[:], start=True, stop=True), R=[bones_b, ysq_b], W=[pq_b])
        yield
        k.op("act", lambda e: e.activation(out=t3[:], in_=pm[:], func=AF.Square), R=[pm_b], W=[t3_b])
        k.op("dve", lambda e: e.tensor_tensor(out=t3[:], in0=pq[:], in1=t3[:], op=ALU.subtract), R=[pq_b, t3_b], W=[t3_b])
        k.op("dve", lambda e: e.tensor_scalar(out=t3[:], in0=t3[:], scalar1=GN_EPS, scalar2=None, op0=ALU.add), R=[t3_b], W=[t3_b])
        k.op("act", lambda e: e.activation(out=t3[:], in_=t3[:], func=AF.Sqrt), R=[t3_b], W=[t3_b])
        k.op("dve", lambda e: e.reciprocal(out=t3[:], in_=t3[:]), R=[t3_b], W=[t3_b])
        k.op("dve", lambda e: e.tensor_tensor(out=yf[:], in0=yf[:], in1=pm[:], op=ALU.subtract), R=[yf_b, pm_b], W=[yf_b])
        yield
        k.op("dve", lambda e: e.scalar_tensor_tensor(out=yf[:], in0=yf[:], scalar=colv(pr, 5), in1=t3[:], op0=ALU.mult, op1=ALU.mult),
             R=[yf_b, cols_b, t3_b], W=[yf_b])
        k.op("dve", lambda e: e.scalar_tensor_tensor(out=yf[:], in0=yf[:], scalar=colv(pr, 6), in1=T1[:], op0=ALU.add, op1=ALU.add),
             R=[yf_b, cols_b, T1_b], W=[yf_b])
        k.op("dve", lambda e: e.tensor_tensor(out=yo[:], in0=yf[:], in1=G[:], op=ALU.mult), R=[yf_b, G_b], W=[yo_b])
        k.dma("sp", yrow(pr * 128)[:, tb:tb + BT], yo[:], R=[yo_b], is_output=True)
        yield

    def lora_hidden(wname, ncols, mi, dst, dst_b, func, col0=0, part=None):
        load_mixed(tsA, wv_(wname)[:, :, col0:col0 + ncols], ncols, mi)
        mixed_proj(tsA, ncols, lambda p_, p_b: k.op("act", lambda e: e.activation(out=dst, in_=p_[:ncols, :], func=func), R=[p_b], W=[dst_b]))

    for blk in range(NBLK):
        for c in range(DC_):
            k.op("pool", lambda e: e.tensor_copy(out=xe[:, c, 7:8], in_=xcar[:, c, :]), R=[xcar_b], W=[xe_b[c]])
            if io.get("hn_src") is not None:
                r_, off = (blk * BT) // 1024, (blk * BT) % 1024
                k.dma("sp", xe[:, c, 8:8 + BT], io["hn_src"](c, r_)[:, off:off + BT], W=[xe_b[c]])
            else:
                k.dma("sp", xe[:, c, 8:8 + BT], io["hn"].rearrange("(c p) t -> p c t", p=128)[:, c, blk * BT:(blk + 1) * BT], W=[xe_b[c]])
        if blk + 1 < NBLK:
            k.op("pool", lambda e: e.tensor_copy(out=xcar[:], in_=xe[:, :, 7 + BT:8 + BT]), R=xe_b, W=[xcar_b])
        lora_hidden("w1", 96, 1, hw[:], hw_b, AF.Tanh)
        lora_hidden("a1", 96, 4, ha[:], ha_b, AF.Copy)
        for gc in range(2):
            lora_hidden("g1", 128, 5, hg[:, gc, :], hg_b, AF.Sigmoid, col0=gc * 128)
        if not first:
            lora_hidden("v1", 64, 3, hv[:], hv_b, AF.Copy)
        for pp in range(4):
            ga = unit(tsA, 2 * pp, blk)
            gb = unit(tsB, 2 * pp + 1, blk)
            alive = [ga, gb]
            while alive:
                for g_ in list(alive):
                    try:
                        next(g_)
                    except StopIteration:
                        alive.remove(g_)


from concourse.bass_utils import run_bass_kernel_spmd
import ml_dtypes
_bf = ml_dtypes.bfloat16
NCORES = 8
DEPTH = 4
LITE = False


def _pc(g):
    return np.ascontiguousarray(np.asarray(g, np.float32).reshape(16, 128).T)


def _consts():
    s = np.arange(128)
    mask = (s[:, None] <= s[None, :]).astype(np.float32)
    mask64 = ((s[:, None] % 64) <= np.arange(64)[None, :]).astype(np.float32)
    rmaskE = np.ones((128, 2048), np.float32); rmaskE[:, ::64] = 0.0
    s6 = np.arange(64)
    su = (s6[:, None] < s6[None, :]).astype(np.float32); iu = (s6[:, None] <= s6[None, :]).astype(np.float32); sl = (s6[:, None] > s6[None, :]).astype(np.float32)
    cmask = np.ascontiguousarray(np.stack([su, iu, su, iu, sl], axis=1))
    blk = np.kron(np.eye(2, dtype=np.float32), np.ones((64, 64), np.float32))
    rmaskO = np.ones((128, 1024), np.float32); rmaskO[:, ::64] = 0.0
    return dict(mask=mask, mask64=mask64, rmaskE=rmaskE, ident=np.eye(128, dtype=np.float32), bones=blk / 64.0, bones1=blk, cmask=cmask, rmaskO=rmaskO)


def _even_inputs(d, j, hf):
    w = d["e_w_in"][j]
    g0 = hf * 4
    cols = [w[:, g0 * 128:(g0 + 4) * 128], w[:, 1024 + g0 * 128:1024 + (g0 + 4) * 128]]
    for hd in range(g0, g0 + 4):
        for part in range(4):
            cols.append(w[:, 2048 + part * 1024 + hd * 128:2048 + part * 1024 + (hd + 1) * 128])
    return dict(w_in=np.ascontiguousarray(np.concatenate(cols, axis=1)),
                wsT=np.ascontiguousarray(np.transpose(d["a_ws"][j, g0:g0 + 4], (0, 2, 1))),
                bias=np.ascontiguousarray(d["a_bs"][j, g0:g0 + 4].reshape(1, 512)),
                gainv=np.ascontiguousarray(d["a_vnorm"][j, g0 * 128:(g0 + 4) * 128].reshape(1, 512)),
                onorm=np.ascontiguousarray(d["b_onorm"][j, g0 * 128:(g0 + 4) * 128].reshape(4, 128).T),
                lbl=np.ascontiguousarray(np.transpose(d["b_lb_logits"][:, g0 * 128:(g0 + 4) * 128].reshape(4, 4, 128), (2, 1, 0))))


def _odd_inputs(d, j, hf):
    cs = slice(hf * 1024, (hf + 1) * 1024)
    m = {}
    for nm, key in (("wr", "c_wr"), ("wk", "c_wk"), ("wv", "c_wv"), ("w2", "c_w2"), ("a2", "c_a2"), ("g2", "c_g2")):
        m[nm] = np.ascontiguousarray(d[key][j][:, cs])
    for nm, key in (("w1", "c_w1"), ("a1", "c_a1"), ("g1", "c_g1")):
        m[nm] = np.ascontiguousarray(d[key][j])
    v0 = d["c_v0"][j - 1] if j > 0 else np.zeros(2048, np.float32)
    vecs = [d["c_w0"][j], d["c_a0"][j], d["c_kk"][j], d["c_ka"][j], d["c_rk"][j].reshape(-1), d["c_gn_g"][j], d["c_gn_b"][j], v0]
    cols = np.stack([np.asarray(v)[cs].reshape(8, 128) for v in vecs], axis=-1)
    m["cols"] = np.ascontiguousarray(np.transpose(cols, (1, 0, 2)))
    m["mixpc"] = np.ascontiguousarray(np.transpose(d["c_mix"][j].reshape(6, 16, 128), (2, 0, 1)))
    if j > 0:
        m["v1"] = np.ascontiguousarray(d["c_v1"][j - 1]); m["v2"] = np.ascontiguousarray(d["c_v2"][j - 1][:, cs])
    return m


_EVEN_SHAPES = dict(w_in=[2048, 3072], wsT=[4, 128, 128], bias=[1, 512], gainv=[1, 512], onorm=[128, 4], lbl=[128, 4, 4])
_ODD_SHAPES = dict(wr=[2048, 1024], wk=[2048, 1024], wv=[2048, 1024], w1=[2048, 96], a1=[2048, 96], g1=[2048, 256], w2=[96, 1024], a2=[96, 1024],
                   g2=[256, 1024], mixpc=[128, 6, 16], cols=[128, 8, 8], v1=[2048, 64], v2=[64, 1024])
_CONST_SHAPES = dict(mask=[128, 128], mask64=[128, 64], rmaskE=[128, 2048], ident=[128, 128], bones=[128, 128], bones1=[128, 128],
                     cmask=[64, 5, 64], rmaskO=[128, 1024])


def _wo_perm(d, i):
    j = i // 2
    if i % 2 == 0:
        return d["e_w_out"][j]
    w = d["c_wo"][j]
    return np.concatenate([w[0:512], w[1024:1536], w[512:1024], w[1536:2048]], axis=0)


def build_fused():
    kb = KB()
    nc = kb.nc
    di = lambda n, s, dt=F32: kb.dram(n, s, dt, "ExternalInput")
    xT = di("xT", [2048, 1024])
    if LITE:
        wg = wu = wd = None
    else:
        wg = di("ffn_wg", [DEPTH, 2, 2048, 5632]); wu = di("ffn_wu", [DEPTH, 2, 2048, 5632]); wd = di("ffn_wd", [DEPTH, 2, 5632, 2048])
    norms = di("norms_pc", [DEPTH, 4, 128, 16]); fnorm = di("fnorm_pc", [128, 16])
    plg = di("ple_wg", [DEPTH, 2048, 2048]); plp = di("ple_wp", [DEPTH, 256, 2048]); pT = di("pT", [DEPTH, 256, 1024])
    wo = di("wo", [DEPTH, 2048, 2048]); sel_d = di("sel", [128, 2])
    cst = {n: di(n, s) for n, s in _CONST_SHAPES.items()}
    ev_in = [{n: di(f"e{j}_{n}", s) for n, s in _EVEN_SHAPES.items()} for j in range(2)]
    od_in = [{n: di(f"o{j}_{n}", s) for n, s in _ODD_SHAPES.items() if j > 0 or n not in ("v1", "v2")} for j in range(2)]
    out = kb.dram("out", [2048, 1024], F32, "ExternalOutput")
    internal = lambda n, s, dt: nc.dram_tensor(n, list(s), dt, kind="Internal").ap()
    hspill = internal("hspill", [2048, 1024], F32)
    hn_loc = [internal(f"hn_loc{a}", [1024, 1024], BF16) for a in range(2)]
    hn_pair = [internal(f"hn_pair{a}", [2048, 1024], BF16) for a in range(2)]
    y_loc = [internal(f"y_loc{a}", [512, 2048], BF16) for a in range(2)]
    y_pair = [internal(f"y_pair{a}", [1024, 2048], BF16) for a in range(2)]
    vf = internal("vf", [1024, 2048], F32)

    def row_phase(i):
        kb.phase += 1
        es = contextlib.ExitStack()
        rs = RowState(kb, es)
        if i < 0:
            load_hT(rs, xT)
        else:
            pst = PleState(rs, es)
            selt = kb.sb("sel", [128, 2], F32, es); sel_b = Buf()
            kb.dma("sp", selt[:], sel_d, W=[sel_b])
            load_hT(rs, hspill)
            load_y_select(rs, y_pair, selt, sel_b)
            out_proj_residual(rs, wo[i])
            rmsnorm(rs, norms[i, 2])
            if not LITE:
                ffn(rs, wg[i, 1], wu[i, 1], wd[i, 1])
            rmsnorm(rs, norms[i, 3])
            if not LITE:
                ple(rs, pst, plg[i], plp[i], pT[i])
        if i == DEPTH - 1:
            rmsnorm(rs, fnorm, dst=rs.hT, dst_b=rs.hT_b)
            store_hT(rs, out, is_output=True)
        else:
            rmsnorm(rs, norms[i + 1, 0])
            if not LITE:
                ffn(rs, wg[i + 1, 0], wu[i + 1, 0], wd[i + 1, 0])
            rmsnorm(rs, norms[i + 1, 1])
            store_hT(rs, hspill, is_output=False)
            store_xn_split(rs, hn_loc)
        kb.barrier()
        es.close()

    def mix_phase(i):
        kb.phase += 1
        j = i // 2
        es = contextlib.ExitStack()
        hn_src = lambda c, r_: hn_pair[c // 8][r_ * 1024 + (c % 8) * 128:r_ * 1024 + (c % 8 + 1) * 128, :]
        y_rows = lambda r0: y_loc[r0 // 512][r0 % 512:r0 % 512 + 128, :]
        if i % 2 == 0:
            io = dict(ev_in[j]); io.update(layer=i, hn_src=hn_src, y_rows=y_rows, mask=cst["mask"], mask64=cst["mask64"], ident=cst["ident"], rmask=cst["rmaskE"])
            mix_even(kb, io, es)
        else:
            io = dict(od_in[j]); io.update(first=(j == 0), hn_src=hn_src, y_rows=y_rows, ident=cst["ident"], bones=cst["bones"], bones1=cst["bones1"],
                                           cmask=cst["cmask"], rmask=cst["rmaskO"], vf_out=vf, vf_in=vf)
            mix_odd(kb, io, es)
        kb.barrier()
        es.close()

    row_phase(-1)
    for i in range(DEPTH):
        for a in range(2):
            kb.allgather_pairs(hn_loc[a], hn_pair[a])
        kb.barrier()
        mix_phase(i)
        for a in range(2):
            kb.allgather_pairs(y_loc[a], y_pair[a])
        kb.barrier()
        row_phase(i)
    return kb.finish()


def kernel(**inputs):
    d = {k_: np.asarray(v_) for k_, v_ in inputs.items()}
    x = d["x"].astype(np.float32)
    tsl = lambda hf: slice(hf * 1024, (hf + 1) * 1024)
    shared = dict(ffn_wg=d["ffn_wg"], ffn_wu=d["ffn_wu"], ffn_wd=d["ffn_wd"], ple_wg=d["ple_wg"], ple_wp=d["ple_wp"],
                  norms_pc=np.ascontiguousarray(np.transpose(d["norms"].reshape(DEPTH, 4, 16, 128), (0, 1, 3, 2))), fnorm_pc=_pc(d["final_norm"]),
                  wo=np.ascontiguousarray(np.stack([_wo_perm(d, i) for i in range(DEPTH)])))
    shared.update(_consts())
    per_half = []
    for hf in range(2):
        m = {}
        for j in range(2):
            for n, v in _even_inputs(d, j, hf).items():
                m[f"e{j}_{n}"] = v
            for n, v in _odd_inputs(d, j, hf).items():
                m[f"o{j}_{n}"] = v
        sel = np.zeros((128, 2), np.float32); sel[:, hf] = 1.0
        m["sel"] = sel
        per_half.append(m)
    ims = []
    for c in range(NCORES):
        b, hf = c // 2, c % 2
        m = dict(shared); m.update(per_half[hf])
        m["xT"] = np.ascontiguousarray(x[b, tsl(hf)].T)
        m["pT"] = np.ascontiguousarray(np.transpose(d["p"][:, b, tsl(hf)], (0, 2, 1)))
        ims.append(m)
    nc = build_fused()
    res = run_bass_kernel_spmd(nc, ims, core_ids=list(range(NCORES))).results
    out = np.empty((4, 2048, 2048), np.float32)
    for c in range(NCORES):
        b, hf = c // 2, c % 2
        out[b, tsl(hf)] = res[c]["out"].T
    return out
```

```python
import contextlib
import numpy as np
import concourse.bass as bass
import concourse.mybir as mybir

F32 = mybir.dt.float32
BF16 = mybir.dt.bfloat16
AF = mybir.ActivationFunctionType
ALU = mybir.AluOpType
AX = mybir.AxisListType


class Buf:
    __slots__ = ("w", "r", "name", "excl")

    def __init__(self, name="", excl=False):
        self.w = {}
        self.r = {}
        self.name = name
        self.excl = excl


class Eng:
    def __init__(self, name, e, sem_key):
        self.name = name
        self.e = e
        self.key = sem_key
        self.cnt = 0
        self.waited = {}
        self.old = {}


class KB:
    N_DMA_SEMS = 8
    SEM_LIMIT = 30000

    def __init__(self):
        self.nc = bass.Bass("TRN2", target_bir_lowering=False)
        nc = self.nc
        self.es = contextlib.ExitStack()
        self.sems = []
        self.E = {}
        for name, e in (("pe", nc.tensor), ("act", nc.scalar), ("dve", nc.vector), ("pool", nc.gpsimd), ("sp", nc.sync)):
            key = self._new_sem("e_" + name)
            self.E[name] = Eng(name, e, key)
        self.dma_pool = {}
        for q in ("sp", "pool", "act"):
            keys = [self._new_sem(f"d_{q}{i}") for i in range(self.N_DMA_SEMS)]
            self.dma_pool[q] = {"keys": keys, "vals": [0] * len(keys), "next": 0}
        self.n_inst = 0
        self.out_events = []
        self.phase = 0
        self.extra = {}

    def _new_sem(self, name):
        s = self.es.enter_context(self.nc.semaphore(name))
        self.sems.append(s)
        return len(self.sems) - 1

    def sb(self, name, shape, dtype, es=None):
        t = (es or self.es).enter_context(self.nc.sbuf_tensor(f"s{self.phase}_" + name, list(shape), dtype))
        return t

    def ps(self, name, shape, dtype=F32, es=None):
        t = (es or self.es).enter_context(self.nc.psum_tensor(f"p{self.phase}_" + name, list(shape), dtype))
        return t

    def dram(self, name, shape, dtype, kind):
        return self.nc.dram_tensor(name, list(shape), dtype, kind=kind).ap()

    def _wait(self, eng, deps, skip_self=False):
        for k, v in deps.items():
            if skip_self and k == eng.key:
                continue
            if eng.waited.get(k, 0) >= v:
                continue
            eng.e.wait_ge(self.sems[k], v)
            eng.waited[k] = v
            self.n_inst += 1

    @staticmethod
    def _merge(dst, src):
        for k, v in src.items():
            if dst.get(k, 0) < v:
                dst[k] = v

    def op(self, engname, fn, R=(), W=()):
        eng = self.E[engname]
        deps = {}
        for b in R:
            self._merge(deps, b.w)
            if b.excl:
                for kk_, vv_ in b.r.items():
                    if kk_ != eng.key and deps.get(kk_, 0) < vv_:
                        deps[kk_] = vv_
        wdeps = {}
        for b in W:
            self._merge(wdeps, b.w)
            self._merge(wdeps, b.r)
        self._merge(deps, wdeps)
        if engname == "pe":
            for k_ in list(deps):
                if k_ == eng.key or k_ in eng.old:
                    del deps[k_]
        self._wait(eng, deps)
        if eng.cnt >= self.SEM_LIMIT:
            eng.old[eng.key] = eng.cnt
            eng.key = self._new_sem(f"e_{engname}_{len(eng.old)}")
            eng.cnt = 0
        ins = fn(eng.e)
        eng.cnt += 1
        ins.then_inc(self.sems[eng.key], 1)
        self.n_inst += 1
        ev = {eng.key: eng.cnt}
        for b in R:
            self._merge(b.r, ev)
        for b in W:
            b.w = dict(ev)
            b.r = {}
        return ins

    def dma(self, q, out, in_, R=(), W=(), is_output=False, **kw):
        eng = self.E[q]
        pool = self.dma_pool[q]
        i = pool["next"]
        pool["next"] = (i + 1) % len(pool["keys"])
        key = pool["keys"][i]
        deps = {}
        for b in R:
            self._merge(deps, b.w)
        for b in W:
            self._merge(deps, b.w)
            self._merge(deps, b.r)
        if pool["vals"][i] > 0:
            self._merge(deps, {key: pool["vals"][i]})
        self._wait(eng, deps)
        ins = eng.e.dma_start(out=out, in_=in_, **kw)
        pool["vals"][i] += 16
        ins.then_inc(self.sems[key], 16)
        self.n_inst += 1
        ev = {key: pool["vals"][i]}
        for b in R:
            self._merge(b.r, ev)
        for b in W:
            b.w = dict(ev)
            b.r = {}
        if is_output:
            self.out_events.append(ev)
        return ins

    def all_events(self):
        ev = {}
        for e in self.E.values():
            if e.cnt:
                ev[e.key] = e.cnt
            ev.update(e.old)
        for p in self.dma_pool.values():
            for k, v in zip(p["keys"], p["vals"]):
                if v:
                    ev[k] = v
        ev.update(self.extra)
        return ev

    def allgather_pairs(self, src, dst):
        eng = self.E["pool"]
        key = self._new_sem(f"cc{len(self.extra)}")
        ins = self.nc.gpsimd.collective_compute("AllGather", mybir.AluOpType.bypass, replica_groups=[[0, 1], [2, 3], [4, 5], [6, 7]],
                                                ins=[src], outs=[dst])
        ins.then_inc(self.sems[key], 1)
        self.n_inst += 1
        self.extra[key] = 1

    def barrier(self):
        ev = self.all_events()
        for eng in self.E.values():
            self._wait(eng, ev)

    def finish(self):
        ev = self.all_events()
        self._wait(self.E["sp"], ev)
        return self.nc


D = 2048
DC = 16
DFF = 5632
FC = 44
T = 1024
TH = 512
NTH = T // TH
EPS = 1e-6


class RowState:
    def __init__(self, kb, es=None):
        k = kb
        self.kb = kb
        self.hT = k.sb("hT", [128, DC, T], F32, es)
        self.hT_b = [[Buf(f"hT{c}_{t}") for t in range(NTH)] for c in range(DC)]
        self.xn = k.sb("xn", [128, DC, T], BF16, es)
        self.xn_b = [[Buf() for t in range(NTH)] for c in range(DC)]
        self.act = k.sb("act", [128, FC // 2, T], BF16, es)
        self.act_b = [[Buf() for t in range(NTH)] for c in range(FC // 2)]
        self.wA = [k.sb(f"wA{i}", [128, DC, 256], BF16, es) for i in range(4)]
        self.wA_b = [Buf() for _ in range(4)]
        self.wA_i = 0
        self.wD = [k.sb(f"wD{i}", [128, FC // 2, 128], BF16, es) for i in range(2)]
        self.wD_b = [Buf() for _ in range(2)]
        self.wD_i = 0
        self.psum = [k.ps(f"ps{i}", [128, TH], F32, es) for i in range(8)]
        self.psum_b = [Buf(excl=True) for _ in range(8)]
        self.ps_i = 0
        self.sq = [k.sb(f"sq{i}", [128, TH], BF16, es) for i in range(2)]
        self.sq_b = [Buf() for _ in range(2)]
        self.rstd = k.sb("rstd", [128, TH], F32, es)
        self.rstd_b = Buf()
        self.sg = [k.sb(f"sg{i}", [128, TH], F32, es) for i in range(2)]
        self.sg_b = [Buf() for _ in range(2)]
        self.sg_i = 0
        self.ones = k.sb("ones", [128, 128], BF16, es)
        self.ones_b = Buf()
        self.gcol = k.sb("gcol", [128, DC], F32, es)
        self.gcol_b = Buf()
        k.op("dve", lambda e: e.memset(self.ones[:], 1.0), W=[self.ones_b])

    def next_ps(self):
        i = self.ps_i
        self.ps_i = (i + 1) % 8
        return self.psum[i], self.psum_b[i]

    def next_wA(self):
        i = self.wA_i
        self.wA_i = (i + 1) % 4
        return self.wA[i], self.wA_b[i]

    def next_wD(self):
        i = self.wD_i
        self.wD_i = (i + 1) % 2
        return self.wD[i], self.wD_b[i]


def load_hT(rs, src):
    k = rs.kb
    v = src.rearrange("(c p) t -> p c t", p=128)
    for c in range(DC):
        k.dma("sp", rs.hT[:, c, :], v[:, c, :], W=rs.hT_b[c])


def store_hT(rs, dst, is_output=True):
    k = rs.kb
    v = dst.rearrange("(c p) t -> p c t", p=128)
    for c in range(DC):
        k.dma("sp", v[:, c, :], rs.hT[:, c, :], R=rs.hT_b[c], is_output=is_output)


def rmsnorm(rs, g_dram_pc, dst=None, dst_b=None, dst_dtype_scale=1.0):
    k = rs.kb
    dst = rs.xn if dst is None else dst
    dst_b = rs.xn_b if dst_b is None else dst_b
    k.dma("sp", rs.gcol[:], g_dram_pc, W=[rs.gcol_b])
    for th in range(NTH):
        ts = slice(th * TH, (th + 1) * TH)
        ps, ps_b = rs.next_ps()
        for c in range(DC):
            sq, sq_b = rs.sq[c % 2], rs.sq_b[c % 2]
            k.op("act", lambda e: e.activation(out=sq[:], in_=rs.hT[:, c, ts], func=AF.Square), R=[rs.hT_b[c][th]], W=[sq_b])
            k.op("pe", lambda e: e.matmul(ps[:], rs.ones[:], sq[:], start=(c == 0), stop=(c == DC - 1)),
                 R=[rs.ones_b, sq_b], W=[ps_b])
        k.op("dve", lambda e: e.tensor_scalar(out=rs.rstd[:], in0=ps[:], scalar1=float(1.0 / D), scalar2=float(EPS),
                                              op0=ALU.mult, op1=ALU.add), R=[ps_b], W=[rs.rstd_b])
        k.op("act", lambda e: e.activation(out=rs.rstd[:], in_=rs.rstd[:], func=AF.Ln), R=[rs.rstd_b], W=[rs.rstd_b])
        k.op("act", lambda e: e.activation(out=rs.rstd[:], in_=rs.rstd[:], func=AF.Exp, scale=-0.5), R=[rs.rstd_b], W=[rs.rstd_b])
        for c in range(DC):
            eng = "dve"
            k.op(eng, lambda e: e.scalar_tensor_tensor(out=dst[:, c, ts], in0=rs.hT[:, c, ts], scalar=rs.gcol[:, c:c + 1],
                                                       in1=rs.rstd[:], op0=ALU.mult, op1=ALU.mult),
                 R=[rs.hT_b[c][th], rs.gcol_b, rs.rstd_b], W=[dst_b[c][th]])


def load_w(rs, view, kc, ncols):
    k = rs.kb
    wt, wb = rs.next_wA()
    k.dma("pool", wt[:, :kc, :ncols], view, W=[wb])
    return wt, wb


def ffn(rs, wg, wu, wd):
    k = rs.kb
    wgv = wg.rearrange("(c p) f -> p c f", p=128)
    wuv = wu.rearrange("(c p) f -> p c f", p=128)
    wdv = wd.rearrange("(c p) d -> p c d", p=128)
    HF = FC // 2
    for fh in range(2):
        for fp in range(HF // 2):
            f0 = (fh * HF + fp * 2) * 128
            wgt, wgb = load_w(rs, wgv[:, :, f0:f0 + 256], DC, 256)
            wut, wub = load_w(rs, wuv[:, :, f0:f0 + 256], DC, 256)
            for fc in range(2):
                fl = fp * 2 + fc
                for th in range(NTH):
                    ts = slice(th * TH, (th + 1) * TH)
                    pg, pg_b = rs.next_ps()
                    pu, pu_b = rs.next_ps()
                    for c in range(DC):
                        k.op("pe", lambda e: e.matmul(pg[:], wgt[:, c, fc * 128:(fc + 1) * 128], rs.xn[:, c, ts],
                                                      start=(c == 0), stop=(c == DC - 1)),
                             R=[wgb, rs.xn_b[c][th]], W=[pg_b])
                    for c in range(DC):
                        k.op("pe", lambda e: e.matmul(pu[:], wut[:, c, fc * 128:(fc + 1) * 128], rs.xn[:, c, ts],
                                                      start=(c == 0), stop=(c == DC - 1)),
                             R=[wub, rs.xn_b[c][th]], W=[pu_b])
                    sg, sg_b = rs.sg[rs.sg_i], rs.sg_b[rs.sg_i]
                    rs.sg_i ^= 1
                    k.op("act", lambda e: e.activation(out=sg[:], in_=pg[:], func=AF.Silu), R=[pg_b], W=[sg_b])
                    k.op("dve", lambda e: e.tensor_tensor(out=rs.act[:, fl, ts], in0=sg[:], in1=pu[:], op=ALU.mult),
                         R=[sg_b, pu_b], W=[rs.act_b[fl][th]])
        for dc in range(DC):
            wdt, wdb = rs.next_wD()
            k.dma("pool", wdt[:], wdv[:, fh * HF:(fh + 1) * HF, dc * 128:(dc + 1) * 128], W=[wdb])
            for th in range(NTH):
                ts = slice(th * TH, (th + 1) * TH)
                po, po_b = rs.next_ps()
                for fl in range(HF):
                    k.op("pe", lambda e: e.matmul(po[:], wdt[:, fl, :], rs.act[:, fl, ts], start=(fl == 0), stop=(fl == HF - 1)),
                         R=[wdb, rs.act_b[fl][th]], W=[po_b])
                k.op("dve", lambda e: e.scalar_tensor_tensor(out=rs.hT[:, dc, ts], in0=po[:], scalar=0.5, in1=rs.hT[:, dc, ts],
                                                             op0=ALU.mult, op1=ALU.add),
                     R=[po_b, rs.hT_b[dc][th]], W=[rs.hT_b[dc][th]])


def load_xn_from(rs, src):
    k = rs.kb
    v = src.rearrange("(c p) t -> p c t", p=128)
    for c in range(DC):
        k.dma("sp", rs.xn[:, c, :], v[:, c, :], W=rs.xn_b[c])


def store_xn_to(rs, dst, is_output=True):
    k = rs.kb
    v = dst.rearrange("(c p) t -> p c t", p=128)
    for c in range(DC):
        k.dma("sp", v[:, c, :], rs.xn[:, c, :], R=rs.xn_b[c], is_output=is_output)


def square_proj(rs, w, evac):
    k = rs.kb
    wv = w.rearrange("(c p) n -> p c n", p=128)
    for op_ in range(DC // 2):
        wt, wb = load_w(rs, wv[:, :, op_ * 256:(op_ + 1) * 256], DC, 256)
        for o2 in range(2):
            oc = op_ * 2 + o2
            for th in range(NTH):
                ts = slice(th * TH, (th + 1) * TH)
                ps, ps_b = rs.next_ps()
                for c in range(DC):
                    k.op("pe", lambda e: e.matmul(ps[:], wt[:, c, o2 * 128:(o2 + 1) * 128], rs.xn[:, c, ts],
                                                  start=(c == 0), stop=(c == DC - 1)),
                         R=[wb, rs.xn_b[c][th]], W=[ps_b])
                evac(oc, th, ts, ps, ps_b)


def out_proj_residual(rs, w):
    k = rs.kb

    def evac(oc, th, ts, ps, ps_b):
        k.op("dve", lambda e: e.tensor_tensor(out=rs.hT[:, oc, ts], in0=ps[:], in1=rs.hT[:, oc, ts], op=ALU.add),
             R=[ps_b, rs.hT_b[oc][th]], W=[rs.hT_b[oc][th]])
    square_proj(rs, w, evac)


class PleState:
    def __init__(self, rs, es=None):
        k = rs.kb
        self.pT = k.sb("pT_sb", [128, 2, T], BF16, es)
        self.pT_b = Buf()
        self.wp = k.sb("wp_sb", [128, 2, D], BF16, es)
        self.wp_b = Buf()
        self.tmp = k.sb("pletmp", [128, TH], F32, es)
        self.tmp_b = Buf()


def ple(rs, pst, wgate, wp, pT_dram):
    k = rs.kb
    k.dma("pool", pst.pT[:], pT_dram.rearrange("(c p) t -> p c t", p=128), W=[pst.pT_b])
    k.dma("pool", pst.wp[:], wp.rearrange("(c p) n -> p c n", p=128), W=[pst.wp_b])

    def evac(oc, th, ts, ps, ps_b):
        sg, sg_b = rs.sg[rs.sg_i], rs.sg_b[rs.sg_i]
        rs.sg_i ^= 1
        k.op("act", lambda e: e.activation(out=sg[:], in_=ps[:], func=AF.Sigmoid), R=[ps_b], W=[sg_b])
        pp, pp_b = rs.next_ps()
        for c in range(2):
            k.op("pe", lambda e: e.matmul(pp[:], pst.wp[:, c, oc * 128:(oc + 1) * 128], pst.pT[:, c, ts], start=(c == 0), stop=(c == 1)),
                 R=[pst.wp_b, pst.pT_b], W=[pp_b])
        k.op("dve", lambda e: e.tensor_tensor(out=pst.tmp[:], in0=pp[:], in1=sg[:], op=ALU.mult), R=[pp_b, sg_b], W=[pst.tmp_b])
        k.op("dve", lambda e: e.tensor_tensor(out=rs.hT[:, oc, ts], in0=pst.tmp[:], in1=rs.hT[:, oc, ts], op=ALU.add),
             R=[pst.tmp_b, rs.hT_b[oc][th]], W=[rs.hT_b[oc][th]])
    square_proj(rs, wgate, evac)


def load_y_select(rs, y_pairs, sel, sel_b):
    k = rs.kb
    for c in range(DC):
        v = y_pairs[c // 8].rearrange("(c p) t -> p c t", p=128)
        k.dma("sp", rs.xn[:, c, :], v[:, c % 8, 0:T], W=rs.xn_b[c])
        k.dma("sp", rs.act[:, c, :], v[:, c % 8, T:2 * T], W=rs.act_b[c])
    for c in range(DC):
        k.op("act", lambda e: e.activation(out=rs.xn[:, c, :], in_=rs.xn[:, c, :], func=AF.Copy, scale=sel[:, 0:1]),
             R=rs.xn_b[c] + [sel_b], W=rs.xn_b[c])
        k.op("dve", lambda e: e.scalar_tensor_tensor(out=rs.xn[:, c, :], in0=rs.act[:, c, :], scalar=sel[:, 1:2], in1=rs.xn[:, c, :],
                                                     op0=ALU.mult, op1=ALU.add),
             R=rs.act_b[c] + rs.xn_b[c] + [sel_b], W=rs.xn_b[c])


def store_xn_split(rs, dsts):
    k = rs.kb
    for c in range(DC):
        v = dsts[c // 8].rearrange("(c p) t -> p c t", p=128)
        k.dma("sp", v[:, c % 8, :], rs.xn[:, c, :], R=rs.xn_b[c])


S = 2048
NB = S // 512
NT = S // 128
NCH = S // 64
GC1 = 1.5957691216057308
GC2 = 0.044715


class PS:
    def __init__(self, kb, es=None):
        self.t = [kb.ps(f"ps{i}", [128, 512], F32, es) for i in range(8)]
        self.b = [Buf(excl=True) for _ in range(8)]
        self.i = 0
        self.ngen = 4

    def next(self):
        i = self.i
        self.i = (i + 1) % self.ngen
        return self.t[i], self.b[i]

    def fixed(self, i):
        return self.t[i], self.b[i]


def gelu_tanh(kb, out, x_ps, x_b, tmp, tmp_b, out_b, shape_slice=slice(None)):
    k = kb
    k.op("act", lambda e: e.activation(out=tmp, in_=x_ps, func=AF.Square), R=[x_b], W=[tmp_b])
    k.op("dve", lambda e: e.tensor_scalar(out=tmp, in0=tmp, scalar1=GC2, scalar2=1.0, op0=ALU.mult, op1=ALU.add), R=[tmp_b], W=[tmp_b])
    k.op("dve", lambda e: e.tensor_tensor(out=tmp, in0=tmp, in1=x_ps, op=ALU.mult), R=[tmp_b, x_b], W=[tmp_b])
    k.op("act", lambda e: e.activation(out=tmp, in_=tmp, func=AF.Sigmoid, scale=GC1), R=[tmp_b], W=[tmp_b])
    k.op("dve", lambda e: e.tensor_tensor(out=out, in0=tmp, in1=x_ps, op=ALU.mult), R=[tmp_b, x_b], W=[out_b])


def mix_even(kb, io, es=None):
    k = kb
    layer = io["layer"]
    ps = PS(kb, es)
    xn = k.sb("xnf", [128, DC, S], BF16, es)
    xn_b = [Buf() for _ in range(DC)]
    if io.get("hn_src") is not None:
        for c in range(DC):
            for r_ in range(2):
                k.dma("sp", xn[:, c, r_ * 1024:(r_ + 1) * 1024], io["hn_src"](c, r_), W=[xn_b[c]])
    else:
        hnv = io["hn"].rearrange("(c p) t -> p c t", p=128)
        for c in range(DC):
            k.dma("sp", xn[:, c, :], hnv[:, c, :], W=[xn_b[c]])
    w_in = io["w_in"].rearrange("(c p) n -> p c n", p=128)
    ones = k.sb("ones", [128, 128], BF16, es); ones_b = Buf()
    k.op("dve", lambda e: e.memset(ones[:], 1.0), W=[ones_b])
    ident = k.sb("ident", [128, 128], BF16, es); ident_b = Buf()
    k.dma("pool", ident[:], io["ident"], W=[ident_b])
    mask = k.sb("mask", [128, 128], F32, es); mask_b = Buf()
    k.dma("sp", mask[:], io["mask"], W=[mask_b])
    mask64 = k.sb("mask64", [128, 64], F32, es); mask64_b = Buf()
    k.dma("sp", mask64[:], io["mask64"], W=[mask64_b])
    rmask = k.sb("rmask", [128, S], F32, es); rmask_b = Buf()
    k.dma("sp", rmask[:], io["rmask"], W=[rmask_b])
    yrow = io.get("y_rows") or (lambda r0: io["yT"][r0:r0 + 128, :])

    wbig = [k.sb(f"wbig{i}", [128, DC, 512], BF16, es) for i in range(2)]
    wbig_b = [Buf(), Buf()]
    k.dma("pool", wbig[0][:], w_in[:, :, 512:1024], W=[wbig_b[0]])
    k.dma("pool", wbig[1][:], w_in[:, :, 0:512], W=[wbig_b[1]])
    esA = contextlib.ExitStack()
    gainv = k.sb("gainv", [128, 512], F32, esA); gainv_b = Buf()
    k.dma("sp", gainv[:], io["gainv"].partition_broadcast(128), W=[gainv_b])
    biasb = k.sb("biasb", [128, 4, 128], F32, esA); biasb_b = Buf()
    k.dma("sp", biasb[:].rearrange("p g t -> p (g t)"), io["bias"].partition_broadcast(128), W=[biasb_b])
    wsT = k.sb("wsT", [128, 4, 128], F32, esA); wsT_b = Buf()
    k.dma("sp", wsT[:], io["wsT"].rearrange("g s t -> s g t"), W=[wsT_b])
    wsm = k.sb("wsm", [128, 4, 128], BF16, esA); wsm_b = Buf()
    k.op("dve", lambda e: e.tensor_tensor(out=wsm[:], in0=wsT[:], in1=mask[:].unsqueeze(1).to_broadcast([128, 4, 128]), op=ALU.mult),
         R=[wsT_b, mask_b], W=[wsm_b])
    vg = k.sb("vg", [128, NT, 512], BF16, esA)
    vg_b = [Buf() for _ in range(NT)]
    tmpA = [k.sb(f"tmpA{i}", [128, 512], F32, esA) for i in range(2)]; tmpA_b = [Buf(), Buf()]
    gvt = [k.sb(f"gvt{i}", [128, 512], F32, esA) for i in range(2)]; gvt_b = [Buf(), Buf()]
    ssq = k.sb("ssq", [128, 4], F32, esA); ssq_b = Buf()
    for tt in range(NT):
        p_, p_b = ps.next()
        for c in range(DC):
            k.op("pe", lambda e: e.matmul(p_[:], xn[:, c, tt * 128:(tt + 1) * 128], wbig[0][:, c, :], start=(c == 0), stop=(c == DC - 1)),
                 R=[xn_b[c], wbig_b[0]], W=[p_b])
        i2 = tt % 2
        gelu_tanh(k, gvt[i2][:], p_[:], p_b, tmpA[i2][:], tmpA_b[i2], gvt_b[i2])
        k.op("dve", lambda e: e.tensor_tensor(out=tmpA[i2][:], in0=gvt[i2][:], in1=gvt[i2][:], op=ALU.mult), R=[gvt_b[i2]], W=[tmpA_b[i2]])
        k.op("dve", lambda e: e.tensor_reduce(out=ssq[:], in_=tmpA[i2][:].rearrange("p (g c) -> p g c", g=4), axis=AX.X, op=ALU.add),
             R=[tmpA_b[i2]], W=[ssq_b])
        k.op("dve", lambda e: e.tensor_scalar(out=ssq[:], in0=ssq[:], scalar1=1.0 / 128, scalar2=1e-6, op0=ALU.mult, op1=ALU.add), R=[ssq_b], W=[ssq_b])
        k.op("act", lambda e: e.activation(out=ssq[:], in_=ssq[:], func=AF.Sqrt), R=[ssq_b], W=[ssq_b])
        k.op("dve", lambda e: e.reciprocal(out=ssq[:], in_=ssq[:]), R=[ssq_b], W=[ssq_b])
        k.op("dve", lambda e: e.tensor_tensor(out=gvt[i2][:].rearrange("p (g c) -> p g c", g=4), in0=gvt[i2][:].rearrange("p (g c) -> p g c", g=4),
                                              in1=ssq[:].unsqueeze(2).to_broadcast([128, 4, 128]), op=ALU.mult), R=[gvt_b[i2], ssq_b], W=[gvt_b[i2]])
        k.op("dve", lambda e: e.tensor_tensor(out=vg[:, tt, :], in0=gvt[i2][:], in1=gainv[:], op=ALU.mult), R=[gvt_b[i2], gainv_b], W=[vg_b[tt]])
    yst = [k.sb(f"yst{i}", [128, 512], BF16, esA) for i in range(2)]; yst_b = [Buf(), Buf()]
    yi = 0
    for g in range(4):
        for blk in range(NB):
            bs = slice(blk * 512, (blk + 1) * 512)
            pu, pu_b = ps.next()
            for c in range(DC):
                k.op("pe", lambda e: e.matmul(pu[:], wbig[1][:, c, g * 128:(g + 1) * 128], xn[:, c, bs], start=(c == 0), stop=(c == DC - 1)),
                     R=[xn_b[c], wbig_b[1]], W=[pu_b])
            pss, pss_b = ps.next()
            for q in range(4):
                tt = blk * 4 + q
                k.op("pe", lambda e: e.matmul(pss[:, q * 128:(q + 1) * 128], vg[:, tt, g * 128:(g + 1) * 128], wsm[:, g, :], start=True, stop=True),
                     R=[vg_b[tt], wsm_b], W=[pss_b])
            i2 = yi % 2
            gelu_tanh(k, gvt[i2][:], pu[:], pu_b, tmpA[i2][:], tmpA_b[i2], gvt_b[i2])
            k.op("dve", lambda e: e.tensor_tensor(out=tmpA[i2][:].rearrange("p (q t) -> p q t", q=4), in0=pss[:].rearrange("p (q t) -> p q t", q=4),
                                                  in1=biasb[:, g, :].unsqueeze(1).to_broadcast([128, 4, 128]), op=ALU.add),
                 R=[pss_b, biasb_b], W=[tmpA_b[i2]])
            k.op("dve", lambda e: e.tensor_tensor(out=yst[i2][:], in0=gvt[i2][:], in1=tmpA[i2][:], op=ALU.mult), R=[gvt_b[i2], tmpA_b[i2]], W=[yst_b[i2]])
            k.dma("sp", yrow(g * 128)[:, bs], yst[i2][:], R=[yst_b[i2]], is_output=True)
            yi += 1

    k.barrier()
    esA.close()
    lbl = k.sb("lbl", [128, 4, 4], F32, es); lbl_b = Buf()
    k.dma("sp", lbl[:], io["lbl"], W=[lbl_b])
    k.op("act", lambda e: e.activation(out=lbl[:], in_=lbl[:], func=AF.Exp), R=[lbl_b], W=[lbl_b])
    tot = k.sb("tot", [128, 4], F32, es); tot_b = Buf()
    lb = k.sb("lb", [128, 4], F32, es); lb_b = Buf()
    oml = k.sb("oml", [128, 4], F32, es); oml_b = Buf()
    k.op("dve", lambda e: e.tensor_reduce(out=tot[:], in_=lbl[:], axis=AX.X, op=ALU.add), R=[lbl_b], W=[tot_b])
    k.op("dve", lambda e: e.reciprocal(out=tot[:], in_=tot[:]), R=[tot_b], W=[tot_b])
    if layer == 0:
        k.op("dve", lambda e: e.memset(lb[:], 0.0), W=[lb_b])
    else:
        k.op("dve", lambda e: e.tensor_reduce(out=lb[:], in_=lbl[:, :, 1:layer + 1], axis=AX.X, op=ALU.add), R=[lbl_b], W=[lb_b])
        k.op("dve", lambda e: e.tensor_tensor(out=lb[:], in0=lb[:], in1=tot[:], op=ALU.mult), R=[lb_b, tot_b], W=[lb_b])
    k.op("dve", lambda e: e.tensor_scalar(out=oml[:], in0=lb[:], scalar1=-1.0, scalar2=1.0, op0=ALU.mult, op1=ALU.add), R=[lb_b], W=[oml_b])
    onorm = k.sb("onorm", [128, 4], F32, es); onorm_b = Buf()
    k.dma("sp", onorm[:], io["onorm"], W=[onorm_b])

    def big(name, dt=F32):
        return k.sb(name, [128, S], dt, es), Buf()
    B1, B1_b = big("B1"); B2, B2_b = big("B2"); B3, B3_b = big("B3"); B4, B4_b = big("B4"); G1, G1_b = big("G1")
    Q1, Q1_b = big("Q1", BF16); Q2, Q2_b = big("Q2", BF16); K1, K1_b = big("K1", BF16); K2, K2_b = big("K2", BF16); I1, I1_b = big("I1", BF16)
    Vt = k.sb("Vt", [128, NT, 128], BF16, es); Vt_b = Buf()
    Kt = k.sb("Kt", [128, NT, 128], BF16, es); Kt_b = Buf()
    att = k.sb("att", [128, NT, 64], BF16, es); att_b = Buf()
    oT, oT_b = big("oT")
    cmid = k.sb("cmid", [128, NCH], F32, es); cmid_b = Buf()
    cend = k.sb("cend", [128, NCH], F32, es); cend_b = Buf()
    decL = k.sb("decL", [128, NCH], F32, es); decL_b = Buf()
    Sf = k.sb("Sf", [128, 128], F32, es); Sf_b = Buf()
    Sb = k.sb("Sb", [128, 128], BF16, es); Sb_b = Buf()
    sqb = [k.sb(f"sqb{i}", [128, 512], BF16, es) for i in range(2)]; sqb_b = [Buf(), Buf()]
    rst = k.sb("rst", [128, 512], F32, es); rst_b = Buf()
    ybig, ybig_b = big("ybig", BF16)

    def proj(col0, evac):
        for blk in range(NB):
            bs = slice(blk * 512, (blk + 1) * 512)
            p_, p_b = ps.next()
            for c in range(DC):
                k.op("pe", lambda e: e.matmul(p_[:], wh[:, c, col0:col0 + 128], xn[:, c, bs], start=(c == 0), stop=(c == DC - 1)),
                     R=[xn_b[c], wh_b], W=[p_b])
            evac(blk, bs, p_, p_b)

    c3 = lambda ap: ap.rearrange("p (n t) -> p n t", t=64)
    for hd in range(4):
        wh, wh_b = wbig[hd % 2], wbig_b[hd % 2]
        k.dma("pool", wh[:], w_in[:, :, 1024 + hd * 512:1024 + (hd + 1) * 512], W=[wh_b])
        proj(0, lambda blk, bs, p_, p_b: k.op("act", lambda e: e.activation(out=B4[:, bs], in_=p_[:], func=AF.Silu), R=[p_b], W=[B4_b]))
        proj(128, lambda blk, bs, p_, p_b: k.op("act", lambda e: e.activation(out=B1[:, bs], in_=p_[:], func=AF.Sigmoid), R=[p_b], W=[B1_b]))
        proj(256, lambda blk, bs, p_, p_b: k.op("act", lambda e: e.activation(out=I1[:, bs], in_=p_[:], func=AF.Copy), R=[p_b], W=[I1_b]))
        proj(384, lambda blk, bs, p_, p_b: k.op("act", lambda e: e.activation(out=G1[:, bs], in_=p_[:], func=AF.Silu), R=[p_b], W=[G1_b]))
        k.op("dve", lambda e: e.tensor_scalar(out=B1[:], in0=B1[:], scalar1=oml[:, hd:hd + 1], scalar2=lb[:, hd:hd + 1], op0=ALU.mult, op1=ALU.add),
             R=[B1_b, oml_b, lb_b], W=[B1_b])
        k.op("dve", lambda e: e.tensor_scalar(out=B3[:], in0=B1[:], scalar1=-1.0, scalar2=1.0, op0=ALU.mult, op1=ALU.add), R=[B1_b], W=[B3_b])
        k.op("dve", lambda e: e.tensor_scalar_max(out=B1[:], in0=B1[:], scalar1=1e-30), R=[B1_b], W=[B1_b])
        k.op("act", lambda e: e.activation(out=B1[:], in_=B1[:], func=AF.Ln), R=[B1_b], W=[B1_b])
        k.op("dve", lambda e: e.tensor_tensor_scan(out=B2[:], data0=rmask[:], data1=B1[:], initial=0.0, op0=ALU.mult, op1=ALU.add),
             R=[rmask_b, B1_b], W=[B2_b])
        k.op("dve", lambda e: e.tensor_copy(out=cmid[:], in_=c3(B2[:])[:, :, 31]), R=[B2_b], W=[cmid_b])
        k.op("dve", lambda e: e.tensor_copy(out=cend[:], in_=c3(B2[:])[:, :, 63]), R=[B2_b], W=[cend_b])
        k.op("act", lambda e: e.activation(out=decL[:], in_=cend[:], func=AF.Exp), R=[cend_b], W=[decL_b])
        k.op("act", lambda e: e.activation(out=B1[:], in_=B2[:], func=AF.Exp), R=[B2_b], W=[B1_b])
        k.op("dve", lambda e: e.tensor_tensor(out=Q1[:], in0=B4[:], in1=B1[:], op=ALU.mult), R=[B4_b, B1_b], W=[Q1_b])
        k.op("dve", lambda e: e.tensor_tensor(out=c3(B1[:]), in0=c3(B2[:]), in1=cmid[:].unsqueeze(2).to_broadcast([128, NCH, 64]), op=ALU.subtract),
             R=[B2_b, cmid_b], W=[B1_b])
        k.op("act", lambda e: e.activation(out=oT[:], in_=B1[:], func=AF.Exp), R=[B1_b], W=[oT_b])
        k.op("dve", lambda e: e.tensor_tensor(out=Q2[:], in0=B4[:], in1=oT[:], op=ALU.mult), R=[B4_b, oT_b], W=[Q2_b])
        k.op("act", lambda e: e.activation(out=B1[:], in_=B1[:], func=AF.Exp, scale=-1.0), R=[B1_b], W=[B1_b])
        k.op("dve", lambda e: e.tensor_tensor(out=K1[:], in0=B3[:], in1=B1[:], op=ALU.mult), R=[B3_b, B1_b], W=[K1_b])
        k.op("dve", lambda e: e.tensor_tensor(out=c3(B1[:]), in0=cend[:].unsqueeze(2).to_broadcast([128, NCH, 64]), in1=c3(B2[:]), op=ALU.subtract),
             R=[B2_b, cend_b], W=[B1_b])
        k.op("act", lambda e: e.activation(out=B1[:], in_=B1[:], func=AF.Exp), R=[B1_b], W=[B1_b])
        k.op("dve", lambda e: e.tensor_tensor(out=K2[:], in0=B3[:], in1=B1[:], op=ALU.mult), R=[B3_b, B1_b], W=[K2_b])
        for src, src_b, dst, dst_b in ((I1, I1_b, Vt, Vt_b), (K2, K2_b, Kt, Kt_b)):
            for q4 in range(NT // 4):
                p_, p_b = ps.next()
                pbf = p_[:].bitcast(BF16)
                for q in range(4):
                    tt = q4 * 4 + q
                    k.op("pe", lambda e: e.transpose(pbf[:, q * 128:(q + 1) * 128], src[:, tt * 128:(tt + 1) * 128], ident[:]),
                         R=[src_b, ident_b], W=[p_b])
                k.op("act", lambda e: e.activation(out=dst[:, q4 * 4:(q4 + 1) * 4, :], in_=pbf[:, 0:512].rearrange("p (q c) -> p q c", q=4), func=AF.Copy),
                     R=[p_b], W=[dst_b])
        for q4 in range(NT // 4):
            p_, p_b = ps.next()
            for q in range(4):
                tt = q4 * 4 + q
                k.op("pe", lambda e: e.matmul(p_[:, q * 128:(q + 1) * 128], K1[:, tt * 128:(tt + 1) * 128], Q2[:, tt * 128:(tt + 1) * 128], start=True, stop=True),
                     R=[K1_b, Q2_b], W=[p_b])
            pv = p_[:].rearrange("p (q t) -> p q t", q=4)
            for half in range(2):
                prt = slice(half * 64, (half + 1) * 64)
                k.op("dve", lambda e: e.tensor_tensor(out=att[prt, q4 * 4:(q4 + 1) * 4, :], in0=pv[prt, :, half * 64:(half + 1) * 64],
                                                      in1=mask64[prt, :].unsqueeze(1).to_broadcast([64, 4, 64]), op=ALU.mult),
                     R=[p_b, mask64_b], W=[att_b])
        k.op("dve", lambda e: e.memset(Sf[:], 0.0), W=[Sf_b])
        k.op("dve", lambda e: e.memset(Sb[:], 0.0), W=[Sb_b])
        psS = [None] * NCH

        def emit_mmS(n):
            bank, bank_b = ps.fixed(6 + n % 2)
            psS[n] = (bank, bank_b)
            tt, prt = n // 2, slice((n % 2) * 64, (n % 2) * 64 + 64)
            k.op("pe", lambda e: e.matmul(bank[:, 0:128], Kt[prt, tt, :], Vt[prt, tt, :], start=True, stop=True), R=[Kt_b, Vt_b], W=[bank_b])
        emit_mmS(0)
        po = po_b = None
        for n in range(NCH):
            tt, prt = n // 2, slice((n % 2) * 64, (n % 2) * 64 + 64)
            if n % 8 == 0:
                po, po_b = ps.fixed(4 + (n // 8) % 2)
            osl = po[:, (n % 8) * 64:(n % 8 + 1) * 64]
            k.op("pe", lambda e: e.matmul(osl, Vt[prt, tt, :], att[prt, tt, :], start=True, stop=False), R=[Vt_b, att_b], W=[po_b])
            if n + 1 < NCH:
                emit_mmS(n + 1)
            k.op("pe", lambda e: e.matmul(osl, Sb[:], Q1[:, n * 64:(n + 1) * 64], start=False, stop=True), R=[Sb_b, Q1_b], W=[po_b])
            bank, bank_b = psS[n]
            k.op("dve", lambda e: e.scalar_tensor_tensor(out=Sf[:], in0=Sf[:], scalar=decL[:, n:n + 1], in1=bank[:, 0:128], op0=ALU.mult, op1=ALU.add),
                 R=[Sf_b, decL_b, bank_b], W=[Sf_b])
            k.op("act", lambda e: e.activation(out=Sb[:], in_=Sf[:], func=AF.Copy), R=[Sf_b], W=[Sb_b])
            if n % 8 == 7:
                blk = n // 8
                k.op("act", lambda e: e.activation(out=oT[:, blk * 512:(blk + 1) * 512], in_=po[:], func=AF.Copy), R=[po_b], W=[oT_b])
        for blk in range(NB):
            bs = slice(blk * 512, (blk + 1) * 512)
            i2 = blk % 2
            k.op("act", lambda e: e.activation(out=sqb[i2][:], in_=oT[:, bs], func=AF.Square), R=[oT_b], W=[sqb_b[i2]])
            p_, p_b = ps.next()
            k.op("pe", lambda e: e.matmul(p_[:], ones[:], sqb[i2][:], start=True, stop=True), R=[ones_b, sqb_b[i2]], W=[p_b])
            k.op("dve", lambda e: e.tensor_scalar(out=rst[:], in0=p_[:], scalar1=1.0 / 128, scalar2=1e-6, op0=ALU.mult, op1=ALU.add), R=[p_b], W=[rst_b])
            k.op("act", lambda e: e.activation(out=rst[:], in_=rst[:], func=AF.Ln), R=[rst_b], W=[rst_b])
            k.op("act", lambda e: e.activation(out=rst[:], in_=rst[:], func=AF.Exp, scale=-0.5), R=[rst_b], W=[rst_b])
            k.op("dve", lambda e: e.scalar_tensor_tensor(out=rst[:], in0=oT[:, bs], scalar=onorm[:, hd:hd + 1], in1=rst[:], op0=ALU.mult, op1=ALU.mult),
                 R=[oT_b, onorm_b, rst_b], W=[rst_b])
            k.op("dve", lambda e: e.tensor_tensor(out=ybig[:, bs], in0=rst[:], in1=G1[:, bs], op=ALU.mult), R=[rst_b, G1_b], W=[ybig_b])
        k.dma("sp", yrow(512 + hd * 128), ybig[:], R=[ybig_b], is_output=True)


L = 64
CPB = 8
WSCALE = -0.6065306597126334
GN_EPS = 64e-5
BT = 512


def mix_odd(kb, io, es=None):
    k = kb
    first = io["first"]
    DC_ = 16
    S_ = 2048
    NBLK = S_ // BT
    pst = [k.ps(f"ps{i}", [128, 512], F32, es) for i in range(8)]
    psb = [Buf(excl=True) for _ in range(8)]
    rot = [0]

    def ps_next():
        i = rot[0]
        rot[0] = (i + 1) % 4
        return pst[i], psb[i]

    def cst(name, shape, dt, src, q="sp"):
        t = k.sb(name, shape, dt, es)
        b = Buf()
        k.dma(q, t[:], src, W=[b])
        return t, b
    ident, ident_b = cst("ident", [128, 128], BF16, io["ident"], "pool")
    bones, bones_b = cst("bones", [128, 128], BF16, io["bones"], "pool")
    bones1, bones1_b = cst("bones1", [128, 128], BF16, io["bones1"], "pool")
    cmask, cmask_b = cst("cmask", [64, 5, 64], F32, io["cmask"])
    rmask, rmask_b = cst("rmask", [128, BT], F32, io["rmask"][:, 0:BT])
    mixm, mixm_b = cst("mixm", [128, 6, DC_], F32, io["mixpc"])
    cols, cols_b = cst("cols", [128, 8, 8], F32, io["cols"])
    mixo = k.sb("mixo", [128, 6, DC_], F32, es); mixo_b = Buf()
    k.op("dve", lambda e: e.tensor_scalar(out=mixo[:], in0=mixm[:], scalar1=-1.0, scalar2=1.0, op0=ALU.mult, op1=ALU.add), R=[mixm_b], W=[mixo_b])
    omka = k.sb("omka", [128, 8], F32, es); omka_b = Buf()
    k.op("dve", lambda e: e.tensor_scalar(out=omka[:], in0=cols[:, :, 3], scalar1=-1.0, scalar2=1.0, op0=ALU.mult, op1=ALU.add), R=[cols_b], W=[omka_b])
    w2t, w2t_b = cst("w2t", [96, 1024], BF16, io["w2"], "pool")
    a2t, a2t_b = cst("a2t", [96, 1024], BF16, io["a2"], "pool")
    g2t, g2t_b = cst("g2t", [128, 2, 1024], BF16, io["g2"].rearrange("(c p) n -> p c n", p=128), "pool")
    if not first:
        v2t, v2t_b = cst("v2t", [64, 1024], BF16, io["v2"], "pool")
    wv_ = lambda name: io[name].rearrange("(c p) n -> p c n", p=128)
    wr, wk, wv = wv_("wr"), wv_("wk"), wv_("wv")
    yrow = io.get("y_rows") or (lambda r0: io["yT"][r0:r0 + 128, :])
    c3 = lambda ap: ap.rearrange("p (n t) -> p n t", t=L)

    def colv(pr, vi):
        return cols[:, pr, vi:vi + 1]

    xe = k.sb("xe", [128, DC_, BT + 8], BF16, es)
    xe_b = [Buf() for _ in range(DC_)]
    xcar = k.sb("xcar", [128, DC_, 1], BF16, es); xcar_b = Buf()
    k.op("pool", lambda e: e.memset(xcar[:], 0.0), W=[xcar_b])
    hw = k.sb("hw", [96, BT], BF16, es); hw_b = Buf()
    ha = k.sb("ha", [96, BT], BF16, es); ha_b = Buf()
    hg = k.sb("hg", [128, 2, BT], BF16, es); hg_b = Buf()
    if not first:
        hv = k.sb("hv", [64, BT], BF16, es); hv_b = Buf()
    Mf = k.sb("Mf", [128, 8, 128], F32, es); Mf_b = [Buf() for _ in range(8)]
    Mb = k.sb("Mb", [128, 8, 128], BF16, es); Mb_b = [Buf() for _ in range(8)]
    k.op("dve", lambda e: e.memset(Mf[:], 0.0), W=Mf_b)
    k.op("pool", lambda e: e.memset(Mb[:], 0.0), W=Mb_b)

    class TS:
        pass

    def make_ts(sfx, py_bank, pch_bank):
        t = TS()
        t.sfx = sfx
        t.py, t.py_b = pst[py_bank], psb[py_bank]
        t.pch, t.pch_b = pst[pch_bank], psb[pch_bank]

        def big(name, dt=F32):
            return k.sb(name + sfx, [128, BT], dt, es), Buf()
        t.V, t.V_b = big("V"); t.K, t.K_b = big("K"); t.A, t.A_b = big("A"); t.KK, t.KK_b = big("KK"); t.CUM, t.CUM_b = big("CUM")
        t.T1, t.T1_b = big("T1"); t.T2, t.T2_b = big("T2"); t.Vb, t.Vb_b = big("Vb", BF16); t.G, t.G_b = big("G", BF16)
        t.AR = k.sb("AR" + sfx, [128, CPB, 2, L], BF16, es); t.AR_b = Buf()
        t.BK = k.sb("BK" + sfx, [128, CPB, 2, L], BF16, es); t.BK_b = Buf()
        t.BhKh = k.sb("BhKh" + sfx, [128, 2, BT], BF16, es); t.BhKh_b = Buf()
        t.cend = k.sb("cend" + sfx, [128, CPB], F32, es); t.cend_b = Buf()
        t.decL = k.sb("decL" + sfx, [128, CPB], F32, es); t.decL_b = Buf()
        t.sqb = k.sb("sqb" + sfx, [128, 512], BF16, es); t.sqb_b = Buf()
        t.pads = []
        for nm in ("Vpad", "Bpad", "Kpad"):
            tt = k.sb(nm + sfx, [64, CPB, 2, 128], BF16, es); bb = Buf()
            k.op("pool", lambda e: e.memset(tt[:], 0.0), W=[bb])
            t.pads.append((tt, bb))
        t.sc = {}
        for nm in ("N", "NT", "N2", "NT2", "X", "X2", "AAK", "PBR", "PKR"):
            t.sc[nm] = (k.sb("sc" + nm + sfx, [64, 2, CPB, L], BF16, es), Buf())
        t.W0 = k.sb("W0" + sfx, [64, 2, L], BF16, es); t.W0_b = Buf()
        t.Upad = k.sb("Upad" + sfx, [64, 384], BF16, es); t.Upad_b = Buf()
        k.op("pool", lambda e: e.memset(t.Upad[:], 0.0), W=[t.Upad_b])
        t.Upad_w = t.Upad[:, 0:384].rearrange("p (h w) -> p h w", w=192)[:, :, 0:64]
        t.Upad_r = t.Upad[:, 0:256].rearrange("p (h w) -> p h w", w=128)
        t.yf = k.sb("yf" + sfx, [128, 512], F32, es); t.yf_b = Buf()
        t.ysb = k.sb("ysb" + sfx, [128, 512], BF16, es); t.ysb_b = Buf()
        t.ysq = k.sb("ysq" + sfx, [128, 512], BF16, es); t.ysq_b = Buf()
        t.t3 = k.sb("t3" + sfx, [128, 512], F32, es); t.t3_b = Buf()
        t.yo = k.sb("yo" + sfx, [128, 512], BF16, es); t.yo_b = Buf()
        t.stage = k.sb("stage" + sfx, [128, DC_, 128], F32, es); t.stage_b = Buf()
        t.Wa = k.sb("Wa" + sfx, [128, DC_, 128], BF16, es); t.Wa_b = Buf()
        t.Wb = k.sb("Wb" + sfx, [128, DC_, 128], BF16, es); t.Wb_b = Buf()
        return t

    def load_mixed(t, wview, ncols, mi):
        k.dma("sp", t.stage[:, :, :ncols], wview, W=[t.stage_b])
        k.op("pool", lambda e: e.tensor_tensor(out=t.Wa[:, :, :ncols], in0=t.stage[:, :, :ncols],
                                               in1=mixo[:, mi, :].unsqueeze(2).to_broadcast([128, DC_, ncols]), op=ALU.mult),
             R=[t.stage_b, mixo_b], W=[t.Wa_b])
        k.op("pool", lambda e: e.tensor_tensor(out=t.Wb[:, :, :ncols], in0=t.stage[:, :, :ncols],
                                               in1=mixm[:, mi, :].unsqueeze(2).to_broadcast([128, DC_, ncols]), op=ALU.mult),
             R=[t.stage_b, mixm_b], W=[t.Wb_b])

    def mixed_proj(t, ncols, evac):
        p_, p_b = ps_next()
        for c in range(DC_):
            k.op("pe", lambda e: e.matmul(p_[:ncols, :], t.Wa[:, c, 0:ncols], xe[:, c, 8:8 + BT], start=(c == 0), stop=False),
                 R=[t.Wa_b, xe_b[c]], W=[p_b])
        for c in range(DC_):
            k.op("pe", lambda e: e.matmul(p_[:ncols, :], t.Wb[:, c, 0:ncols], xe[:, c, 7:7 + BT], start=False, stop=(c == DC_ - 1)),
                 R=[t.Wb_b, xe_b[c]], W=[p_b])
        evac(p_, p_b)

    tsA = make_ts("A", 4, 6)
    tsB = make_ts("B", 5, 7)

    def unit(t, pr, blk):
        pc = slice(pr * 128, (pr + 1) * 128)
        tb = blk * BT
        V, V_b, K, K_b, A, A_b, KK, KK_b, CUM, CUM_b, T1, T1_b, T2, T2_b = t.V, t.V_b, t.K, t.K_b, t.A, t.A_b, t.KK, t.KK_b, t.CUM, t.CUM_b, t.T1, t.T1_b, t.T2, t.T2_b
        Vb, Vb_b, G, G_b, AR, AR_b, BK, BK_b, BhKh, BhKh_b = t.Vb, t.Vb_b, t.G, t.G_b, t.AR, t.AR_b, t.BK, t.BK_b, t.BhKh, t.BhKh_b
        cend, cend_b, decL, decL_b, sqb, sqb_b = t.cend, t.cend_b, t.decL, t.decL_b, t.sqb, t.sqb_b
        load_mixed(t, wv[:, :, pc], 128, 3)
        yield
        mixed_proj(t, 128, lambda p_, p_b: k.op("act", lambda e: e.activation(out=V[:], in_=p_[:], func=AF.Copy), R=[p_b], W=[V_b]))
        yield
        if first:
            k.dma("sp", io["vf_out"][pc, tb:tb + BT], V[:], R=[V_b], is_output=True)
        else:
            k.dma("sp", T1[:], io["vf_in"][pc, tb:tb + BT], W=[T1_b])
            p_, p_b = ps_next()
            k.op("pe", lambda e: e.matmul(p_[:], v2t[:, pc], hv[:], start=True, stop=True), R=[v2t_b, hv_b], W=[p_b])
            k.op("act", lambda e: e.activation(out=T2[:], in_=p_[:], func=AF.Sigmoid, bias=colv(pr, 7)), R=[p_b, cols_b], W=[T2_b])
            k.op("dve", lambda e: e.tensor_tensor(out=T1[:], in0=T1[:], in1=V[:], op=ALU.subtract), R=[T1_b, V_b], W=[T1_b])
            k.op("dve", lambda e: e.tensor_tensor(out=T1[:], in0=T1[:], in1=T2[:], op=ALU.mult), R=[T1_b, T2_b], W=[T1_b])
            k.op("dve", lambda e: e.tensor_tensor(out=V[:], in0=V[:], in1=T1[:], op=ALU.add), R=[T1_b, V_b], W=[V_b])
        k.op("act", lambda e: e.activation(out=Vb[:], in_=V[:], func=AF.Copy), R=[V_b], W=[Vb_b])
        load_mixed(t, wk[:, :, pc], 128, 2)
        yield
        mixed_proj(t, 128, lambda p_, p_b: k.op("act", lambda e: e.activation(out=K[:], in_=p_[:], func=AF.Copy), R=[p_b], W=[K_b]))
        yield
        p_, p_b = ps_next()
        k.op("pe", lambda e: e.matmul(p_[:], a2t[:, pc], ha[:], start=True, stop=True), R=[a2t_b, ha_b], W=[p_b])
        k.op("act", lambda e: e.activation(out=A[:], in_=p_[:], func=AF.Sigmoid, bias=colv(pr, 1)), R=[p_b, cols_b], W=[A_b])
        p_, p_b = ps_next()
        k.op("pe", lambda e: e.matmul(p_[:], w2t[:, pc], hw[:], start=True, stop=True), R=[w2t_b, hw_b], W=[p_b])
        k.op("act", lambda e: e.activation(out=T1[:], in_=p_[:], func=AF.Sigmoid, bias=colv(pr, 0)), R=[p_b, cols_b], W=[T1_b])
        p_, p_b = ps_next()
        for gc in range(2):
            k.op("pe", lambda e: e.matmul(p_[:], g2t[:, gc, pc], hg[:, gc, :], start=(gc == 0), stop=(gc == 1)), R=[g2t_b, hg_b], W=[p_b])
        k.op("act", lambda e: e.activation(out=G[:], in_=p_[:], func=AF.Copy), R=[p_b], W=[G_b])
        yield
        k.op("dve", lambda e: e.tensor_scalar(out=T1[:], in0=T1[:], scalar1=WSCALE, scalar2=None, op0=ALU.mult), R=[T1_b], W=[T1_b])
        k.op("dve", lambda e: e.tensor_tensor_scan(out=CUM[:], data0=rmask[:], data1=T1[:], initial=0.0, op0=ALU.mult, op1=ALU.add),
             R=[rmask_b, T1_b], W=[CUM_b])
        k.op("dve", lambda e: e.tensor_copy(out=cend[:], in_=c3(CUM[:])[:, :, L - 1]), R=[CUM_b], W=[cend_b])
        k.op("act", lambda e: e.activation(out=decL[:], in_=cend[:], func=AF.Exp), R=[cend_b], W=[decL_b])
        k.op("dve", lambda e: e.tensor_scalar(out=KK[:], in0=K[:], scalar1=colv(pr, 2), scalar2=None, op0=ALU.mult), R=[K_b, cols_b], W=[KK_b])
        k.op("act", lambda e: e.activation(out=sqb[:], in_=KK[:], func=AF.Square), R=[KK_b], W=[sqb_b])
        p_, p_b = ps_next()
        k.op("pe", lambda e: e.matmul(p_[:], bones1[:], sqb[:], start=True, stop=True), R=[bones1_b, sqb_b], W=[p_b])
        k.op("dve", lambda e: e.tensor_scalar_max(out=T2[:], in0=p_[:], scalar1=1e-24), R=[p_b], W=[T2_b])
        yield
        k.op("act", lambda e: e.activation(out=T2[:], in_=T2[:], func=AF.Ln), R=[T2_b], W=[T2_b])
        k.op("act", lambda e: e.activation(out=T2[:], in_=T2[:], func=AF.Exp, scale=-0.5), R=[T2_b], W=[T2_b])
        k.op("dve", lambda e: e.tensor_tensor(out=KK[:], in0=KK[:], in1=T2[:], op=ALU.mult), R=[KK_b, T2_b], W=[KK_b])
        k.op("dve", lambda e: e.tensor_scalar(out=T2[:], in0=A[:], scalar1=colv(pr, 3), scalar2=omka[:, pr:pr + 1], op0=ALU.mult, op1=ALU.add),
             R=[A_b, cols_b, omka_b], W=[T2_b])
        k.op("dve", lambda e: e.tensor_tensor(out=K[:], in0=K[:], in1=T2[:], op=ALU.mult), R=[K_b, T2_b], W=[K_b])
        yield
        k.op("dve", lambda e: e.tensor_tensor(out=T2[:], in0=CUM[:], in1=T1[:], op=ALU.subtract), R=[CUM_b, T1_b], W=[T2_b])
        k.op("act", lambda e: e.activation(out=T2[:], in_=T2[:], func=AF.Exp), R=[T2_b], W=[T2_b])
        k.op("dve", lambda e: e.scalar_tensor_tensor(out=AR[:, :, 0, :], in0=c3(KK[:]), scalar=-1.0, in1=c3(T2[:]), op0=ALU.mult, op1=ALU.mult),
             R=[KK_b, T2_b], W=[AR_b])
        k.op("dve", lambda e: e.tensor_tensor(out=T1[:], in0=KK[:], in1=A[:], op=ALU.mult), R=[KK_b, A_b], W=[T1_b])
        k.op("act", lambda e: e.activation(out=T2[:], in_=CUM[:], func=AF.Exp, scale=-1.0), R=[CUM_b], W=[T2_b])
        yield
        k.op("dve", lambda e: e.tensor_tensor(out=BK[:, :, 0, :], in0=c3(T1[:]), in1=c3(T2[:]), op=ALU.mult), R=[T1_b, T2_b], W=[BK_b])
        k.op("dve", lambda e: e.tensor_tensor(out=BK[:, :, 1, :], in0=c3(K[:]), in1=c3(T2[:]), op=ALU.mult), R=[K_b, T2_b], W=[BK_b])
        k.op("dve", lambda e: e.tensor_tensor(out=c3(T2[:]), in0=cend[:].unsqueeze(2).to_broadcast([128, CPB, L]), in1=c3(CUM[:]), op=ALU.subtract),
             R=[cend_b, CUM_b], W=[T2_b])
        k.op("act", lambda e: e.activation(out=T2[:], in_=T2[:], func=AF.Exp), R=[T2_b], W=[T2_b])
        yield
        k.op("dve", lambda e: e.tensor_tensor(out=BhKh[:, 0, :], in0=T1[:], in1=T2[:], op=ALU.mult), R=[T1_b, T2_b], W=[BhKh_b])
        k.op("dve", lambda e: e.tensor_tensor(out=BhKh[:, 1, :], in0=K[:], in1=T2[:], op=ALU.mult), R=[K_b, T2_b], W=[BhKh_b])
        k.op("act", lambda e: e.activation(out=T2[:], in_=CUM[:], func=AF.Exp), R=[CUM_b], W=[T2_b])
        load_mixed(t, wr[:, :, pc], 128, 0)
        yield

        def ev(p_, p_b):
            k.op("dve", lambda e: e.tensor_tensor(out=AR[:, :, 1, :], in0=c3(p_[:]), in1=c3(T2[:]), op=ALU.mult), R=[p_b, T2_b], W=[AR_b])
            k.op("dve", lambda e: e.scalar_tensor_tensor(out=sqb[:], in0=p_[:], scalar=colv(pr, 4), in1=K[:], op0=ALU.mult, op1=ALU.mult),
                 R=[p_b, cols_b, K_b], W=[sqb_b])
            p2, p2_b = ps_next()
            k.op("pe", lambda e: e.matmul(p2[:], bones1[:], sqb[:], start=True, stop=True), R=[bones1_b, sqb_b], W=[p2_b])
            k.op("dve", lambda e: e.tensor_tensor(out=T1[:], in0=p2[:], in1=V[:], op=ALU.mult), R=[p2_b, V_b], W=[T1_b])
        mixed_proj(t, 128, ev)
        yield
        (Vpad, Vpad_b), (Bpad, Bpad_b), (Kpad, Kpad_b) = t.pads
        for src, dst, dst_b, src_b in ((Vb[:, :], Vpad, Vpad_b, Vb_b), (BhKh[:, 0, :], Bpad, Bpad_b, BhKh_b), (BhKh[:, 1, :], Kpad, Kpad_b, BhKh_b)):
            p_, p_b = ps_next()
            pbf = p_[:].bitcast(BF16)
            for n in range(CPB):
                k.op("pe", lambda e: e.transpose(pbf[0:64, n * 128:(n + 1) * 128], src[:, n * L:(n + 1) * L], ident[:]), R=[src_b, ident_b], W=[p_b])
            pv = pbf[0:64, :].rearrange("p (n c) -> p n c", c=128)
            k.op("act", lambda e: e.activation(out=dst[:, :, 0, 0:64], in_=pv[:, :, 0:64], func=AF.Copy), R=[p_b], W=[dst_b])
            k.op("act", lambda e: e.activation(out=dst[:, :, 1, 64:128], in_=pv[:, :, 64:128], func=AF.Copy), R=[p_b], W=[dst_b])
            yield
        sc = t.sc
        for h in range(2):
            ph = slice(h * 64, (h + 1) * 64)
            for nm, lhs, li, rhs, ri, mi in (("N", BK, 0, AR, 0, 0), ("PBR", BK, 0, AR, 1, 1), ("AAK", BK, 1, AR, 0, 2), ("PKR", BK, 1, AR, 1, 3), ("NT", AR, 0, BK, 0, 4)):
                p_, p_b = ps_next()
                for n in range(CPB):
                    k.op("pe", lambda e: e.matmul(p_[0:64, n * L:(n + 1) * L], lhs[ph, n, li, :], rhs[ph, n, ri, :], start=True, stop=True),
                         R=[AR_b, BK_b], W=[p_b])
                dst, dst_b = sc[nm]
                k.op("dve", lambda e: e.tensor_tensor(out=dst[:, h, :, :], in0=p_[0:64, :].rearrange("p (n t) -> p n t", t=L),
                                                      in1=cmask[:, mi, :].unsqueeze(1).to_broadcast([64, CPB, L]), op=ALU.mult),
                     R=[p_b, cmask_b], W=[dst_b])
                yield
        N_, N_b_ = sc["N"]; NT_, NT_b_ = sc["NT"]; N2_, N2_b_ = sc["N2"]; NT2_, NT2_b_ = sc["NT2"]; X_, X_b_ = sc["X"]; X2_, X2_b_ = sc["X2"]
        k.op("dve", lambda e: e.tensor_tensor(out=X_[:], in0=N_[:], in1=ident[0:64, 0:64].unsqueeze(1).unsqueeze(1).to_broadcast([64, 2, CPB, L]), op=ALU.add),
             R=[N_b_, ident_b], W=[X_b_])
        cur = (N_, N_b_, NT_, NT_b_, X_, X_b_)
        nxt = (N2_, N2_b_, NT2_, NT2_b_, X2_, X2_b_)
        for j in range(1, 6):
            cN, cN_b, cNT, cNT_b, cX, cX_b = cur
            nN, nN_b, nNT, nNT_b, nX, nX_b = nxt
            for h in range(2):
                if j < 5:
                    p_, p_b = ps_next()
                    for n in range(CPB):
                        k.op("pe", lambda e: e.matmul(p_[0:64, n * L:(n + 1) * L], cNT[:, h, n, :], cN[:, h, n, :], start=True, stop=True), R=[cNT_b, cN_b], W=[p_b])
                    k.op("act", lambda e: e.activation(out=nN[:, h, :, :], in_=p_[0:64, :].rearrange("p (n t) -> p n t", t=L), func=AF.Copy), R=[p_b], W=[nN_b])
                p_, p_b = ps_next()
                for n in range(CPB):
                    k.op("pe", lambda e: e.matmul(p_[0:64, n * L:(n + 1) * L], cN[:, h, n, :], cNT[:, h, n, :], start=True, stop=True), R=[cNT_b, cN_b], W=[p_b])
                k.op("dve", lambda e: e.tensor_copy(out=nNT[:, h, :, :], in_=p_[0:64, :].rearrange("p (n t) -> p n t", t=L)), R=[p_b], W=[nNT_b])
                yield
            for h in range(2):
                p_, p_b = ps_next()
                for n in range(CPB):
                    k.op("pe", lambda e: e.matmul(p_[0:64, n * L:(n + 1) * L], nNT[:, h, n, :], cX[:, h, n, :], start=True, stop=True), R=[nNT_b, cX_b], W=[p_b])
                k.op("dve", lambda e: e.tensor_tensor(out=nX[:, h, :, :], in0=p_[0:64, :].rearrange("p (n t) -> p n t", t=L), in1=cX[:, h, :, :], op=ALU.add),
                     R=[p_b, cX_b], W=[nX_b])
                yield
            cur, nxt = nxt, cur
        Xf, Xf_b = cur[4], cur[5]
        AAK, AAK_b = sc["AAK"]; PBR, PBR_b = sc["PBR"]; PKR, PKR_b = sc["PKR"]
        py, py_b, pch, pch_b = t.py, t.py_b, t.pch, t.pch_b
        W0, W0_b, Upad_b, Upad_w, Upad_r = t.W0, t.W0_b, t.Upad_b, t.Upad_w, t.Upad_r
        mf, mb, mf_b, mb_b = Mf[:, pr, :], Mb[:, pr, :], Mf_b[pr], Mb_b[pr]
        for n in range(CPB):
            k.op("pe", lambda e: e.matmul(pch[0:64, 0:128], AR[:, n, 0, :], mb, start=True, stop=False), R=[AR_b, mb_b], W=[pch_b])
            for h in range(2):
                k.op("pe", lambda e: e.matmul(pch[0:64, 0:128], AAK[:, h, n, :], Vpad[:, n, h, :], start=False, stop=(h == 1)),
                     R=[AAK_b, Vpad_b], W=[pch_b])
            k.op("act", lambda e: e.activation(out=W0[:], in_=pch[0:64, 0:128].rearrange("p (h v) -> p h v", h=2), func=AF.Copy), R=[pch_b], W=[W0_b])
            yield
            for h in range(2):
                k.op("pe", lambda e: e.matmul(pch[0:64, 128 + h * L:128 + (h + 1) * L], Xf[:, h, n, :], W0[:, h, :], start=True, stop=True),
                     R=[Xf_b, W0_b], W=[pch_b])
            k.op("act", lambda e: e.activation(out=Upad_w, in_=pch[0:64, 128:256].rearrange("p (h v) -> p h v", h=2), func=AF.Copy), R=[pch_b], W=[Upad_b])
            yield
            ysl = py[:, n * L:(n + 1) * L]
            k.op("pe", lambda e: e.matmul(ysl, mb, AR[:, n, 1, :], start=True, stop=False), R=[mb_b, AR_b], W=[py_b])
            for h in range(2):
                k.op("pe", lambda e: e.matmul(ysl, Upad_r[:, h, :], PBR[:, h, n, :], start=False, stop=False), R=[Upad_b, PBR_b], W=[py_b])
                k.op("pe", lambda e: e.matmul(ysl, Vpad[:, n, h, :], PKR[:, h, n, :], start=False, stop=(h == 1)), R=[Vpad_b, PKR_b], W=[py_b])
            for h in range(2):
                k.op("pe", lambda e: e.matmul(pch[:, 256:384], Bpad[:, n, h, :], Upad_r[:, h, :], start=(h == 0), stop=False), R=[Bpad_b, Upad_b], W=[pch_b])
                k.op("pe", lambda e: e.matmul(pch[:, 256:384], Kpad[:, n, h, :], Vpad[:, n, h, :], start=False, stop=(h == 1)), R=[Kpad_b, Vpad_b], W=[pch_b])
            k.op("dve", lambda e: e.scalar_tensor_tensor(out=mb, in0=mf, scalar=decL[:, n:n + 1], in1=pch[:, 256:384], op0=ALU.mult, op1=ALU.add),
                 R=[mf_b, decL_b, pch_b], W=[mb_b])
            k.op("dve", lambda e: e.scalar_tensor_tensor(out=mf, in0=mf, scalar=decL[:, n:n + 1], in1=pch[:, 256:384], op0=ALU.mult, op1=ALU.add),
                 R=[mf_b, decL_b, pch_b], W=[mf_b])
            yield
        yf, yf_b, ysb, ysb_b, ysq, ysq_b, t3, t3_b, yo, yo_b = t.yf, t.yf_b, t.ysb, t.ysb_b, t.ysq, t.ysq_b, t.t3, t.t3_b, t.yo, t.yo_b
        k.op("act", lambda e: e.activation(out=yf[:], in_=py[:], func=AF.Copy), R=[py_b], W=[yf_b])
        if io.get("yraw") is not None:
            k.dma("sp", io["yraw"][pc, tb:tb + BT], yf[:], R=[yf_b], is_output=True)
        k.op("dve", lambda e: e.tensor_copy(out=ysb[:], in_=yf[:]), R=[yf_b], W=[ysb_b])
        k.op("act", lambda e: e.activation(out=ysq[:], in_=yf[:], func=AF.Square), R=[yf_b], W=[ysq_b])
        pm, pm_b = ps_next()
        k.op("pe", lambda e: e.matmul(pm[:], bones[:], ysb[:], start=True, stop=True), R=[bones_b, ysb_b], W=[pm_b])
        pq, pq_b = ps_next()
        k.op("pe", lambda e: e.matmul(pq[:], bones[:], ysq[:], start=True, stop=True), R=[bones_b, ysq_b], W=[pq_b])
        yield
        k.op("act", lambda e: e.activation(out=t3[:], in_=pm[:], func=AF.Square), R=[pm_b], W=[t3_b])
        k.op("dve", lambda e: e.tensor_tensor(out=t3[:], in0=pq[:], in1=t3[:], op=ALU.subtract), R=[pq_b, t3_b], W=[t3_b])
        k.op("dve", lambda e: e.tensor_scalar(out=t3[:], in0=t3[:], scalar1=GN_EPS, scalar2=None, op0=ALU.add), R=[t3_b], W=[t3_b])
        k.op("act", lambda e: e.activation(out=t3[:], in_=t3[:], func=AF.Ln), R=[t3_b], W=[t3_b])
        k.op("act", lambda e: e.activation(out=t3[:], in_=t3[:], func=AF.Exp, scale=-0.5), R=[t3_b], W=[t3_b])
        k.op("dve", lambda e: e.tensor_tensor(out=yf[:], in0=yf[:], in1=pm[:], op=ALU.subtract), R=[yf_b, pm_b], W=[yf_b])
        yield
        k.op("dve", lambda e: e.scalar_tensor_tensor(out=yf[:], in0=yf[:], scalar=colv(pr, 5), in1=t3[:], op0=ALU.mult, op1=ALU.mult),
             R=[yf_b, cols_b, t3_b], W=[yf_b])
        k.op("dve", lambda e: e.scalar_tensor_tensor(out=yf[:], in0=yf[:], scalar=colv(pr, 6), in1=T1[:], op0=ALU.add, op1=ALU.add),
             R=[yf_b, cols_b, T1_b], W=[yf_b])
        k.op("dve", lambda e: e.tensor_tensor(out=yo[:], in0=yf[:], in1=G[:], op=ALU.mult), R=[yf_b, G_b], W=[yo_b])
        k.dma("sp", yrow(pr * 128)[:, tb:tb + BT], yo[:], R=[yo_b], is_output=True)
        yield

    def lora_hidden(wname, ncols, mi, dst, dst_b, func, col0=0, part=None):
        load_mixed(tsA, wv_(wname)[:, :, col0:col0 + ncols], ncols, mi)
        mixed_proj(tsA, ncols, lambda p_, p_b: k.op("act", lambda e: e.activation(out=dst, in_=p_[:ncols, :], func=func), R=[p_b], W=[dst_b]))

    for blk in range(NBLK):
        for c in range(DC_):
            k.op("pool", lambda e: e.tensor_copy(out=xe[:, c, 7:8], in_=xcar[:, c, :]), R=[xcar_b], W=[xe_b[c]])
            if io.get("hn_src") is not None:
                r_, off = (blk * BT) // 1024, (blk * BT) % 1024
                k.dma("sp", xe[:, c, 8:8 + BT], io["hn_src"](c, r_)[:, off:off + BT], W=[xe_b[c]])
            else:
                k.dma("sp", xe[:, c, 8:8 + BT], io["hn"].rearrange("(c p) t -> p c t", p=128)[:, c, blk * BT:(blk + 1) * BT], W=[xe_b[c]])
        if blk + 1 < NBLK:
            k.op("pool", lambda e: e.tensor_copy(out=xcar[:], in_=xe[:, :, 7 + BT:8 + BT]), R=xe_b, W=[xcar_b])
        lora_hidden("w1", 96, 1, hw[:], hw_b, AF.Tanh)
        lora_hidden("a1", 96, 4, ha[:], ha_b, AF.Copy)
        for gc in range(2):
            lora_hidden("g1", 128, 5, hg[:, gc, :], hg_b, AF.Sigmoid, col0=gc * 128)
        if not first:
            lora_hidden("v1", 64, 3, hv[:], hv_b, AF.Copy)
        for pp in range(4):
            ga = unit(tsA, 2 * pp, blk)
            gb = unit(tsB, 2 * pp + 1, blk)
            alive = [ga, gb]
            while alive:
                for g_ in list(alive):
                    try:
                        next(g_)
                    except StopIteration:
                        alive.remove(g_)


from concourse.bass_utils import run_bass_kernel_spmd
import ml_dtypes
_bf = ml_dtypes.bfloat16
NCORES = 8
DEPTH = 4
LITE = False


def _pc(g):
    return np.ascontiguousarray(np.asarray(g, np.float32).reshape(16, 128).T)


def _consts():
    s = np.arange(128)
    mask = (s[:, None] <= s[None, :]).astype(np.float32)
    mask64 = ((s[:, None] % 64) <= np.arange(64)[None, :]).astype(np.float32)
    rmaskE = np.ones((128, 2048), np.float32); rmaskE[:, ::64] = 0.0
    s6 = np.arange(64)
    su = (s6[:, None] < s6[None, :]).astype(np.float32); iu = (s6[:, None] <= s6[None, :]).astype(np.float32); sl = (s6[:, None] > s6[None, :]).astype(np.float32)
    cmask = np.ascontiguousarray(np.stack([su, iu, su, iu, sl], axis=1))
    blk = np.kron(np.eye(2, dtype=np.float32), np.ones((64, 64), np.float32))
    rmaskO = np.ones((128, 1024), np.float32); rmaskO[:, ::64] = 0.0
    return dict(mask=mask, mask64=mask64, rmaskE=rmaskE, ident=np.eye(128, dtype=np.float32), bones=blk / 64.0, bones1=blk, cmask=cmask, rmaskO=rmaskO)


def _even_inputs(d, j, hf):
    w = d["e_w_in"][j]
    g0 = hf * 4
    cols = [w[:, g0 * 128:(g0 + 4) * 128], w[:, 1024 + g0 * 128:1024 + (g0 + 4) * 128]]
    for hd in range(g0, g0 + 4):
        for part in range(4):
            cols.append(w[:, 2048 + part * 1024 + hd * 128:2048 + part * 1024 + (hd + 1) * 128])
    return dict(w_in=np.ascontiguousarray(np.concatenate(cols, axis=1)),
                wsT=np.ascontiguousarray(np.transpose(d["a_ws"][j, g0:g0 + 4], (0, 2, 1))),
                bias=np.ascontiguousarray(d["a_bs"][j, g0:g0 + 4].reshape(1, 512)),
                gainv=np.ascontiguousarray(d["a_vnorm"][j, g0 * 128:(g0 + 4) * 128].reshape(1, 512)),
                onorm=np.ascontiguousarray(d["b_onorm"][j, g0 * 128:(g0 + 4) * 128].reshape(4, 128).T),
                lbl=np.ascontiguousarray(np.transpose(d["b_lb_logits"][:, g0 * 128:(g0 + 4) * 128].reshape(4, 4, 128), (2, 1, 0))))


def _odd_inputs(d, j, hf):
    cs = slice(hf * 1024, (hf + 1) * 1024)
    m = {}
    for nm, key in (("wr", "c_wr"), ("wk", "c_wk"), ("wv", "c_wv"), ("w2", "c_w2"), ("a2", "c_a2"), ("g2", "c_g2")):
        m[nm] = np.ascontiguousarray(d[key][j][:, cs])
    for nm, key in (("w1", "c_w1"), ("a1", "c_a1"), ("g1", "c_g1")):
        m[nm] = np.ascontiguousarray(d[key][j])
    v0 = d["c_v0"][j - 1] if j > 0 else np.zeros(2048, np.float32)
    vecs = [d["c_w0"][j], d["c_a0"][j], d["c_kk"][j], d["c_ka"][j], d["c_rk"][j].reshape(-1), d["c_gn_g"][j], d["c_gn_b"][j], v0]
    cols = np.stack([np.asarray(v)[cs].reshape(8, 128) for v in vecs], axis=-1)
    m["cols"] = np.ascontiguousarray(np.transpose(cols, (1, 0, 2)))
    m["mixpc"] = np.ascontiguousarray(np.transpose(d["c_mix"][j].reshape(6, 16, 128), (2, 0, 1)))
    if j > 0:
        m["v1"] = np.ascontiguousarray(d["c_v1"][j - 1]); m["v2"] = np.ascontiguousarray(d["c_v2"][j - 1][:, cs])
    return m


_EVEN_SHAPES = dict(w_in=[2048, 3072], wsT=[4, 128, 128], bias=[1, 512], gainv=[1, 512], onorm=[128, 4], lbl=[128, 4, 4])
_ODD_SHAPES = dict(wr=[2048, 1024], wk=[2048, 1024], wv=[2048, 1024], w1=[2048, 96], a1=[2048, 96], g1=[2048, 256], w2=[96, 1024], a2=[96, 1024],
                   g2=[256, 1024], mixpc=[128, 6, 16], cols=[128, 8, 8], v1=[2048, 64], v2=[64, 1024])
_CONST_SHAPES = dict(mask=[128, 128], mask64=[128, 64], rmaskE=[128, 2048], ident=[128, 128], bones=[128, 128], bones1=[128, 128],
                     cmask=[64, 5, 64], rmaskO=[128, 1024])


def _wo_perm(d, i):
    j = i // 2
    if i % 2 == 0:
        return d["e_w_out"][j]
    w = d["c_wo"][j]
    return np.concatenate([w[0:512], w[1024:1536], w[512:1024], w[1536:2048]], axis=0)


def build_fused():
    kb = KB()
    nc = kb.nc
    di = lambda n, s, dt=F32: kb.dram(n, s, dt, "ExternalInput")
    xT = di("xT", [2048, 1024])
    if LITE:
        wg = wu = wd = None
    else:
        wg = di("ffn_wg", [DEPTH, 2, 2048, 5632]); wu = di("ffn_wu", [DEPTH, 2, 2048, 5632]); wd = di("ffn_wd", [DEPTH, 2, 5632, 2048])
    norms = di("norms_pc", [DEPTH, 4, 128, 16]); fnorm = di("fnorm_pc", [128, 16])
    plg = di("ple_wg", [DEPTH, 2048, 2048]); plp = di("ple_wp", [DEPTH, 256, 2048]); pT = di("pT", [DEPTH, 256, 1024])
    wo = di("wo", [DEPTH, 2048, 2048]); sel_d = di("sel", [128, 2])
    cst = {n: di(n, s) for n, s in _CONST_SHAPES.items()}
    ev_in = [{n: di(f"e{j}_{n}", s) for n, s in _EVEN_SHAPES.items()} for j in range(2)]
    od_in = [{n: di(f"o{j}_{n}", s) for n, s in _ODD_SHAPES.items() if j > 0 or n not in ("v1", "v2")} for j in range(2)]
    out = kb.dram("out", [2048, 1024], F32, "ExternalOutput")
    internal = lambda n, s, dt: nc.dram_tensor(n, list(s), dt, kind="Internal").ap()
    hspill = internal("hspill", [2048, 1024], F32)
    hn_loc = [internal(f"hn_loc{a}", [1024, 1024], BF16) for a in range(2)]
    hn_pair = [internal(f"hn_pair{a}", [2048, 1024], BF16) for a in range(2)]
    y_loc = [internal(f"y_loc{a}", [512, 2048], BF16) for a in range(2)]
    y_pair = [internal(f"y_pair{a}", [1024, 2048], BF16) for a in range(2)]
    vf = internal("vf", [1024, 2048], F32)

    def row_phase(i):
        kb.phase += 1
        es = contextlib.ExitStack()
        rs = RowState(kb, es)
        if i < 0:
            load_hT(rs, xT)
        else:
            pst = PleState(rs, es)
            selt = kb.sb("sel", [128, 2], F32, es); sel_b = Buf()
            kb.dma("sp", selt[:], sel_d, W=[sel_b])
            load_hT(rs, hspill)
            load_y_select(rs, y_pair, selt, sel_b)
            out_proj_residual(rs, wo[i])
            rmsnorm(rs, norms[i, 2])
            if not LITE:
                ffn(rs, wg[i, 1], wu[i, 1], wd[i, 1])
            rmsnorm(rs, norms[i, 3])
            if not LITE:
                ple(rs, pst, plg[i], plp[i], pT[i])
        if i == DEPTH - 1:
            rmsnorm(rs, fnorm, dst=rs.hT, dst_b=rs.hT_b)
            store_hT(rs, out, is_output=True)
        else:
            rmsnorm(rs, norms[i + 1, 0])
            if not LITE:
                ffn(rs, wg[i + 1, 0], wu[i + 1, 0], wd[i + 1, 0])
            rmsnorm(rs, norms[i + 1, 1])
            store_hT(rs, hspill, is_output=False)
            store_xn_split(rs, hn_loc)
        kb.barrier()
        es.close()

    def mix_phase(i):
        kb.phase += 1
        j = i // 2
        es = contextlib.ExitStack()
        hn_src = lambda c, r_: hn_pair[c // 8][r_ * 1024 + (c % 8) * 128:r_ * 1024 + (c % 8 + 1) * 128, :]
        y_rows = lambda r0: y_loc[r0 // 512][r0 % 512:r0 % 512 + 128, :]
        if i % 2 == 0:
            io = dict(ev_in[j]); io.update(layer=i, hn_src=hn_src, y_rows=y_rows, mask=cst["mask"], mask64=cst["mask64"], ident=cst["ident"], rmask=cst["rmaskE"])
            mix_even(kb, io, es)
        else:
            io = dict(od_in[j]); io.update(first=(j == 0), hn_src=hn_src, y_rows=y_rows, ident=cst["ident"], bones=cst["bones"], bones1=cst["bones1"],
                                           cmask=cst["cmask"], rmask=cst["rmaskO"], vf_out=vf, vf_in=vf)
            mix_odd(kb, io, es)
        kb.barrier()
        es.close()

    row_phase(-1)
    for i in range(DEPTH):
        for a in range(2):
            kb.allgather_pairs(hn_loc[a], hn_pair[a])
        kb.barrier()
        mix_phase(i)
        for a in range(2):
            kb.allgather_pairs(y_loc[a], y_pair[a])
        kb.barrier()
        row_phase(i)
    return kb.finish()


def kernel(**inputs):
    d = {k_: np.asarray(v_) for k_, v_ in inputs.items()}
    x = d["x"].astype(np.float32)
    tsl = lambda hf: slice(hf * 1024, (hf + 1) * 1024)
    shared = dict(ffn_wg=d["ffn_wg"], ffn_wu=d["ffn_wu"], ffn_wd=d["ffn_wd"], ple_wg=d["ple_wg"], ple_wp=d["ple_wp"],
                  norms_pc=np.ascontiguousarray(np.transpose(d["norms"].reshape(DEPTH, 4, 16, 128), (0, 1, 3, 2))), fnorm_pc=_pc(d["final_norm"]),
                  wo=np.ascontiguousarray(np.stack([_wo_perm(d, i) for i in range(DEPTH)])))
    shared.update(_consts())
    per_half = []
    for hf in range(2):
        m = {}
        for j in range(2):
            for n, v in _even_inputs(d, j, hf).items():
                m[f"e{j}_{n}"] = v
            for n, v in _odd_inputs(d, j, hf).items():
                m[f"o{j}_{n}"] = v
        sel = np.zeros((128, 2), np.float32); sel[:, hf] = 1.0
        m["sel"] = sel
        per_half.append(m)
    ims = []
    for c in range(NCORES):
        b, hf = c // 2, c % 2
        m = dict(shared); m.update(per_half[hf])
        m["xT"] = np.ascontiguousarray(x[b, tsl(hf)].T)
        m["pT"] = np.ascontiguousarray(np.transpose(d["p"][:, b, tsl(hf)], (0, 2, 1)))
        ims.append(m)
    nc = build_fused()
    res = run_bass_kernel_spmd(nc, ims, core_ids=list(range(NCORES))).results
    out = np.empty((4, 2048, 2048), np.float32)
    for c in range(NCORES):
        b, hf = c // 2, c % 2
        out[b, tsl(hf)] = res[c]["out"].T
    return out
```

```python
import contextlib
import numpy as np
import concourse.bass as bass
import concourse.mybir as mybir

F32 = mybir.dt.float32
BF16 = mybir.dt.bfloat16
AF = mybir.ActivationFunctionType
ALU = mybir.AluOpType
AX = mybir.AxisListType


class Buf:
    __slots__ = ("w", "r", "name", "excl")

    def __init__(self, name="", excl=False):
        self.w = {}
        self.r = {}
        self.name = name
        self.excl = excl


class Eng:
    def __init__(self, name, e, sem_key):
        self.name = name
        self.e = e
        self.key = sem_key
        self.cnt = 0
        self.waited = {}
        self.old = {}


class KB:
    N_DMA_SEMS = 8
    SEM_LIMIT = 30000

    def __init__(self):
        self.nc = bass.Bass("TRN2", target_bir_lowering=False)
        nc = self.nc
        self.es = contextlib.ExitStack()
        self.sems = []
        self.E = {}
        for name, e in (("pe", nc.tensor), ("act", nc.scalar), ("dve", nc.vector), ("pool", nc.gpsimd), ("sp", nc.sync)):
            key = self._new_sem("e_" + name)
            self.E[name] = Eng(name, e, key)
        self.dma_pool = {}
        for q in ("sp", "pool", "act"):
            keys = [self._new_sem(f"d_{q}{i}") for i in range(self.N_DMA_SEMS)]
            self.dma_pool[q] = {"keys": keys, "vals": [0] * len(keys), "next": 0}
        self.n_inst = 0
        self.out_events = []
        self.phase = 0
        self.extra = {}

    def _new_sem(self, name):
        s = self.es.enter_context(self.nc.semaphore(name))
        self.sems.append(s)
        return len(self.sems) - 1

    def sb(self, name, shape, dtype, es=None):
        t = (es or self.es).enter_context(self.nc.sbuf_tensor(f"s{self.phase}_" + name, list(shape), dtype))
        return t

    def ps(self, name, shape, dtype=F32, es=None):
        t = (es or self.es).enter_context(self.nc.psum_tensor(f"p{self.phase}_" + name, list(shape), dtype))
        return t

    def dram(self, name, shape, dtype, kind):
        return self.nc.dram_tensor(name, list(shape), dtype, kind=kind).ap()

    def _wait(self, eng, deps, skip_self=False):
        for k, v in deps.items():
            if skip_self and k == eng.key:
                continue
            if eng.waited.get(k, 0) >= v:
                continue
            eng.e.wait_ge(self.sems[k], v)
            eng.waited[k] = v
            self.n_inst += 1

    @staticmethod
    def _merge(dst, src):
        for k, v in src.items():
            if dst.get(k, 0) < v:
                dst[k] = v

    def op(self, engname, fn, R=(), W=()):
        eng = self.E[engname]
        deps = {}
        for b in R:
            self._merge(deps, b.w)
            if b.excl:
                for kk_, vv_ in b.r.items():
                    if kk_ != eng.key and deps.get(kk_, 0) < vv_:
                        deps[kk_] = vv_
        wdeps = {}
        for b in W:
            self._merge(wdeps, b.w)
            self._merge(wdeps, b.r)
        self._merge(deps, wdeps)
        if engname == "pe":
            for k_ in list(deps):
                if k_ == eng.key or k_ in eng.old:
                    del deps[k_]
        self._wait(eng, deps)
        if eng.cnt >= self.SEM_LIMIT:
            eng.old[eng.key] = eng.cnt
            eng.key = self._new_sem(f"e_{engname}_{len(eng.old)}")
            eng.cnt = 0
        ins = fn(eng.e)
        eng.cnt += 1
        ins.then_inc(self.sems[eng.key], 1)
        self.n_inst += 1
        ev = {eng.key: eng.cnt}
        for b in R:
            self._merge(b.r, ev)
        for b in W:
            b.w = dict(ev)
            b.r = {}
        return ins

    def dma(self, q, out, in_, R=(), W=(), is_output=False, **kw):
        eng = self.E[q]
        pool = self.dma_pool[q]
        i = pool["next"]
        pool["next"] = (i + 1) % len(pool["keys"])
        key = pool["keys"][i]
        deps = {}
        for b in R:
            self._merge(deps, b.w)
        for b in W:
            self._merge(deps, b.w)
            self._merge(deps, b.r)
        if pool["vals"][i] > 0:
            self._merge(deps, {key: pool["vals"][i]})
        self._wait(eng, deps)
        ins = eng.e.dma_start(out=out, in_=in_, **kw)
        pool["vals"][i] += 16
        ins.then_inc(self.sems[key], 16)
        self.n_inst += 1
        ev = {key: pool["vals"][i]}
        for b in R:
            self._merge(b.r, ev)
        for b in W:
            b.w = dict(ev)
            b.r = {}
        if is_output:
            self.out_events.append(ev)
        return ins

    def all_events(self):
        ev = {}
        for e in self.E.values():
            if e.cnt:
                ev[e.key] = e.cnt
            ev.update(e.old)
        for p in self.dma_pool.values():
            for k, v in zip(p["keys"], p["vals"]):
                if v:
                    ev[k] = v
        ev.update(self.extra)
        return ev

    def allgather_pairs(self, src, dst):
        eng = self.E["pool"]
        key = self._new_sem(f"cc{len(self.extra)}")
        ins = self.nc.gpsimd.collective_compute("AllGather", mybir.AluOpType.bypass, replica_groups=[[0, 1], [2, 3], [4, 5], [6, 7]],
                                                ins=[src], outs=[dst])
        ins.then_inc(self.sems[key], 1)
        self.n_inst += 1
        self.extra[key] = 1

    def barrier(self):
        ev = self.all_events()
        for eng in self.E.values():
            self._wait(eng, ev)

    def finish(self):
        ev = self.all_events()
        self._wait(self.E["sp"], ev)
        return self.nc


D = 2048
DC = 16
DFF = 5632
FC = 44
T = 1024
TH = 512
NTH = T // TH
EPS = 1e-6


class RowState:
    def __init__(self, kb, es=None):
        k = kb
        self.kb = kb
        self.hT = k.sb("hT", [128, DC, T], F32, es)
        self.hT_b = [[Buf(f"hT{c}_{t}") for t in range(NTH)] for c in range(DC)]
        self.xn = k.sb("xn", [128, DC, T], BF16, es)
        self.xn_b = [[Buf() for t in range(NTH)] for c in range(DC)]
        self.act = k.sb("act", [128, FC // 2, T], BF16, es)
        self.act_b = [[Buf() for t in range(NTH)] for c in range(FC // 2)]
        self.wA = [k.sb(f"wA{i}", [128, DC, 256], BF16, es) for i in range(4)]
        self.wA_b = [Buf() for _ in range(4)]
        self.wA_i = 0
        self.wD = [k.sb(f"wD{i}", [128, FC // 2, 128], BF16, es) for i in range(2)]
        self.wD_b = [Buf() for _ in range(2)]
        self.wD_i = 0
        self.psum = [k.ps(f"ps{i}", [128, TH], F32, es) for i in range(8)]
        self.psum_b = [Buf(excl=True) for _ in range(8)]
        self.ps_i = 0
        self.sq = [k.sb(f"sq{i}", [128, TH], BF16, es) for i in range(2)]
        self.sq_b = [Buf() for _ in range(2)]
        self.rstd = k.sb("rstd", [128, TH], F32, es)
        self.rstd_b = Buf()
        self.sg = [k.sb(f"sg{i}", [128, TH], F32, es) for i in range(2)]
        self.sg_b = [Buf() for _ in range(2)]
        self.sg_i = 0
        self.ones = k.sb("ones", [128, 128], BF16, es)
        self.ones_b = Buf()
        self.gcol = k.sb("gcol", [128, DC], F32, es)
        self.gcol_b = Buf()
        k.op("dve", lambda e: e.memset(self.ones[:], 1.0), W=[self.ones_b])

    def next_ps(self):
        i = self.ps_i
        self.ps_i = (i + 1) % 8
        return self.psum[i], self.psum_b[i]

    def next_wA(self):
        i = self.wA_i
        self.wA_i = (i + 1) % 4
        return self.wA[i], self.wA_b[i]

    def next_wD(self):
        i = self.wD_i
        self.wD_i = (i + 1) % 2
        return self.wD[i], self.wD_b[i]


def load_hT(rs, src):
    k = rs.kb
    v = src.rearrange("(c p) t -> p c t", p=128)
    for c in range(DC):
        k.dma("sp", rs.hT[:, c, :], v[:, c, :], W=rs.hT_b[c])


def store_hT(rs, dst, is_output=True):
    k = rs.kb
    v = dst.rearrange("(c p) t -> p c t", p=128)
    for c in range(DC):
        k.dma("sp", v[:, c, :], rs.hT[:, c, :], R=rs.hT_b[c], is_output=is_output)


def rmsnorm(rs, g_dram_pc, dst=None, dst_b=None, dst_dtype_scale=1.0):
    k = rs.kb
    dst = rs.xn if dst is None else dst
    dst_b = rs.xn_b if dst_b is None else dst_b
    k.dma("sp", rs.gcol[:], g_dram_pc, W=[rs.gcol_b])
    for th in range(NTH):
        ts = slice(th * TH, (th + 1) * TH)
        ps, ps_b = rs.next_ps()
        for c in range(DC):
            sq, sq_b = rs.sq[c % 2], rs.sq_b[c % 2]
            k.op("act", lambda e: e.activation(out=sq[:], in_=rs.hT[:, c, ts], func=AF.Square), R=[rs.hT_b[c][th]], W=[sq_b])
            k.op("pe", lambda e: e.matmul(ps[:], rs.ones[:], sq[:], start=(c == 0), stop=(c == DC - 1)),
                 R=[rs.ones_b, sq_b], W=[ps_b])
        k.op("dve", lambda e: e.tensor_scalar(out=rs.rstd[:], in0=ps[:], scalar1=float(1.0 / D), scalar2=float(EPS),
                                              op0=ALU.mult, op1=ALU.add), R=[ps_b], W=[rs.rstd_b])
        k.op("act", lambda e: e.activation(out=rs.rstd[:], in_=rs.rstd[:], func=AF.Ln), R=[rs.rstd_b], W=[rs.rstd_b])
        k.op("act", lambda e: e.activation(out=rs.rstd[:], in_=rs.rstd[:], func=AF.Exp, scale=-0.5), R=[rs.rstd_b], W=[rs.rstd_b])
        for c in range(DC):
            eng = "dve"
            k.op(eng, lambda e: e.scalar_tensor_tensor(out=dst[:, c, ts], in0=rs.hT[:, c, ts], scalar=rs.gcol[:, c:c + 1],
                                                       in1=rs.rstd[:], op0=ALU.mult, op1=ALU.mult),
                 R=[rs.hT_b[c][th], rs.gcol_b, rs.rstd_b], W=[dst_b[c][th]])


def load_w(rs, view, kc, ncols):
    k = rs.kb
    wt, wb = rs.next_wA()
    k.dma("pool", wt[:, :kc, :ncols], view, W=[wb])
    return wt, wb


def ffn(rs, wg, wu, wd):
    k = rs.kb
    wgv = wg.rearrange("(c p) f -> p c f", p=128)
    wuv = wu.rearrange("(c p) f -> p c f", p=128)
    wdv = wd.rearrange("(c p) d -> p c d", p=128)
    HF = FC // 2
    for fh in range(2):
        for fp in range(HF // 2):
            f0 = (fh * HF + fp * 2) * 128
            wgt, wgb = load_w(rs, wgv[:, :, f0:f0 + 256], DC, 256)
            wut, wub = load_w(rs, wuv[:, :, f0:f0 + 256], DC, 256)
            for fc in range(2):
                fl = fp * 2 + fc
                for th in range(NTH):
                    ts = slice(th * TH, (th + 1) * TH)
                    pg, pg_b = rs.next_ps()
                    pu, pu_b = rs.next_ps()
                    for c in range(DC):
                        k.op("pe", lambda e: e.matmul(pg[:], wgt[:, c, fc * 128:(fc + 1) * 128], rs.xn[:, c, ts],
                                                      start=(c == 0), stop=(c == DC - 1)),
                             R=[wgb, rs.xn_b[c][th]], W=[pg_b])
                    for c in range(DC):
                        k.op("pe", lambda e: e.matmul(pu[:], wut[:, c, fc * 128:(fc + 1) * 128], rs.xn[:, c, ts],
                                                      start=(c == 0), stop=(c == DC - 1)),
                             R=[wub, rs.xn_b[c][th]], W=[pu_b])
                    sg, sg_b = rs.sg[rs.sg_i], rs.sg_b[rs.sg_i]
                    rs.sg_i ^= 1
                    k.op("act", lambda e: e.activation(out=sg[:], in_=pg[:], func=AF.Silu), R=[pg_b], W=[sg_b])
                    k.op("dve", lambda e: e.tensor_tensor(out=rs.act[:, fl, ts], in0=sg[:], in1=pu[:], op=ALU.mult),
                         R=[sg_b, pu_b], W=[rs.act_b[fl][th]])
        for dc in range(DC):
            wdt, wdb = rs.next_wD()
            k.dma("pool", wdt[:], wdv[:, fh * HF:(fh + 1) * HF, dc * 128:(dc + 1) * 128], W=[wdb])
            for th in range(NTH):
                ts = slice(th * TH, (th + 1) * TH)
                po, po_b = rs.next_ps()
                for fl in range(HF):
                    k.op("pe", lambda e: e.matmul(po[:], wdt[:, fl, :], rs.act[:, fl, ts], start=(fl == 0), stop=(fl == HF - 1)),
                         R=[wdb, rs.act_b[fl][th]], W=[po_b])
                k.op("dve", lambda e: e.scalar_tensor_tensor(out=rs.hT[:, dc, ts], in0=po[:], scalar=0.5, in1=rs.hT[:, dc, ts],
                                                             op0=ALU.mult, op1=ALU.add),
                     R=[po_b, rs.hT_b[dc][th]], W=[rs.hT_b[dc][th]])


def load_xn_from(rs, src):
    k = rs.kb
    v = src.rearrange("(c p) t -> p c t", p=128)
    for c in range(DC):
        k.dma("sp", rs.xn[:, c, :], v[:, c, :], W=rs.xn_b[c])


def store_xn_to(rs, dst, is_output=True):
    k = rs.kb
    v = dst.rearrange("(c p) t -> p c t", p=128)
    for c in range(DC):
        k.dma("sp", v[:, c, :], rs.xn[:, c, :], R=rs.xn_b[c], is_output=is_output)


def square_proj(rs, w, evac):
    k = rs.kb
    wv = w.rearrange("(c p) n -> p c n", p=128)
    for op_ in range(DC // 2):
        wt, wb = load_w(rs, wv[:, :, op_ * 256:(op_ + 1) * 256], DC, 256)
        for o2 in range(2):
            oc = op_ * 2 + o2
            for th in range(NTH):
                ts = slice(th * TH, (th + 1) * TH)
                ps, ps_b = rs.next_ps()
                for c in range(DC):
                    k.op("pe", lambda e: e.matmul(ps[:], wt[:, c, o2 * 128:(o2 + 1) * 128], rs.xn[:, c, ts],
                                                  start=(c == 0), stop=(c == DC - 1)),
                         R=[wb, rs.xn_b[c][th]], W=[ps_b])
                evac(oc, th, ts, ps, ps_b)


def out_proj_residual(rs, w):
    k = rs.kb

    def evac(oc, th, ts, ps, ps_b):
        k.op("dve", lambda e: e.tensor_tensor(out=rs.hT[:, oc, ts], in0=ps[:], in1=rs.hT[:, oc, ts], op=ALU.add),
             R=[ps_b, rs.hT_b[oc][th]], W=[rs.hT_b[oc][th]])
    square_proj(rs, w, evac)


class PleState:
    def __init__(self, rs, es=None):
        k = rs.kb
        self.pT = k.sb("pT_sb", [128, 2, T], BF16, es)
        self.pT_b = Buf()
        self.wp = k.sb("wp_sb", [128, 2, D], BF16, es)
        self.wp_b = Buf()
        self.tmp = k.sb("pletmp", [128, TH], F32, es)
        self.tmp_b = Buf()


def ple(rs, pst, wgate, wp, pT_dram):
    k = rs.kb
    k.dma("pool", pst.pT[:], pT_dram.rearrange("(c p) t -> p c t", p=128), W=[pst.pT_b])
    k.dma("pool", pst.wp[:], wp.rearrange("(c p) n -> p c n", p=128), W=[pst.wp_b])

    def evac(oc, th, ts, ps, ps_b):
        sg, sg_b = rs.sg[rs.sg_i], rs.sg_b[rs.sg_i]
        rs.sg_i ^= 1
        k.op("act", lambda e: e.activation(out=sg[:], in_=ps[:], func=AF.Sigmoid), R=[ps_b], W=[sg_b])
        pp, pp_b = rs.next_ps()
        for c in range(2):
            k.op("pe", lambda e: e.matmul(pp[:], pst.wp[:, c, oc * 128:(oc + 1) * 128], pst.pT[:, c, ts], start=(c == 0), stop=(c == 1)),
                 R=[pst.wp_b, pst.pT_b], W=[pp_b])
        k.op("dve", lambda e: e.tensor_tensor(out=pst.tmp[:], in0=pp[:], in1=sg[:], op=ALU.mult), R=[pp_b, sg_b], W=[pst.tmp_b])
        k.op("dve", lambda e: e.tensor_tensor(out=rs.hT[:, oc, ts], in0=pst.tmp[:], in1=rs.hT[:, oc, ts], op=ALU.add),
             R=[pst.tmp_b, rs.hT_b[oc][th]], W=[rs.hT_b[oc][th]])
    square_proj(rs, wgate, evac)


def load_y_select(rs, y_pairs, sel, sel_b):
    k = rs.kb
    for c in range(DC):
        v = y_pairs[c // 8].rearrange("(c p) t -> p c t", p=128)
        k.dma("sp", rs.xn[:, c, :], v[:, c % 8, 0:T], W=rs.xn_b[c])
        k.dma("sp", rs.act[:, c, :], v[:, c % 8, T:2 * T], W=rs.act_b[c])
    for c in range(DC):
        k.op("act", lambda e: e.activation(out=rs.xn[:, c, :], in_=rs.xn[:, c, :], func=AF.Copy, scale=sel[:, 0:1]),
             R=rs.xn_b[c] + [sel_b], W=rs.xn_b[c])
        k.op("dve", lambda e: e.scalar_tensor_tensor(out=rs.xn[:, c, :], in0=rs.act[:, c, :], scalar=sel[:, 1:2], in1=rs.xn[:, c, :],
                                                     op0=ALU.mult, op1=ALU.add),
             R=rs.act_b[c] + rs.xn_b[c] + [sel_b], W=rs.xn_b[c])


def store_xn_split(rs, dsts):
    k = rs.kb
    for c in range(DC):
        v = dsts[c // 8].rearrange("(c p) t -> p c t", p=128)
        k.dma("sp", v[:, c % 8, :], rs.xn[:, c, :], R=rs.xn_b[c])


S = 2048
NB = S // 512
NT = S // 128
NCH = S // 64
GC1 = 1.5957691216057308
GC2 = 0.044715


class PS:
    def __init__(self, kb, es=None):
        self.t = [kb.ps(f"ps{i}", [128, 512], F32, es) for i in range(8)]
        self.b = [Buf(excl=True) for _ in range(8)]
        self.i = 0
        self.ngen = 4

    def next(self):
        i = self.i
        self.i = (i + 1) % self.ngen
        return self.t[i], self.b[i]

    def fixed(self, i):
        return self.t[i], self.b[i]


def gelu_tanh(kb, out, x_ps, x_b, tmp, tmp_b, out_b, shape_slice=slice(None)):
    k = kb
    k.op("act", lambda e: e.activation(out=tmp, in_=x_ps, func=AF.Square), R=[x_b], W=[tmp_b])
    k.op("dve", lambda e: e.tensor_scalar(out=tmp, in0=tmp, scalar1=GC2, scalar2=1.0, op0=ALU.mult, op1=ALU.add), R=[tmp_b], W=[tmp_b])
    k.op("dve", lambda e: e.tensor_tensor(out=tmp, in0=tmp, in1=x_ps, op=ALU.mult), R=[tmp_b, x_b], W=[tmp_b])
    k.op("act", lambda e: e.activation(out=tmp, in_=tmp, func=AF.Sigmoid, scale=GC1), R=[tmp_b], W=[tmp_b])
    k.op("dve", lambda e: e.tensor_tensor(out=out, in0=tmp, in1=x_ps, op=ALU.mult), R=[tmp_b, x_b], W=[out_b])


def mix_even(kb, io, es=None):
    k = kb
    layer = io["layer"]
    ps = PS(kb, es)
    xn = k.sb("xnf", [128, DC, S], BF16, es)
    xn_b = [Buf() for _ in range(DC)]
    if io.get("hn_src") is not None:
        for c in range(DC):
            for r_ in range(2):
                k.dma("sp", xn[:, c, r_ * 1024:(r_ + 1) * 1024], io["hn_src"](c, r_), W=[xn_b[c]])
    else:
        hnv = io["hn"].rearrange("(c p) t -> p c t", p=128)
        for c in range(DC):
            k.dma("sp", xn[:, c, :], hnv[:, c, :], W=[xn_b[c]])
    w_in = io["w_in"].rearrange("(c p) n -> p c n", p=128)
    ones = k.sb("ones", [128, 128], BF16, es); ones_b = Buf()
    k.op("dve", lambda e: e.memset(ones[:], 1.0), W=[ones_b])
    ident = k.sb("ident", [128, 128], BF16, es); ident_b = Buf()
    k.dma("pool", ident[:], io["ident"], W=[ident_b])
    mask = k.sb("mask", [128, 128], F32, es); mask_b = Buf()
    k.dma("sp", mask[:], io["mask"], W=[mask_b])
    mask64 = k.sb("mask64", [128, 64], F32, es); mask64_b = Buf()
    k.dma("sp", mask64[:], io["mask64"], W=[mask64_b])
    rmask = k.sb("rmask", [128, S], F32, es); rmask_b = Buf()
    k.dma("sp", rmask[:], io["rmask"], W=[rmask_b])
    yrow = io.get("y_rows") or (lambda r0: io["yT"][r0:r0 + 128, :])

    wbig = [k.sb(f"wbig{i}", [128, DC, 512], BF16, es) for i in range(2)]
    wbig_b = [Buf(), Buf()]
    k.dma("pool", wbig[0][:], w_in[:, :, 512:1024], W=[wbig_b[0]])
    k.dma("pool", wbig[1][:], w_in[:, :, 0:512], W=[wbig_b[1]])
    esA = contextlib.ExitStack()
    gainv = k.sb("gainv", [128, 512], F32, esA); gainv_b = Buf()
    k.dma("sp", gainv[:], io["gainv"].partition_broadcast(128), W=[gainv_b])
    biasb = k.sb("biasb", [128, 4, 128], F32, esA); biasb_b = Buf()
    k.dma("sp", biasb[:].rearrange("p g t -> p (g t)"), io["bias"].partition_broadcast(128), W=[biasb_b])
    wsT = k.sb("wsT", [128, 4, 128], F32, esA); wsT_b = Buf()
    k.dma("sp", wsT[:], io["wsT"].rearrange("g s t -> s g t"), W=[wsT_b])
    wsm = k.sb("wsm", [128, 4, 128], BF16, esA); wsm_b = Buf()
    k.op("dve", lambda e: e.tensor_tensor(out=wsm[:], in0=wsT[:], in1=mask[:].unsqueeze(1).to_broadcast([128, 4, 128]), op=ALU.mult),
         R=[wsT_b, mask_b], W=[wsm_b])
    vg = k.sb("vg", [128, NT, 512], BF16, esA)
    vg_b = [Buf() for _ in range(NT)]
    tmpA = [k.sb(f"tmpA{i}", [128, 512], F32, esA) for i in range(2)]; tmpA_b = [Buf(), Buf()]
    gvt = [k.sb(f"gvt{i}", [128, 512], F32, esA) for i in range(2)]; gvt_b = [Buf(), Buf()]
    ssq = k.sb("ssq", [128, 4], F32, esA); ssq_b = Buf()
    for tt in range(NT):
        p_, p_b = ps.next()
        for c in range(DC):
            k.op("pe", lambda e: e.matmul(p_[:], xn[:, c, tt * 128:(tt + 1) * 128], wbig[0][:, c, :], start=(c == 0), stop=(c == DC - 1)),
                 R=[xn_b[c], wbig_b[0]], W=[p_b])
        i2 = tt % 2
        gelu_tanh(k, gvt[i2][:], p_[:], p_b, tmpA[i2][:], tmpA_b[i2], gvt_b[i2])
        k.op("dve", lambda e: e.tensor_tensor(out=tmpA[i2][:], in0=gvt[i2][:], in1=gvt[i2][:], op=ALU.mult), R=[gvt_b[i2]], W=[tmpA_b[i2]])
        k.op("dve", lambda e: e.tensor_reduce(out=ssq[:], in_=tmpA[i2][:].rearrange("p (g c) -> p g c", g=4), axis=AX.X, op=ALU.add),
             R=[tmpA_b[i2]], W=[ssq_b])
        k.op("dve", lambda e: e.tensor_scalar(out=ssq[:], in0=ssq[:], scalar1=1.0 / 128, scalar2=1e-6, op0=ALU.mult, op1=ALU.add), R=[ssq_b], W=[ssq_b])
        k.op("act", lambda e: e.activation(out=ssq[:], in_=ssq[:], func=AF.Sqrt), R=[ssq_b], W=[ssq_b])
        k.op("dve", lambda e: e.reciprocal(out=ssq[:], in_=ssq[:]), R=[ssq_b], W=[ssq_b])
        k.op("dve", lambda e: e.tensor_tensor(out=gvt[i2][:].rearrange("p (g c) -> p g c", g=4), in0=gvt[i2][:].rearrange("p (g c) -> p g c", g=4),
                                              in1=ssq[:].unsqueeze(2).to_broadcast([128, 4, 128]), op=ALU.mult), R=[gvt_b[i2], ssq_b], W=[gvt_b[i2]])
        k.op("dve", lambda e: e.tensor_tensor(out=vg[:, tt, :], in0=gvt[i2][:], in1=gainv[:], op=ALU.mult), R=[gvt_b[i2], gainv_b], W=[vg_b[tt]])
    yst = [k.sb(f"yst{i}", [128, 512], BF16, esA) for i in range(2)]; yst_b = [Buf(), Buf()]
    yi = 0
    for g in range(4):
        for blk in range(NB):
            bs = slice(blk * 512, (blk + 1) * 512)
            pu, pu_b = ps.next()
            for c in range(DC):
                k.op("pe", lambda e: e.matmul(pu[:], wbig[1][:, c, g * 128:(g + 1) * 128], xn[:, c, bs], start=(c == 0), stop=(c == DC - 1)),
                     R=[xn_b[c], wbig_b[1]], W=[pu_b])
            pss, pss_b = ps.next()
            for q in range(4):
                tt = blk * 4 + q
                k.op("pe", lambda e: e.matmul(pss[:, q * 128:(q + 1) * 128], vg[:, tt, g * 128:(g + 1) * 128], wsm[:, g, :], start=True, stop=True),
                     R=[vg_b[tt], wsm_b], W=[pss_b])
            i2 = yi % 2
            gelu_tanh(k, gvt[i2][:], pu[:], pu_b, tmpA[i2][:], tmpA_b[i2], gvt_b[i2])
            k.op("dve", lambda e: e.tensor_tensor(out=tmpA[i2][:].rearrange("p (q t) -> p q t", q=4), in0=pss[:].rearrange("p (q t) -> p q t", q=4),
                                                  in1=biasb[:, g, :].unsqueeze(1).to_broadcast([128, 4, 128]), op=ALU.add),
                 R=[pss_b, biasb_b], W=[tmpA_b[i2]])
            k.op("dve", lambda e: e.tensor_tensor(out=yst[i2][:], in0=gvt[i2][:], in1=tmpA[i2][:], op=ALU.mult), R=[gvt_b[i2], tmpA_b[i2]], W=[yst_b[i2]])
            k.dma("sp", yrow(g * 128)[:, bs], yst[i2][:], R=[yst_b[i2]], is_output=True)
            yi += 1

    k.barrier()
    esA.close()
    lbl = k.sb("lbl", [128, 4, 4], F32, es); lbl_b = Buf()
    k.dma("sp", lbl[:], io["lbl"], W=[lbl_b])
    k.op("act", lambda e: e.activation(out=lbl[:], in_=lbl[:], func=AF.Exp), R=[lbl_b], W=[lbl_b])
    tot = k.sb("tot", [128, 4], F32, es); tot_b = Buf()
    lb = k.sb("lb", [128, 4], F32, es); lb_b = Buf()
    oml = k.sb("oml", [128, 4], F32, es); oml_b = Buf()
    k.op("dve", lambda e: e.tensor_reduce(out=tot[:], in_=lbl[:], axis=AX.X, op=ALU.add), R=[lbl_b], W=[tot_b])
    k.op("dve", lambda e: e.reciprocal(out=tot[:], in_=tot[:]), R=[tot_b], W=[tot_b])
    if layer == 0:
        k.op("dve", lambda e: e.memset(lb[:], 0.0), W=[lb_b])
    else:
        k.op("dve", lambda e: e.tensor_reduce(out=lb[:], in_=lbl[:, :, 1:layer + 1], axis=AX.X, op=ALU.add), R=[lbl_b], W=[lb_b])
        k.op("dve", lambda e: e.tensor_tensor(out=lb[:], in0=lb[:], in1=tot[:], op=ALU.mult), R=[lb_b, tot_b], W=[lb_b])
    k.op("dve", lambda e: e.tensor_scalar(out=oml[:], in0=lb[:], scalar1=-1.0, scalar2=1.0, op0=ALU.mult, op1=ALU.add), R=[lb_b], W=[oml_b])
    onorm = k.sb("onorm", [128, 4], F32, es); onorm_b = Buf()
    k.dma("sp", onorm[:], io["onorm"], W=[onorm_b])

    def big(name, dt=F32):
        return k.sb(name, [128, S], dt, es), Buf()
    B1, B1_b = big("B1"); B2, B2_b = big("B2"); B3, B3_b = big("B3"); B4, B4_b = big("B4"); G1, G1_b = big("G1")
    Q1, Q1_b = big("Q1", BF16); Q2, Q2_b = big("Q2", BF16); K1, K1_b = big("K1", BF16); K2, K2_b = big("K2", BF16); I1, I1_b = big("I1", BF16)
    Vt = k.sb("Vt", [128, NT, 128], BF16, es); Vt_b = Buf()
    Kt = k.sb("Kt", [128, NT, 128], BF16, es); Kt_b = Buf()
    att = k.sb("att", [128, NT, 64], BF16, es); att_b = Buf()
    oT, oT_b = big("oT")
    cmid = k.sb("cmid", [128, NCH], F32, es); cmid_b = Buf()
    cend = k.sb("cend", [128, NCH], F32, es); cend_b = Buf()
    decL = k.sb("decL", [128, NCH], F32, es); decL_b = Buf()
    Sf = k.sb("Sf", [128, 128], F32, es); Sf_b = Buf()
    Sb = k.sb("Sb", [128, 128], BF16, es); Sb_b = Buf()
    sqb = [k.sb(f"sqb{i}", [128, 512], BF16, es) for i in range(2)]; sqb_b = [Buf(), Buf()]
    rst = k.sb("rst", [128, 512], F32, es); rst_b = Buf()
    ybig, ybig_b = big("ybig", BF16)

    def proj(col0, evac):
        for blk in range(NB):
            bs = slice(blk * 512, (blk + 1) * 512)
            p_, p_b = ps.next()
            for c in range(DC):
                k.op("pe", lambda e: e.matmul(p_[:], wh[:, c, col0:col0 + 128], xn[:, c, bs], start=(c == 0), stop=(c == DC - 1)),
                     R=[xn_b[c], wh_b], W=[p_b])
            evac(blk, bs, p_, p_b)

    c3 = lambda ap: ap.rearrange("p (n t) -> p n t", t=64)
    for hd in range(4):
        wh, wh_b = wbig[hd % 2], wbig_b[hd % 2]
        k.dma("pool", wh[:], w_in[:, :, 1024 + hd * 512:1024 + (hd + 1) * 512], W=[wh_b])
        proj(0, lambda blk, bs, p_, p_b: k.op("act", lambda e: e.activation(out=B4[:, bs], in_=p_[:], func=AF.Silu), R=[p_b], W=[B4_b]))
        proj(128, lambda blk, bs, p_, p_b: k.op("act", lambda e: e.activation(out=B1[:, bs], in_=p_[:], func=AF.Sigmoid), R=[p_b], W=[B1_b]))
        proj(256, lambda blk, bs, p_, p_b: k.op("act", lambda e: e.activation(out=I1[:, bs], in_=p_[:], func=AF.Copy), R=[p_b], W=[I1_b]))
        proj(384, lambda blk, bs, p_, p_b: k.op("act", lambda e: e.activation(out=G1[:, bs], in_=p_[:], func=AF.Silu), R=[p_b], W=[G1_b]))
        k.op("dve", lambda e: e.tensor_scalar(out=B1[:], in0=B1[:], scalar1=oml[:, hd:hd + 1], scalar2=lb[:, hd:hd + 1], op0=ALU.mult, op1=ALU.add),
             R=[B1_b, oml_b, lb_b], W=[B1_b])
        k.op("dve", lambda e: e.tensor_scalar(out=B3[:], in0=B1[:], scalar1=-1.0, scalar2=1.0, op0=ALU.mult, op1=ALU.add), R=[B1_b], W=[B3_b])
        k.op("dve", lambda e: e.tensor_scalar_max(out=B1[:], in0=B1[:], scalar1=1e-30), R=[B1_b], W=[B1_b])
        k.op("act", lambda e: e.activation(out=B1[:], in_=B1[:], func=AF.Ln), R=[B1_b], W=[B1_b])
        k.op("dve", lambda e: e.tensor_tensor_scan(out=B2[:], data0=rmask[:], data1=B1[:], initial=0.0, op0=ALU.mult, op1=ALU.add),
             R=[rmask_b, B1_b], W=[B2_b])
        k.op("dve", lambda e: e.tensor_copy(out=cmid[:], in_=c3(B2[:])[:, :, 31]), R=[B2_b], W=[cmid_b])
        k.op("dve", lambda e: e.tensor_copy(out=cend[:], in_=c3(B2[:])[:, :, 63]), R=[B2_b], W=[cend_b])
        k.op("act", lambda e: e.activation(out=decL[:], in_=cend[:], func=AF.Exp), R=[cend_b], W=[decL_b])
        k.op("act", lambda e: e.activation(out=B1[:], in_=B2[:], func=AF.Exp), R=[B2_b], W=[B1_b])
        k.op("dve", lambda e: e.tensor_tensor(out=Q1[:], in0=B4[:], in1=B1[:], op=ALU.mult), R=[B4_b, B1_b], W=[Q1_b])
        k.op("dve", lambda e: e.tensor_tensor(out=c3(B1[:]), in0=c3(B2[:]), in1=cmid[:].unsqueeze(2).to_broadcast([128, NCH, 64]), op=ALU.subtract),
             R=[B2_b, cmid_b], W=[B1_b])
        k.op("act", lambda e: e.activation(out=oT[:], in_=B1[:], func=AF.Exp), R=[B1_b], W=[oT_b])
        k.op("dve", lambda e: e.tensor_tensor(out=Q2[:], in0=B4[:], in1=oT[:], op=ALU.mult), R=[B4_b, oT_b], W=[Q2_b])
        k.op("act", lambda e: e.activation(out=B1[:], in_=B1[:], func=AF.Exp, scale=-1.0), R=[B1_b], W=[B1_b])
        k.op("dve", lambda e: e.tensor_tensor(out=K1[:], in0=B3[:], in1=B1[:], op=ALU.mult), R=[B3_b, B1_b], W=[K1_b])
        k.op("dve", lambda e: e.tensor_tensor(out=c3(B1[:]), in0=cend[:].unsqueeze(2).to_broadcast([128, NCH, 64]), in1=c3(B2[:]), op=ALU.subtract),
             R=[B2_b, cend_b], W=[B1_b])
        k.op("act", lambda e: e.activation(out=B1[:], in_=B1[:], func=AF.Exp), R=[B1_b], W=[B1_b])
        k.op("dve", lambda e: e.tensor_tensor(out=K2[:], in0=B3[:], in1=B1[:], op=ALU.mult), R=[B3_b, B1_b], W=[K2_b])
        for src, src_b, dst, dst_b in ((I1, I1_b, Vt, Vt_b), (K2, K2_b, Kt, Kt_b)):
            for q4 in range(NT // 4):
                p_, p_b = ps.next()
                pbf = p_[:].bitcast(BF16)
                for q in range(4):
                    tt = q4 * 4 + q
                    k.op("pe", lambda e: e.transpose(pbf[:, q * 128:(q + 1) * 128], src[:, tt * 128:(tt + 1) * 128], ident[:]),
                         R=[src_b, ident_b], W=[p_b])
                k.op("act", lambda e: e.activation(out=dst[:, q4 * 4:(q4 + 1) * 4, :], in_=pbf[:, 0:512].rearrange("p (q c) -> p q c", q=4), func=AF.Copy),
                     R=[p_b], W=[dst_b])
        for q4 in range(NT // 4):
            p_, p_b = ps.next()
            for q in range(4):
                tt = q4 * 4 + q
                k.op("pe", lambda e: e.matmul(p_[:, q * 128:(q + 1) * 128], K1[:, tt * 128:(tt + 1) * 128], Q2[:, tt * 128:(tt + 1) * 128], start=True, stop=True),
                     R=[K1_b, Q2_b], W=[p_b])
            pv = p_[:].rearrange("p (q t) -> p q t", q=4)
            for half in range(2):
                prt = slice(half * 64, (half + 1) * 64)
                k.op("dve", lambda e: e.tensor_tensor(out=att[prt, q4 * 4:(q4 + 1) * 4, :], in0=pv[prt, :, half * 64:(half + 1) * 64],
                                                      in1=mask64[prt, :].unsqueeze(1).to_broadcast([64, 4, 64]), op=ALU.mult),
                     R=[p_b, mask64_b], W=[att_b])
        k.op("dve", lambda e: e.memset(Sf[:], 0.0), W=[Sf_b])
        k.op("dve", lambda e: e.memset(Sb[:], 0.0), W=[Sb_b])
        psS = [None] * NCH

        def emit_mmS(n):
            bank, bank_b = ps.fixed(6 + n % 2)
            psS[n] = (bank, bank_b)
            tt, prt = n // 2, slice((n % 2) * 64, (n % 2) * 64 + 64)
            k.op("pe", lambda e: e.matmul(bank[:, 0:128], Kt[prt, tt, :], Vt[prt, tt, :], start=True, stop=True), R=[Kt_b, Vt_b], W=[bank_b])
        emit_mmS(0)
        po = po_b = None
        for n in range(NCH):
            tt, prt = n // 2, slice((n % 2) * 64, (n % 2) * 64 + 64)
            if n % 8 == 0:
                po, po_b = ps.fixed(4 + (n // 8) % 2)
            osl = po[:, (n % 8) * 64:(n % 8 + 1) * 64]
            k.op("pe", lambda e: e.matmul(osl, Vt[prt, tt, :], att[prt, tt, :], start=True, stop=False), R=[Vt_b, att_b], W=[po_b])
            if n + 1 < NCH:
                emit_mmS(n + 1)
            k.op("pe", lambda e: e.matmul(osl, Sb[:], Q1[:, n * 64:(n + 1) * 64], start=False, stop=True), R=[Sb_b, Q1_b], W=[po_b])
            bank, bank_b = psS[n]
            k.op("dve", lambda e: e.scalar_tensor_tensor(out=Sf[:], in0=Sf[:], scalar=decL[:, n:n + 1], in1=bank[:, 0:128], op0=ALU.mult, op1=ALU.add),
                 R=[Sf_b, decL_b, bank_b], W=[Sf_b])
            k.op("act", lambda e: e.activation(out=Sb[:], in_=Sf[:], func=AF.Copy), R=[Sf_b], W=[Sb_b])
            if n % 8 == 7:
                blk = n // 8
                k.op("act", lambda e: e.activation(out=oT[:, blk * 512:(blk + 1) * 512], in_=po[:], func=AF.Copy), R=[po_b], W=[oT_b])
        for blk in range(NB):
            bs = slice(blk * 512, (blk + 1) * 512)
            i2 = blk % 2
            k.op("act", lambda e: e.activation(out=sqb[i2][:], in_=oT[:, bs], func=AF.Square), R=[oT_b], W=[sqb_b[i2]])
            p_, p_b = ps.next()
            k.op("pe", lambda e: e.matmul(p_[:], ones[:], sqb[i2][:], start=True, stop=True), R=[ones_b, sqb_b[i2]], W=[p_b])
            k.op("dve", lambda e: e.tensor_scalar(out=rst[:], in0=p_[:], scalar1=1.0 / 128, scalar2=1e-6, op0=ALU.mult, op1=ALU.add), R=[p_b], W=[rst_b])
            k.op("act", lambda e: e.activation(out=rst[:], in_=rst[:], func=AF.Ln), R=[rst_b], W=[rst_b])
            k.op("act", lambda e: e.activation(out=rst[:], in_=rst[:], func=AF.Exp, scale=-0.5), R=[rst_b], W=[rst_b])
            k.op("dve", lambda e: e.scalar_tensor_tensor(out=rst[:], in0=oT[:, bs], scalar=onorm[:, hd:hd + 1], in1=rst[:], op0=ALU.mult, op1=ALU.mult),
                 R=[oT_b, onorm_b, rst_b], W=[rst_b])
            k.op("dve", lambda e: e.tensor_tensor(out=ybig[:, bs], in0=rst[:], in1=G1[:, bs], op=ALU.mult), R=[rst_b, G1_b], W=[ybig_b])
        k.dma("sp", yrow(512 + hd * 128), ybig[:], R=[ybig_b], is_output=True)


L = 64
CPB = 8
WSCALE = -0.6065306597126334
GN_EPS = 64e-5
BT = 512


def mix_odd(kb, io, es=None):
    k = kb
    first = io["first"]
    DC_ = 16
    S_ = 2048
    NBLK = S_ // BT
    pst = [k.ps(f"ps{i}", [128, 512], F32, es) for i in range(8)]
    psb = [Buf(excl=True) for _ in range(8)]
    rot = [0]

    def ps_next():
        i = rot[0]
        rot[0] = (i + 1) % 4
        return pst[i], psb[i]

    def cst(name, shape, dt, src, q="sp"):
        t = k.sb(name, shape, dt, es)
        b = Buf()
        k.dma(q, t[:], src, W=[b])
        return t, b
    ident, ident_b = cst("ident", [128, 128], BF16, io["ident"], "pool")
    bones, bones_b = cst("bones", [128, 128], BF16, io["bones"], "pool")
    bones1, bones1_b = cst("bones1", [128, 128], BF16, io["bones1"], "pool")
    cmask, cmask_b = cst("cmask", [64, 5, 64], F32, io["cmask"])
    rmask, rmask_b = cst("rmask", [128, BT], F32, io["rmask"][:, 0:BT])
    mixm, mixm_b = cst("mixm", [128, 6, DC_], F32, io["mixpc"])
    cols, cols_b = cst("cols", [128, 8, 8], F32, io["cols"])
    mixo = k.sb("mixo", [128, 6, DC_], F32, es); mixo_b = Buf()
    k.op("dve", lambda e: e.tensor_scalar(out=mixo[:], in0=mixm[:], scalar1=-1.0, scalar2=1.0, op0=ALU.mult, op1=ALU.add), R=[mixm_b], W=[mixo_b])
    omka = k.sb("omka", [128, 8], F32, es); omka_b = Buf()
    k.op("dve", lambda e: e.tensor_scalar(out=omka[:], in0=cols[:, :, 3], scalar1=-1.0, scalar2=1.0, op0=ALU.mult, op1=ALU.add), R=[cols_b], W=[omka_b])
    w2t, w2t_b = cst("w2t", [96, 1024], BF16, io["w2"], "pool")
    a2t, a2t_b = cst("a2t", [96, 1024], BF16, io["a2"], "pool")
    g2t, g2t_b = cst("g2t", [128, 2, 1024], BF16, io["g2"].rearrange("(c p) n -> p c n", p=128), "pool")
    if not first:
        v2t, v2t_b = cst("v2t", [64, 1024], BF16, io["v2"], "pool")
    wv_ = lambda name: io[name].rearrange("(c p) n -> p c n", p=128)
    wr, wk, wv = wv_("wr"), wv_("wk"), wv_("wv")
    yrow = io.get("y_rows") or (lambda r0: io["yT"][r0:r0 + 128, :])
    c3 = lambda ap: ap.rearrange("p (n t) -> p n t", t=L)

    def colv(pr, vi):
        return cols[:, pr, vi:vi + 1]

    xe = k.sb("xe", [128, DC_, BT + 8], BF16, es)
    xe_b = [Buf() for _ in range(DC_)]
    xcar = k.sb("xcar", [128, DC_, 1], BF16, es); xcar_b = Buf()
    k.op("pool", lambda e: e.memset(xcar[:], 0.0), W=[xcar_b])
    hw = k.sb("hw", [96, BT], BF16, es); hw_b = Buf()
    ha = k.sb("ha", [96, BT], BF16, es); ha_b = Buf()
    hg = k.sb("hg", [128, 2, BT], BF16, es); hg_b = Buf()
    if not first:
        hv = k.sb("hv", [64, BT], BF16, es); hv_b = Buf()
    Mf = k.sb("Mf", [128, 8, 128], F32, es); Mf_b = [Buf() for _ in range(8)]
    Mb = k.sb("Mb", [128, 8, 128], BF16, es); Mb_b = [Buf() for _ in range(8)]
    k.op("dve", lambda e: e.memset(Mf[:], 0.0), W=Mf_b)
    k.op("pool", lambda e: e.memset(Mb[:], 0.0), W=Mb_b)

    class TS:
        pass

    def make_ts(sfx, py_bank, pch_bank):
        t = TS()
        t.sfx = sfx
        t.py, t.py_b = pst[py_bank], psb[py_bank]
        t.pch, t.pch_b = pst[pch_bank], psb[pch_bank]

        def big(name, dt=F32):
            return k.sb(name + sfx, [128, BT], dt, es), Buf()
        t.V, t.V_b = big("V"); t.K, t.K_b = big("K"); t.A, t.A_b = big("A"); t.KK, t.KK_b = big("KK"); t.CUM, t.CUM_b = big("CUM")
        t.T1, t.T1_b = big("T1"); t.T2, t.T2_b = big("T2"); t.Vb, t.Vb_b = big("Vb", BF16); t.G, t.G_b = big("G", BF16)
        t.AR = k.sb("AR" + sfx, [128, CPB, 2, L], BF16, es); t.AR_b = Buf()
        t.BK = k.sb("BK" + sfx, [128, CPB, 2, L], BF16, es); t.BK_b = Buf()
        t.BhKh = k.sb("BhKh" + sfx, [128, 2, BT], BF16, es); t.BhKh_b = Buf()
        t.cend = k.sb("cend" + sfx, [128, CPB], F32, es); t.cend_b = Buf()
        t.decL = k.sb("decL" + sfx, [128, CPB], F32, es); t.decL_b = Buf()
        t.sqb = k.sb("sqb" + sfx, [128, 512], BF16, es); t.sqb_b = Buf()
        t.pads = []
        for nm in ("Vpad", "Bpad", "Kpad"):
            tt = k.sb(nm + sfx, [64, CPB, 2, 128], BF16, es); bb = Buf()
            k.op("pool", lambda e: e.memset(tt[:], 0.0), W=[bb])
            t.pads.append((tt, bb))
        t.sc = {}
        for nm in ("N", "NT", "N2", "NT2", "X", "X2", "AAK", "PBR", "PKR"):
            t.sc[nm] = (k.sb("sc" + nm + sfx, [64, 2, CPB, L], BF16, es), Buf())
        t.W0 = k.sb("W0" + sfx, [64, 2, L], BF16, es); t.W0_b = Buf()
        t.Upad = k.sb("Upad" + sfx, [64, 384], BF16, es); t.Upad_b = Buf()
        k.op("pool", lambda e: e.memset(t.Upad[:], 0.0), W=[t.Upad_b])
        t.Upad_w = t.Upad[:, 0:384].rearrange("p (h w) -> p h w", w=192)[:, :, 0:64]
        t.Upad_r = t.Upad[:, 0:256].rearrange("p (h w) -> p h w", w=128)
        t.yf = k.sb("yf" + sfx, [128, 512], F32, es); t.yf_b = Buf()
        t.ysb = k.sb("ysb" + sfx, [128, 512], BF16, es); t.ysb_b = Buf()
        t.ysq = k.sb("ysq" + sfx, [128, 512], BF16, es); t.ysq_b = Buf()
        t.t3 = k.sb("t3" + sfx, [128, 512], F32, es); t.t3_b = Buf()
        t.yo = k.sb("yo" + sfx, [128, 512], BF16, es); t.yo_b = Buf()
        t.stage = k.sb("stage" + sfx, [128, DC_, 128], F32, es); t.stage_b = Buf()
        t.Wa = k.sb("Wa" + sfx, [128, DC_, 128], BF16, es); t.Wa_b = Buf()
        t.Wb = k.sb("Wb" + sfx, [128, DC_, 128], BF16, es); t.Wb_b = Buf()
        return t

    def load_mixed(t, wview, ncols, mi):
        k.dma("sp", t.stage[:, :, :ncols], wview, W=[t.stage_b])
        k.op("pool", lambda e: e.tensor_tensor(out=t.Wa[:, :, :ncols], in0=t.stage[:, :, :ncols],
                                               in1=mixo[:, mi, :].unsqueeze(2).to_broadcast([128, DC_, ncols]), op=ALU.mult),
             R=[t.stage_b, mixo_b], W=[t.Wa_b])
        k.op("pool", lambda e: e.tensor_tensor(out=t.Wb[:, :, :ncols], in0=t.stage[:, :, :ncols],
                                               in1=mixm[:, mi, :].unsqueeze(2).to_broadcast([128, DC_, ncols]), op=ALU.mult),
             R=[t.stage_b, mixm_b], W=[t.Wb_b])

    def mixed_proj(t, ncols, evac):
        p_, p_b = ps_next()
        for c in range(DC_):
            k.op("pe", lambda e: e.matmul(p_[:ncols, :], t.Wa[:, c, 0:ncols], xe[:, c, 8:8 + BT], start=(c == 0), stop=False),
                 R=[t.Wa_b, xe_b[c]], W=[p_b])
        for c in range(DC_):
            k.op("pe", lambda e: e.matmul(p_[:ncols, :], t.Wb[:, c, 0:ncols], xe[:, c, 7:7 + BT], start=False, stop=(c == DC_ - 1)),
                 R=[t.Wb_b, xe_b[c]], W=[p_b])
        evac(p_, p_b)

    tsA = make_ts("A", 4, 6)
    tsB = make_ts("B", 5, 7)

    def unit(t, pr, blk):
        pc = slice(pr * 128, (pr + 1) * 128)
        tb = blk * BT
        V, V_b, K, K_b, A, A_b, KK, KK_b, CUM, CUM_b, T1, T1_b, T2, T2_b = t.V, t.V_b, t.K, t.K_b, t.A, t.A_b, t.KK, t.KK_b, t.CUM, t.CUM_b, t.T1, t.T1_b, t.T2, t.T2_b
        Vb, Vb_b, G, G_b, AR, AR_b, BK, BK_b, BhKh, BhKh_b = t.Vb, t.Vb_b, t.G, t.G_b, t.AR, t.AR_b, t.BK, t.BK_b, t.BhKh, t.BhKh_b
        cend, cend_b, decL, decL_b, sqb, sqb_b = t.cend, t.cend_b, t.decL, t.decL_b, t.sqb, t.sqb_b
        load_mixed(t, wv[:, :, pc], 128, 3)
        yield
        mixed_proj(t, 128, lambda p_, p_b: k.op("act", lambda e: e.activation(out=V[:], in_=p_[:], func=AF.Copy), R=[p_b], W=[V_b]))
        yield
        if first:
            k.dma("sp", io["vf_out"][pc, tb:tb + BT], V[:], R=[V_b], is_output=True)
        else:
            k.dma("sp", T1[:], io["vf_in"][pc, tb:tb + BT], W=[T1_b])
            p_, p_b = ps_next()
            k.op("pe", lambda e: e.matmul(p_[:], v2t[:, pc], hv[:], start=True, stop=True), R=[v2t_b, hv_b], W=[p_b])
            k.op("act", lambda e: e.activation(out=T2[:], in_=p_[:], func=AF.Sigmoid, bias=colv(pr, 7)), R=[p_b, cols_b], W=[T2_b])
            k.op("dve", lambda e: e.tensor_tensor(out=T1[:], in0=T1[:], in1=V[:], op=ALU.subtract), R=[T1_b, V_b], W=[T1_b])
            k.op("dve", lambda e: e.tensor_tensor(out=T1[:], in0=T1[:], in1=T2[:], op=ALU.mult), R=[T1_b, T2_b], W=[T1_b])
            k.op("dve", lambda e: e.tensor_tensor(out=V[:], in0=V[:], in1=T1[:], op=ALU.add), R=[T1_b, V_b], W=[V_b])
        k.op("act", lambda e: e.activation(out=Vb[:], in_=V[:], func=AF.Copy), R=[V_b], W=[Vb_b])
        load_mixed(t, wk[:, :, pc], 128, 2)
        yield
        mixed_proj(t, 128, lambda p_, p_b: k.op("act", lambda e: e.activation(out=K[:], in_=p_[:], func=AF.Copy), R=[p_b], W=[K_b]))
        yield
        p_, p_b = ps_next()
        k.op("pe", lambda e: e.matmul(p_[:], a2t[:, pc], ha[:], start=True, stop=True), R=[a2t_b, ha_b], W=[p_b])
        k.op("act", lambda e: e.activation(out=A[:], in_=p_[:], func=AF.Sigmoid, bias=colv(pr, 1)), R=[p_b, cols_b], W=[A_b])
        p_, p_b = ps_next()
        k.op("pe", lambda e: e.matmul(p_[:], w2t[:, pc], hw[:], start=True, stop=True), R=[w2t_b, hw_b], W=[p_b])
        k.op("act", lambda e: e.activation(out=T1[:], in_=p_[:], func=AF.Sigmoid, bias=colv(pr, 0)), R=[p_b, cols_b], W=[T1_b])
        p_, p_b = ps_next()
        for gc in range(2):
            k.op("pe", lambda e: e.matmul(p_[:], g2t[:, gc, pc], hg[:, gc, :], start=(gc == 0), stop=(gc == 1)), R=[g2t_b, hg_b], W=[p_b])
        k.op("act", lambda e: e.activation(out=G[:], in_=p_[:], func=AF.Copy), R=[p_b], W=[G_b])
        yield
        k.op("dve", lambda e: e.tensor_scalar(out=T1[:], in0=T1[:], scalar1=WSCALE, scalar2=None, op0=ALU.mult), R=[T1_b], W=[T1_b])
        k.op("dve", lambda e: e.tensor_tensor_scan(out=CUM[:], data0=rmask[:], data1=T1[:], initial=0.0, op0=ALU.mult, op1=ALU.add),
             R=[rmask_b, T1_b], W=[CUM_b])
        k.op("dve", lambda e: e.tensor_copy(out=cend[:], in_=c3(CUM[:])[:, :, L - 1]), R=[CUM_b], W=[cend_b])
        k.op("act", lambda e: e.activation(out=decL[:], in_=cend[:], func=AF.Exp), R=[cend_b], W=[decL_b])
        k.op("dve", lambda e: e.tensor_scalar(out=KK[:], in0=K[:], scalar1=colv(pr, 2), scalar2=None, op0=ALU.mult), R=[K_b, cols_b], W=[KK_b])
        k.op("act", lambda e: e.activation(out=sqb[:], in_=KK[:], func=AF.Square), R=[KK_b], W=[sqb_b])
        p_, p_b = ps_next()
        k.op("pe", lambda e: e.matmul(p_[:], bones1[:], sqb[:], start=True, stop=True), R=[bones1_b, sqb_b], W=[p_b])
        k.op("dve", lambda e: e.tensor_scalar_max(out=T2[:], in0=p_[:], scalar1=1e-24), R=[p_b], W=[T2_b])
        yield
        k.op("act", lambda e: e.activation(out=T2[:], in_=T2[:], func=AF.Ln), R=[T2_b], W=[T2_b])
        k.op("act", lambda e: e.activation(out=T2[:], in_=T2[:], func=AF.Exp, scale=-0.5), R=[T2_b], W=[T2_b])
        k.op("dve", lambda e: e.tensor_tensor(out=KK[:], in0=KK[:], in1=T2[:], op=ALU.mult), R=[KK_b, T2_b], W=[KK_b])
        k.op("dve", lambda e: e.tensor_scalar(out=T2[:], in0=A[:], scalar1=colv(pr, 3), scalar2=omka[:, pr:pr + 1], op0=ALU.mult, op1=ALU.add),
             R=[A_b, cols_b, omka_b], W=[T2_b])
        k.op("dve", lambda e: e.tensor_tensor(out=K[:], in0=K[:], in1=T2[:], op=ALU.mult), R=[K_b, T2_b], W=[K_b])
        yield
        k.op("dve", lambda e: e.tensor_tensor(out=T2[:], in0=CUM[:], in1=T1[:], op=ALU.subtract), R=[CUM_b, T1_b], W=[T2_b])
        k.op("act", lambda e: e.activation(out=T2[:], in_=T2[:], func=AF.Exp), R=[T2_b], W=[T2_b])
        k.op("dve", lambda e: e.scalar_tensor_tensor(out=AR[:, :, 0, :], in0=c3(KK[:]), scalar=-1.0, in1=c3(T2[:]), op0=ALU.mult, op1=ALU.mult),
             R=[KK_b, T2_b], W=[AR_b])
        k.op("dve", lambda e: e.tensor_tensor(out=T1[:], in0=KK[:], in1=A[:], op=ALU.mult), R=[KK_b, A_b], W=[T1_b])
        k.op("act", lambda e: e.activation(out=T2[:], in_=CUM[:], func=AF.Exp, scale=-1.0), R=[CUM_b], W=[T2_b])
        yield
        k.op("dve", lambda e: e.tensor_tensor(out=BK[:, :, 0, :], in0=c3(T1[:]), in1=c3(T2[:]), op=ALU.mult), R=[T1_b, T2_b], W=[BK_b])
        k.op("dve", lambda e: e.tensor_tensor(out=BK[:, :, 1, :], in0=c3(K[:]), in1=c3(T2[:]), op=ALU.mult), R=[K_b, T2_b], W=[BK_b])
        k.op("dve", lambda e: e.tensor_tensor(out=c3(T2[:]), in0=cend[:].unsqueeze(2).to_broadcast([128, CPB, L]), in1=c3(CUM[:]), op=ALU.subtract),
             R=[cend_b, CUM_b], W=[T2_b])
        k.op("act", lambda e: e.activation(out=T2[:], in_=T2[:], func=AF.Exp), R=[T2_b], W=[T2_b])
        yield
        k.op("dve", lambda e: e.tensor_tensor(out=BhKh[:, 0, :], in0=T1[:], in1=T2[:], op=ALU.mult), R=[T1_b, T2_b], W=[BhKh_b])
        k.op("dve", lambda e: e.tensor_tensor(out=BhKh[:, 1, :], in0=K[:], in1=T2[:], op=ALU.mult), R=[K_b, T2_b], W=[BhKh_b])
        k.op("act", lambda e: e.activation(out=T2[:], in_=CUM[:], func=AF.Exp), R=[CUM_b], W=[T2_b])
        load_mixed(t, wr[:, :, pc], 128, 0)
        yield

        def ev(p_, p_b):
            k.op("dve", lambda e: e.tensor_tensor(out=AR[:, :, 1, :], in0=c3(p_[:]), in1=c3(T2[:]), op=ALU.mult), R=[p_b, T2_b], W=[AR_b])
            k.op("dve", lambda e: e.scalar_tensor_tensor(out=sqb[:], in0=p_[:], scalar=colv(pr, 4), in1=K[:], op0=ALU.mult, op1=ALU.mult),
                 R=[p_b, cols_b, K_b], W=[sqb_b])
            p2, p2_b = ps_next()
            k.op("pe", lambda e: e.matmul(p2[:], bones1[:], sqb[:], start=True, stop=True), R=[bones1_b, sqb_b], W=[p2_b])
            k.op("dve", lambda e: e.tensor_tensor(out=T1[:], in0=p2[:], in1=V[:], op=ALU.mult), R=[p2_b, V_b], W=[T1_b])
        mixed_proj(t, 128, ev)
        yield
        (Vpad, Vpad_b), (Bpad, Bpad_b), (Kpad, Kpad_b) = t.pads
        for src, dst, dst_b, src_b in ((Vb[:, :], Vpad, Vpad_b, Vb_b), (BhKh[:, 0, :], Bpad, Bpad_b, BhKh_b), (BhKh[:, 1, :], Kpad, Kpad_b, BhKh_b)):
            p_, p_b = ps_next()
            pbf = p_[:].bitcast(BF16)
            for n in range(CPB):
                k.op("pe", lambda e: e.transpose(pbf[0:64, n * 128:(n + 1) * 128], src[:, n * L:(n + 1) * L], ident[:]), R=[src_b, ident_b], W=[p_b])
            pv = pbf[0:64, :].rearrange("p (n c) -> p n c", c=128)
            k.op("act", lambda e: e.activation(out=dst[:, :, 0, 0:64], in_=pv[:, :, 0:64], func=AF.Copy), R=[p_b], W=[dst_b])
            k.op("act", lambda e: e.activation(out=dst[:, :, 1, 64:128], in_=pv[:, :, 64:128], func=AF.Copy), R=[p_b], W=[dst_b])
            yield
        sc = t.sc
        for h in range(2):
            ph = slice(h * 64, (h + 1) * 64)
            for nm, lhs, li, rhs, ri, mi in (("N", BK, 0, AR, 0, 0), ("PBR", BK, 0, AR, 1, 1), ("AAK", BK, 1, AR, 0, 2), ("PKR", BK, 1, AR, 1, 3), ("NT", AR, 0, BK, 0, 4)):
                p_, p_b = ps_next()
                for n in range(CPB):
                    k.op("pe", lambda e: e.matmul(p_[0:64, n * L:(n + 1) * L], lhs[ph, n, li, :], rhs[ph, n, ri, :], start=True, stop=True),
                         R=[AR_b, BK_b], W=[p_b])
                dst, dst_b = sc[nm]
                k.op("dve", lambda e: e.tensor_tensor(out=dst[:, h, :, :], in0=p_[0:64, :].rearrange("p (n t) -> p n t", t=L),
                                                      in1=cmask[:, mi, :].unsqueeze(1).to_broadcast([64, CPB, L]), op=ALU.mult),
                     R=[p_b, cmask_b], W=[dst_b])
                yield
        N_, N_b_ = sc["N"]; NT_, NT_b_ = sc["NT"]; N2_, N2_b_ = sc["N2"]; NT2_, NT2_b_ = sc["NT2"]; X_, X_b_ = sc["X"]; X2_, X2_b_ = sc["X2"]
        k.op("dve", lambda e: e.tensor_tensor(out=X_[:], in0=N_[:], in1=ident[0:64, 0:64].unsqueeze(1).unsqueeze(1).to_broadcast([64, 2, CPB, L]), op=ALU.add),
             R=[N_b_, ident_b], W=[X_b_])
        cur = (N_, N_b_, NT_, NT_b_, X_, X_b_)
        nxt = (N2_, N2_b_, NT2_, NT2_b_, X2_, X2_b_)
        for j in range(1, 6):
            cN, cN_b, cNT, cNT_b, cX, cX_b = cur
            nN, nN_b, nNT, nNT_b, nX, nX_b = nxt
            for h in range(2):
                if j < 5:
                    p_, p_b = ps_next()
                    for n in range(CPB):
                        k.op("pe", lambda e: e.matmul(p_[0:64, n * L:(n + 1) * L], cNT[:, h, n, :], cN[:, h, n, :], start=True, stop=True), R=[cNT_b, cN_b], W=[p_b])
                    k.op("act", lambda e: e.activation(out=nN[:, h, :, :], in_=p_[0:64, :].rearrange("p (n t) -> p n t", t=L), func=AF.Copy), R=[p_b], W=[nN_b])
                p_, p_b = ps_next()
                for n in range(CPB):
                    k.op("pe", lambda e: e.matmul(p_[0:64, n * L:(n + 1) * L], cN[:, h, n, :], cNT[:, h, n, :], start=True, stop=True), R=[cNT_b, cN_b], W=[p_b])
                k.op("dve", lambda e: e.tensor_copy(out=nNT[:, h, :, :], in_=p_[0:64, :].rearrange("p (n t) -> p n t", t=L)), R=[p_b], W=[nNT_b])
                yield
            for h in range(2):
                p_, p_b = ps_next()
                for n in range(CPB):
                    k.op("pe", lambda e: e.matmul(p_[0:64, n * L:(n + 1) * L], nNT[:, h, n, :], cX[:, h, n, :], start=True, stop=True), R=[nNT_b, cX_b], W=[p_b])
                k.op("dve", lambda e: e.tensor_tensor(out=nX[:, h, :, :], in0=p_[0:64, :].rearrange("p (n t) -> p n t", t=L), in1=cX[:, h, :, :], op=ALU.add),
                     R=[p_b, cX_b], W=[nX_b])
                yield
            cur, nxt = nxt, cur
        Xf, Xf_b = cur[4], cur[5]
        AAK, AAK_b = sc["AAK"]; PBR, PBR_b = sc["PBR"]; PKR, PKR_b = sc["PKR"]
        py, py_b, pch, pch_b = t.py, t.py_b, t.pch, t.pch_b
        W0, W0_b, Upad_b, Upad_w, Upad_r = t.W0, t.W0_b, t.Upad_b, t.Upad_w, t.Upad_r
        mf, mb, mf_b, mb_b = Mf[:, pr, :], Mb[:, pr, :], Mf_b[pr], Mb_b[pr]
        for n in range(CPB):
            k.op("pe", lambda e: e.matmul(pch[0:64, 0:128], AR[:, n, 0, :], mb, start=True, stop=False), R=[AR_b, mb_b], W=[pch_b])
            for h in range(2):
                k.op("pe", lambda e: e.matmul(pch[0:64, 0:128], AAK[:, h, n, :], Vpad[:, n, h, :], start=False, stop=(h == 1)),
                     R=[AAK_b, Vpad_b], W=[pch_b])
            k.op("act", lambda e: e.activation(out=W0[:], in_=pch[0:64, 0:128].rearrange("p (h v) -> p h v", h=2), func=AF.Copy), R=[pch_b], W=[W0_b])
            yield
            for h in range(2):
                k.op("pe", lambda e: e.matmul(pch[0:64, 128 + h * L:128 + (h + 1) * L], Xf[:, h, n, :], W0[:, h, :], start=True, stop=True),
                     R=[Xf_b, W0_b], W=[pch_b])
            k.op("act", lambda e: e.activation(out=Upad_w, in_=pch[0:64, 128:256].rearrange("p (h v) -> p h v", h=2), func=AF.Copy), R=[pch_b], W=[Upad_b])
            yield
            ysl = py[:, n * L:(n + 1) * L]
            k.op("pe", lambda e: e.matmul(ysl, mb, AR[:, n, 1, :], start=True, stop=False), R=[mb_b, AR_b], W=[py_b])
            for h in range(2):
                k.op("pe", lambda e: e.matmul(ysl, Upad_r[:, h, :], PBR[:, h, n, :], start=False, stop=False), R=[Upad_b, PBR_b], W=[py_b])
                k.op("pe", lambda e: e.matmul(ysl, Vpad[:, n, h, :], PKR[:, h, n, :], start=False, stop=(h == 1)), R=[Vpad_b, PKR_b], W=[py_b])
            for h in range(2):
                k.op("pe", lambda e: e.matmul(pch[:, 256:384], Bpad[:, n, h, :], Upad_r[:, h, :], start=(h == 0), stop=False), R=[Bpad_b, Upad_b], W=[pch_b])
                k.op("pe", lambda e: e.matmul(pch[:, 256:384], Kpad[:, n, h, :], Vpad[:, n, h, :], start=False, stop=(h == 1)), R=[Kpad_b, Vpad_b], W=[pch_b])
            k.op("dve", lambda e: e.scalar_tensor_tensor(out=mb, in0=mf, scalar=decL[:, n:n + 1], in1=pch[:, 256:384], op0=ALU.mult, op1=ALU.add),
                 R=[mf_b, decL_b, pch_b], W=[mb_b])
            k.op("dve", lambda e: e.scalar_tensor_tensor(out=mf, in0=mf, scalar=decL[:, n:n + 1], in1=pch[:, 256:384], op0=ALU.mult, op1=ALU.add),
                 R=[mf_b, decL_b, pch_b], W=[mf_b])
            yield
        yf, yf_b, ysb, ysb_b, ysq, ysq_b, t3, t3_b, yo, yo_b = t.yf, t.yf_b, t.ysb, t.ysb_b, t.ysq, t.ysq_b, t.t3, t.t3_b, t.yo, t.yo_b
        k.op("act", lambda e: e.activation(out=yf[:], in_=py[:], func=AF.Copy), R=[py_b], W=[yf_b])
        if io.get("yraw") is not None:
            k.dma("sp", io["yraw"][pc, tb:tb + BT], yf[:], R=[yf_b], is_output=True)
        k.op("dve", lambda e: e.tensor_copy(out=ysb[:], in_=yf[:]), R=[yf_b], W=[ysb_b])
        k.op("act", lambda e: e.activation(out=ysq[:], in_=yf[:], func=AF.Square), R=[yf_b], W=[ysq_b])
        pm, pm_b = ps_next()
        k.op("pe", lambda e: e.matmul(pm[:], bones[:], ysb[:], start=True, stop=True), R=[bones_b, ysb_b], W=[pm_b])
        pq, pq_b = ps_next()
        k.op("pe", lambda e: e.matmul(pq[:], bones[:], ysq[:], start=True, stop=True), R=[bones_b, ysq_b], W=[pq_b])
        yield
        k.op("act", lambda e: e.activation(out=t3[:], in_=pm[:], func=AF.Square), R=[pm_b], W=[t3_b])
        k.op("dve", lambda e: e.tensor_tensor(out=t3[:], in0=pq[:], in1=t3[:], op=ALU.subtract), R=[pq_b, t3_b], W=[t3_b])
        k.op("dve", lambda e: e.tensor_scalar(out=t3[:], in0=t3[:], scalar1=0.0, scalar2=GN_EPS, op0=ALU.max, op1=ALU.add), R=[t3_b], W=[t3_b])
        k.op("act", lambda e: e.activation(out=t3[:], in_=t3[:], func=AF.Ln), R=[t3_b], W=[t3_b])
        k.op("act", lambda e: e.activation(out=t3[:], in_=t3[:], func=AF.Exp, scale=-0.5), R=[t3_b], W=[t3_b])
        k.op("dve", lambda e: e.tensor_tensor(out=yf[:], in0=yf[:], in1=pm[:], op=ALU.subtract), R=[yf_b, pm_b], W=[yf_b])
        yield
        k.op("dve", lambda e: e.scalar_tensor_tensor(out=yf[:], in0=yf[:], scalar=colv(pr, 5), in1=t3[:], op0=ALU.mult, op1=ALU.mult),
             R=[yf_b, cols_b, t3_b], W=[yf_b])
        k.op("dve", lambda e: e.scalar_tensor_tensor(out=yf[:], in0=yf[:], scalar=colv(pr, 6), in1=T1[:], op0=ALU.add, op1=ALU.add),
             R=[yf_b, cols_b, T1_b], W=[yf_b])
        k.op("dve", lambda e: e.tensor_tensor(out=yo[:], in0=yf[:], in1=G[:], op=ALU.mult), R=[yf_b, G_b], W=[yo_b])
        k.dma("sp", yrow(pr * 128)[:, tb:tb + BT], yo[:], R=[yo_b], is_output=True)
        yield

    def lora_hidden(wname, ncols, mi, dst, dst_b, func, col0=0, part=None):
        load_mixed(tsA, wv_(wname)[:, :, col0:col0 + ncols], ncols, mi)
        mixed_proj(tsA, ncols, lambda p_, p_b: k.op("act", lambda e: e.activation(out=dst, in_=p_[:ncols, :], func=func), R=[p_b], W=[dst_b]))

    for blk in range(NBLK):
        for c in range(DC_):
            k.op("pool", lambda e: e.tensor_copy(out=xe[:, c, 7:8], in_=xcar[:, c, :]), R=[xcar_b], W=[xe_b[c]])
            if io.get("hn_src") is not None:
                r_, off = (blk * BT) // 1024, (blk * BT) % 1024
                k.dma("sp", xe[:, c, 8:8 + BT], io["hn_src"](c, r_)[:, off:off + BT], W=[xe_b[c]])
            else:
                k.dma("sp", xe[:, c, 8:8 + BT], io["hn"].rearrange("(c p) t -> p c t", p=128)[:, c, blk * BT:(blk + 1) * BT], W=[xe_b[c]])
        if blk + 1 < NBLK:
            k.op("pool", lambda e: e.tensor_copy(out=xcar[:], in_=xe[:, :, 7 + BT:8 + BT]), R=xe_b, W=[xcar_b])
        lora_hidden("w1", 96, 1, hw[:], hw_b, AF.Tanh)
        lora_hidden("a1", 96, 4, ha[:], ha_b, AF.Copy)
        for gc in range(2):
            lora_hidden("g1", 128, 5, hg[:, gc, :], hg_b, AF.Sigmoid, col0=gc * 128)
        if not first:
            lora_hidden("v1", 64, 3, hv[:], hv_b, AF.Copy)
        for pp in range(4):
            ga = unit(tsA, 2 * pp, blk)
            gb = unit(tsB, 2 * pp + 1, blk)
            alive = [ga, gb]
            while alive:
                for g_ in list(alive):
                    try:
                        next(g_)
                    except StopIteration:
                        alive.remove(g_)


from concourse.bass_utils import run_bass_kernel_spmd
import ml_dtypes
_bf = ml_dtypes.bfloat16
NCORES = 8
DEPTH = 4
LITE = False


def _pc(g):
    return np.ascontiguousarray(np.asarray(g, np.float32).reshape(16, 128).T)


def _consts():
    s = np.arange(128)
    mask = (s[:, None] <= s[None, :]).astype(np.float32)
    mask64 = ((s[:, None] % 64) <= np.arange(64)[None, :]).astype(np.float32)
    rmaskE = np.ones((128, 2048), np.float32); rmaskE[:, ::64] = 0.0
    s6 = np.arange(64)
    su = (s6[:, None] < s6[None, :]).astype(np.float32); iu = (s6[:, None] <= s6[None, :]).astype(np.float32); sl = (s6[:, None] > s6[None, :]).astype(np.float32)
    cmask = np.ascontiguousarray(np.stack([su, iu, su, iu, sl], axis=1))
    blk = np.kron(np.eye(2, dtype=np.float32), np.ones((64, 64), np.float32))
    rmaskO = np.ones((128, 1024), np.float32); rmaskO[:, ::64] = 0.0
    return dict(mask=mask, mask64=mask64, rmaskE=rmaskE, ident=np.eye(128, dtype=np.float32), bones=blk / 64.0, bones1=blk, cmask=cmask, rmaskO=rmaskO)


def _even_inputs(d, j, hf):
    w = d["e_w_in"][j]
    g0 = hf * 4
    cols = [w[:, g0 * 128:(g0 + 4) * 128], w[:, 1024 + g0 * 128:1024 + (g0 + 4) * 128]]
    for hd in range(g0, g0 + 4):
        for part in range(4):
            cols.append(w[:, 2048 + part * 1024 + hd * 128:2048 + part * 1024 + (hd + 1) * 128])
    return dict(w_in=np.ascontiguousarray(np.concatenate(cols, axis=1)),
                wsT=np.ascontiguousarray(np.transpose(d["a_ws"][j, g0:g0 + 4], (0, 2, 1))),
                bias=np.ascontiguousarray(d["a_bs"][j, g0:g0 + 4].reshape(1, 512)),
                gainv=np.ascontiguousarray(d["a_vnorm"][j, g0 * 128:(g0 + 4) * 128].reshape(1, 512)),
                onorm=np.ascontiguousarray(d["b_onorm"][j, g0 * 128:(g0 + 4) * 128].reshape(4, 128).T),
                lbl=np.ascontiguousarray(np.transpose(d["b_lb_logits"][:, g0 * 128:(g0 + 4) * 128].reshape(4, 4, 128), (2, 1, 0))))


def _odd_inputs(d, j, hf):
    cs = slice(hf * 1024, (hf + 1) * 1024)
    m = {}
    for nm, key in (("wr", "c_wr"), ("wk", "c_wk"), ("wv", "c_wv"), ("w2", "c_w2"), ("a2", "c_a2"), ("g2", "c_g2")):
        m[nm] = np.ascontiguousarray(d[key][j][:, cs])
    for nm, key in (("w1", "c_w1"), ("a1", "c_a1"), ("g1", "c_g1")):
        m[nm] = np.ascontiguousarray(d[key][j])
    v0 = d["c_v0"][j - 1] if j > 0 else np.zeros(2048, np.float32)
    vecs = [d["c_w0"][j], d["c_a0"][j], d["c_kk"][j], d["c_ka"][j], d["c_rk"][j].reshape(-1), d["c_gn_g"][j], d["c_gn_b"][j], v0]
    cols = np.stack([np.asarray(v)[cs].reshape(8, 128) for v in vecs], axis=-1)
    m["cols"] = np.ascontiguousarray(np.transpose(cols, (1, 0, 2)))
    m["mixpc"] = np.ascontiguousarray(np.transpose(d["c_mix"][j].reshape(6, 16, 128), (2, 0, 1)))
    if j > 0:
        m["v1"] = np.ascontiguousarray(d["c_v1"][j - 1]); m["v2"] = np.ascontiguousarray(d["c_v2"][j - 1][:, cs])
    return m


_EVEN_SHAPES = dict(w_in=[2048, 3072], wsT=[4, 128, 128], bias=[1, 512], gainv=[1, 512], onorm=[128, 4], lbl=[128, 4, 4])
_ODD_SHAPES = dict(wr=[2048, 1024], wk=[2048, 1024], wv=[2048, 1024], w1=[2048, 96], a1=[2048, 96], g1=[2048, 256], w2=[96, 1024], a2=[96, 1024],
                   g2=[256, 1024], mixpc=[128, 6, 16], cols=[128, 8, 8], v1=[2048, 64], v2=[64, 1024])
_CONST_SHAPES = dict(mask=[128, 128], mask64=[128, 64], rmaskE=[128, 2048], ident=[128, 128], bones=[128, 128], bones1=[128, 128],
                     cmask=[64, 5, 64], rmaskO=[128, 1024])


def _wo_perm(d, i):
    j = i // 2
    if i % 2 == 0:
        return d["e_w_out"][j]
    w = d["c_wo"][j]
    return np.concatenate([w[0:512], w[1024:1536], w[512:1024], w[1536:2048]], axis=0)


def build_fused():
    kb = KB()
    nc = kb.nc
    di = lambda n, s, dt=F32: kb.dram(n, s, dt, "ExternalInput")
    xT = di("xT", [2048, 1024])
    if LITE:
        wg = wu = wd = None
    else:
        wg = di("ffn_wg", [DEPTH, 2, 2048, 5632]); wu = di("ffn_wu", [DEPTH, 2, 2048, 5632]); wd = di("ffn_wd", [DEPTH, 2, 5632, 2048])
    norms = di("norms_pc", [DEPTH, 4, 128, 16]); fnorm = di("fnorm_pc", [128, 16])
    plg = di("ple_wg", [DEPTH, 2048, 2048]); plp = di("ple_wp", [DEPTH, 256, 2048]); pT = di("pT", [DEPTH, 256, 1024])
    wo = di("wo", [DEPTH, 2048, 2048]); sel_d = di("sel", [128, 2])
    cst = {n: di(n, s) for n, s in _CONST_SHAPES.items()}
    ev_in = [{n: di(f"e{j}_{n}", s) for n, s in _EVEN_SHAPES.items()} for j in range(2)]
    od_in = [{n: di(f"o{j}_{n}", s) for n, s in _ODD_SHAPES.items() if j > 0 or n not in ("v1", "v2")} for j in range(2)]
    out = kb.dram("out", [2048, 1024], F32, "ExternalOutput")
    internal = lambda n, s, dt: nc.dram_tensor(n, list(s), dt, kind="Internal").ap()
    hspill = internal("hspill", [2048, 1024], F32)
    hn_loc = [internal(f"hn_loc{a}", [1024, 1024], BF16) for a in range(2)]
    hn_pair = [internal(f"hn_pair{a}", [2048, 1024], BF16) for a in range(2)]
    y_loc = [internal(f"y_loc{a}", [512, 2048], BF16) for a in range(2)]
    y_pair = [internal(f"y_pair{a}", [1024, 2048], BF16) for a in range(2)]
    vf = internal("vf", [1024, 2048], F32)

    def row_phase(i):
        kb.phase += 1
        es = contextlib.ExitStack()
        rs = RowState(kb, es)
        if i < 0:
            load_hT(rs, xT)
        else:
            pst = PleState(rs, es)
            selt = kb.sb("sel", [128, 2], F32, es); sel_b = Buf()
            kb.dma("sp", selt[:], sel_d, W=[sel_b])
            load_hT(rs, hspill)
            load_y_select(rs, y_pair, selt, sel_b)
            out_proj_residual(rs, wo[i])
            rmsnorm(rs, norms[i, 2])
            if not LITE:
                ffn(rs, wg[i, 1], wu[i, 1], wd[i, 1])
            rmsnorm(rs, norms[i, 3])
            if not LITE:
                ple(rs, pst, plg[i], plp[i], pT[i])
        if i == DEPTH - 1:
            rmsnorm(rs, fnorm, dst=rs.hT, dst_b=rs.hT_b)
            store_hT(rs, out, is_output=True)
        else:
            rmsnorm(rs, norms[i + 1, 0])
            if not LITE:
                ffn(rs, wg[i + 1, 0], wu[i + 1, 0], wd[i + 1, 0])
            rmsnorm(rs, norms[i + 1, 1])
            store_hT(rs, hspill, is_output=False)
            store_xn_split(rs, hn_loc)
        kb.barrier()
        es.close()

    def mix_phase(i):
        kb.phase += 1
        j = i // 2
        es = contextlib.ExitStack()
        hn_src = lambda c, r_: hn_pair[c // 8][r_ * 1024 + (c % 8) * 128:r_ * 1024 + (c % 8 + 1) * 128, :]
        y_rows = lambda r0: y_loc[r0 // 512][r0 % 512:r0 % 512 + 128, :]
        if i % 2 == 0:
            io = dict(ev_in[j]); io.update(layer=i, hn_src=hn_src, y_rows=y_rows, mask=cst["mask"], mask64=cst["mask64"], ident=cst["ident"], rmask=cst["rmaskE"])
            mix_even(kb, io, es)
        else:
            io = dict(od_in[j]); io.update(first=(j == 0), hn_src=hn_src, y_rows=y_rows, ident=cst["ident"], bones=cst["bones"], bones1=cst["bones1"],
                                           cmask=cst["cmask"], rmask=cst["rmaskO"], vf_out=vf, vf_in=vf)
            mix_odd(kb, io, es)
        kb.barrier()
        es.close()

    row_phase(-1)
    for i in range(DEPTH):
        for a in range(2):
            kb.allgather_pairs(hn_loc[a], hn_pair[a])
        kb.barrier()
        mix_phase(i)
        for a in range(2):
            kb.allgather_pairs(y_loc[a], y_pair[a])
        kb.barrier()
        row_phase(i)
    return kb.finish()


def kernel(**inputs):
    d = {k_: np.asarray(v_) for k_, v_ in inputs.items()}
    x = d["x"].astype(np.float32)
    tsl = lambda hf: slice(hf * 1024, (hf + 1) * 1024)
    shared = dict(ffn_wg=d["ffn_wg"], ffn_wu=d["ffn_wu"], ffn_wd=d["ffn_wd"], ple_wg=d["ple_wg"], ple_wp=d["ple_wp"],
                  norms_pc=np.ascontiguousarray(np.transpose(d["norms"].reshape(DEPTH, 4, 16, 128), (0, 1, 3, 2))), fnorm_pc=_pc(d["final_norm"]),
                  wo=np.ascontiguousarray(np.stack([_wo_perm(d, i) for i in range(DEPTH)])))
    shared.update(_consts())
    per_half = []
    for hf in range(2):
        m = {}
        for j in range(2):
            for n, v in _even_inputs(d, j, hf).items():
                m[f"e{j}_{n}"] = v
            for n, v in _odd_inputs(d, j, hf).items():
                m[f"o{j}_{n}"] = v
        sel = np.zeros((128, 2), np.float32); sel[:, hf] = 1.0
        m["sel"] = sel
        per_half.append(m)
    ims = []
    for c in range(NCORES):
        b, hf = c // 2, c % 2
        m = dict(shared); m.update(per_half[hf])
        m["xT"] = np.ascontiguousarray(x[b, tsl(hf)].T)
        m["pT"] = np.ascontiguousarray(np.transpose(d["p"][:, b, tsl(hf)], (0, 2, 1)))
        ims.append(m)
    nc = build_fused()
    res = run_bass_kernel_spmd(nc, ims, core_ids=list(range(NCORES))).results
    out = np.empty((4, 2048, 2048), np.float32)
    for c in range(NCORES):
        b, hf = c // 2, c % 2
        out[b, tsl(hf)] = res[c]["out"].T
    return out
```
